# Optimizing a Trainium2 kernel written in Bass

```python
import jax, jax.numpy as jnp
from jax import lax
import numpy as np

D_MODEL = 1024
BATCH = 8
SEQ = 4096
DEPTH = 1

D_PLE = 256
D_FF = 2816
NH_M = 4
DV_M = D_MODEL // 8
DK_M = DV_M // 2
CHUNK_M = 64
CONV_W = 4
NH_F = 8
DH_F = D_MODEL // 16
Q_BLOCK = 128
D_MIX = NH_M * DV_M + NH_F * DH_F
IN_SIZES = (2 * NH_M * DK_M, NH_M * DV_M, NH_M * DV_M, 2 * NH_M,
            NH_F * DH_F, NH_F * DH_F, NH_F * DH_F, NH_F)
N_IN = sum(IN_SIZES)
EPS = 1e-6
MLSTM_F_BIAS = 3.0
FOX_F_BIAS = 2.0

kernel_name = "hymba_mlstm_fox_macaron"


def rmsnorm(x, g):
    xf = x.astype(jnp.float32)
    y = xf * lax.rsqrt(jnp.mean(xf * xf, axis=-1, keepdims=True) + EPS)
    return (y * g.astype(jnp.float32)).astype(x.dtype)


def swiglu(x, w_gate, w_up, w_down):
    return (jax.nn.silu(x @ w_gate) * (x @ w_up)) @ w_down


def causal_depthwise_conv(x, w):
    return lax.conv_general_dilated(
        x, w[:, None, :].astype(x.dtype), window_strides=(1,),
        padding=[(w.shape[0] - 1, 0)], dimension_numbers=("NWC", "WIO", "NWC"),
        feature_group_count=x.shape[-1])


def head_rmsnorm(t, g):
    B, S, NH, DH = t.shape
    return rmsnorm(t, g.reshape(NH, DH)).reshape(B, S, NH * DH)


def mlstm_chunkwise(q, k, v, i_pre, f_pre):
    B, NH, S, DK = q.shape
    DV = v.shape[-1]
    L = CHUNK_M
    NC = S // L
    q = q.reshape(B, NH, NC, L, DK) * DK ** -0.5
    k = k.reshape(B, NH, NC, L, DK)
    v = v.reshape(B, NH, NC, L, DV)
    log_i = i_pre.reshape(B, NH, NC, L)
    log_f = jax.nn.log_sigmoid(f_pre).reshape(B, NH, NC, L)
    b = jnp.cumsum(log_f, axis=-1)
    g = b[..., -1]
    a = g[..., None] - b + log_i

    def step(carry, xs):
        C, n, m = carry
        k_c, v_c, a_c, g_c = xs
        m_new = jnp.maximum(g_c + m, jnp.max(a_c, axis=-1))
        decay = jnp.exp(g_c + m - m_new)
        w = jnp.exp(a_c - m_new[..., None])
        C_new = decay[..., None, None] * C + jnp.einsum("bhl,bhld,bhle->bhde", w, k_c, v_c)
        n_new = decay[..., None] * n + jnp.einsum("bhl,bhld->bhd", w, k_c)
        return (C_new, n_new, m_new), (C, n, m)

    init = (jnp.zeros((B, NH, DK, DV), jnp.float32), jnp.zeros((B, NH, DK), jnp.float32),
            jnp.zeros((B, NH), jnp.float32))
    xs = (jnp.moveaxis(k, 2, 0), jnp.moveaxis(v, 2, 0), jnp.moveaxis(a, 2, 0), jnp.moveaxis(g, 2, 0))
    _, (C_prev, n_prev, m_prev) = lax.scan(step, init, xs)
    C_prev = jnp.moveaxis(C_prev, 0, 2)
    n_prev = jnp.moveaxis(n_prev, 0, 2)
    m_prev = jnp.moveaxis(m_prev, 0, 2)

    inter = b + m_prev[..., None]
    causal = jnp.tril(jnp.ones((L, L), dtype=bool))
    D = jnp.where(causal, b[..., :, None] - b[..., None, :] + log_i[..., None, :], -jnp.inf)
    m_t = jnp.maximum(inter, jnp.max(D, axis=-1))
    w_inter = jnp.exp(inter - m_t)
    P = jnp.exp(D - m_t[..., None]) * jnp.einsum("bhcld,bhcsd->bhcls", q, k)
    num = (w_inter[..., None] * jnp.einsum("bhcld,bhcde->bhcle", q, C_prev)
           + jnp.einsum("bhcls,bhcse->bhcle", P, v))
    den = w_inter * jnp.einsum("bhcld,bhcd->bhcl", q, n_prev) + jnp.sum(P, axis=-1)
    h = num / jnp.maximum(jnp.abs(den), jnp.exp(-m_t))[..., None]
    return h.reshape(B, NH, S, DV)


def forgetting_attention(q, k, v, f_pre):
    S = q.shape[2]
    scale = q.shape[-1] ** -0.5
    c = jnp.cumsum(jax.nn.log_sigmoid(f_pre.astype(jnp.float32)), axis=-1)
    outs = []
    for blk in range(S // Q_BLOCK):
        q0, q1 = blk * Q_BLOCK, (blk + 1) * Q_BLOCK
        logits = jnp.einsum("bhqd,bhkd->bhqk", q[:, :, q0:q1], k[:, :, :q1]).astype(jnp.float32) * scale
        logits = logits + c[:, :, q0:q1, None] - c[:, :, None, :q1]
        mask = (q0 + jnp.arange(Q_BLOCK))[:, None] >= jnp.arange(q1)[None, :]
        probs = jax.nn.softmax(jnp.where(mask, logits, -jnp.inf), axis=-1).astype(v.dtype)
        outs.append(jnp.einsum("bhqk,bhkd->bhqd", probs, v[:, :, :q1]))
    return jnp.concatenate(outs, axis=2)


def setup_inputs(seed: int = 0) -> dict:
    key = jax.random.key(seed)
    ks = jax.random.split(key, 24)
    f32 = jnp.float32

    def nrm(k, shape, fan_in):
        return jax.random.normal(k, shape, f32) * fan_in ** -0.5

    def gain(k, n):
        return 1.0 + 0.02 * jax.random.normal(k, (DEPTH, n), f32)

    b_gates = jnp.concatenate(
        [0.1 * jax.random.normal(ks[9], (DEPTH, NH_M), f32),
         MLSTM_F_BIAS + 0.1 * jax.random.normal(ks[10], (DEPTH, NH_M), f32)], axis=-1)
    return {
        "x": jax.random.normal(ks[0], (BATCH, SEQ, D_MODEL), f32),
        "p": jax.random.normal(ks[1], (DEPTH, BATCH, SEQ, D_PLE), f32),
        "ffn1_norm": gain(ks[2], D_MODEL),
        "ffn1_w_gate": nrm(ks[3], (DEPTH, D_MODEL, D_FF), D_MODEL),
        "ffn1_w_up": nrm(ks[4], (DEPTH, D_MODEL, D_FF), D_MODEL),
        "ffn1_w_down": nrm(ks[5], (DEPTH, D_FF, D_MODEL), D_FF),
        "mix_norm": gain(ks[6], D_MODEL),
        "w_in": nrm(ks[7], (DEPTH, D_MODEL, N_IN), D_MODEL),
        "conv_qk": nrm(ks[8], (DEPTH, CONV_W, 2 * NH_M * DK_M), CONV_W),
        "b_mlstm_gates": b_gates,
        "b_fox_f": FOX_F_BIAS + 0.1 * jax.random.normal(ks[11], (DEPTH, NH_F), f32),
        "mlstm_out_norm": gain(ks[12], NH_M * DV_M),
        "fox_out_norm": gain(ks[13], NH_F * DH_F),
        "w_out": nrm(ks[14], (DEPTH, D_MIX, D_MODEL), D_MIX),
        "ffn2_norm": gain(ks[15], D_MODEL),
        "ffn2_w_gate": nrm(ks[16], (DEPTH, D_MODEL, D_FF), D_MODEL),
        "ffn2_w_up": nrm(ks[17], (DEPTH, D_MODEL, D_FF), D_MODEL),
        "ffn2_w_down": nrm(ks[18], (DEPTH, D_FF, D_MODEL), D_FF),
        "ple_gate_norm": gain(ks[19], D_MODEL),
        "w_ple_gate": nrm(ks[20], (DEPTH, D_MODEL, D_MODEL), D_MODEL),
        "w_ple_proj": nrm(ks[21], (DEPTH, D_PLE, D_MODEL), D_PLE),
        "ple_proj_norm": gain(ks[22], D_MODEL),
        "final_norm": 1.0 + 0.02 * jax.random.normal(ks[23], (D_MODEL,), f32),
    }


def reference(x, p, ffn1_norm, ffn1_w_gate, ffn1_w_up, ffn1_w_down, mix_norm, w_in, conv_qk,
              b_mlstm_gates, b_fox_f, mlstm_out_norm, fox_out_norm, w_out, ffn2_norm,
              ffn2_w_gate, ffn2_w_up, ffn2_w_down, ple_gate_norm, w_ple_gate, w_ple_proj,
              ple_proj_norm, final_norm):
    B, S, _ = x.shape
    split_idx = [int(s) for s in np.cumsum(IN_SIZES)[:-1]]

    def heads(t, nh):
        return t.reshape(B, S, nh, -1).transpose(0, 2, 1, 3)

    h = x
    for i in range(DEPTH):
        h = h + 0.5 * swiglu(rmsnorm(h, ffn1_norm[i]), ffn1_w_gate[i], ffn1_w_up[i], ffn1_w_down[i])

        u = rmsnorm(h, mix_norm[i])
        z = u @ w_in[i]
        m_qk, m_v, m_o, m_if, f_q, f_k, f_v, f_f = jnp.split(z, split_idx, axis=-1)

        m_qk = jax.nn.silu(causal_depthwise_conv(m_qk, conv_qk[i]))
        m_q, m_k = jnp.split(m_qk, 2, axis=-1)
        gates = (m_if.astype(jnp.float32) + b_mlstm_gates[i].astype(jnp.float32)).transpose(0, 2, 1)
        h_m = mlstm_chunkwise(heads(m_q, NH_M).astype(jnp.float32), heads(m_k, NH_M).astype(jnp.float32),
                              heads(m_v, NH_M).astype(jnp.float32), gates[:, :NH_M], gates[:, NH_M:])
        h_m = h_m.astype(u.dtype).transpose(0, 2, 1, 3)
        y_m = head_rmsnorm(h_m, mlstm_out_norm[i]) * jax.nn.sigmoid(m_o)

        f_gate = (f_f + b_fox_f[i]).transpose(0, 2, 1)
        h_f = forgetting_attention(heads(f_q, NH_F), heads(f_k, NH_F), heads(f_v, NH_F), f_gate)
        y_f = head_rmsnorm(h_f.transpose(0, 2, 1, 3), fox_out_norm[i])

        h = h + jnp.concatenate([y_m, y_f], axis=-1) @ w_out[i]

        h = h + 0.5 * swiglu(rmsnorm(h, ffn2_norm[i]), ffn2_w_gate[i], ffn2_w_up[i], ffn2_w_down[i])

        gate = jax.nn.sigmoid(rmsnorm(h, ple_gate_norm[i]) @ w_ple_gate[i])
        h = h + gate * rmsnorm(p[i] @ w_ple_proj[i], ple_proj_norm[i])

    return rmsnorm(h, final_norm)
```

```python
import contextlib
import math
import numpy as np
import concourse.bass as bass
import concourse.mybir as mybir
from concourse.bass_utils import run_bass_kernel_spmd

F32 = mybir.dt.float32
BF16 = mybir.dt.bfloat16
AF = mybir.ActivationFunctionType
ALU = mybir.AluOpType

D = 1024
S = 4096
DFF = 2816
NIN = 3088
T = 512
NT = S // T
NKC = D // 128
NHC = DFF // 128
EPS = 1e-6
NSLOT = 6
SLOT_ELEMS = 2048
HALF_A = 12


class Buf:
    __slots__ = ("name", "lw", "rd")

    def __init__(self, name=""):
        self.name = name
        self.lw = None
        self.rd = []


class Prog:
    ENGS = ("pe", "act", "dve", "pool", "sp")

    def __init__(self, nc):
        self.nc = nc
        self.ops = {e: [] for e in self.ENGS}
        self.dma_keys = {}
        self.out_dmas = []
        self.halt = False

    def _deps(self, me, reads, writes):
        deps = set()
        for b in reads:
            if b.lw is not None:
                deps.add(b.lw)
        for b in writes:
            if b.lw is not None:
                deps.add(b.lw)
            deps.update(b.rd)
        deps.discard(me)
        return deps

    def _commit(self, me, reads, writes):
        for b in reads:
            b.rd.append(me)
        for b in writes:
            b.lw = me
            b.rd = []

    def op(self, eng, fn, reads=(), writes=()):
        if self.halt:
            return None
        idx = len(self.ops[eng])
        me = (eng, idx)
        deps = self._deps(me, reads, writes)
        self.ops[eng].append({"fn": fn, "deps": deps, "sig": False, "dma": None})
        self._commit(me, reads, writes)
        return me

    def dma(self, eng, out, in_, key, reads=(), writes=(), is_out=False):
        if self.halt:
            return None
        n = self.dma_keys.get(key, 0) + 1
        self.dma_keys[key] = n
        me = ("dma", key, n)
        deps = self._deps(me, reads, writes)
        self.ops[eng].append({"fn": None, "deps": deps, "sig": False, "dma": (out, in_, key, n)})
        self._commit(me, reads, writes)
        if is_out:
            self.out_dmas.append(me)
        return me

    def emit(self):
        nc = self.nc
        for e in self.ENGS:
            for o in self.ops[e]:
                for d in o["deps"]:
                    if d[0] != "dma":
                        self.ops[d[0]][d[1]]["sig"] = True
        signo = {}
        for e in self.ENGS:
            c = 0
            for i, o in enumerate(self.ops[e]):
                if o["sig"]:
                    c += 1
                    signo[(e, i)] = c
        with contextlib.ExitStack() as st:
            esem = {e: st.enter_context(nc.semaphore("s_" + e)) for e in self.ENGS}
            dsem = {}
            for k in self.dma_keys:
                dsem[k] = st.enter_context(nc.semaphore("d_" + str(k).replace(" ", "").replace("'", "").replace("(", "").replace(")", "").replace(",", "_")))
            block = st.enter_context(nc.Block())

            def run(ename, E):
                known = {}
                for i, o in enumerate(self.ops[ename]):
                    need = {}
                    for d in o["deps"]:
                        if d[0] == "dma":
                            k = ("dma", d[1])
                            v = 16 * d[2]
                        else:
                            if d[0] == ename and ename == "pe":
                                continue
                            k = d[0]
                            v = signo[d]
                        if v > need.get(k, 0):
                            need[k] = v
                    for k, v in need.items():
                        if known.get(k, 0) >= v:
                            continue
                        known[k] = v
                        sem = dsem[k[1]] if isinstance(k, tuple) else esem[k]
                        E.wait_ge(sem, v)
                    if o["dma"] is not None:
                        out, in_, key, n = o["dma"]
                        E.dma_start(out=out, in_=in_).then_inc(dsem[key], 16)
                    else:
                        ins = o["fn"](E)
                        if o["sig"]:
                            ins.then_inc(esem[ename], 1)
                if ename == "sp":
                    last = {}
                    for d in self.out_dmas:
                        last[d[1]] = max(last.get(d[1], 0), d[2])
                    for k, n in last.items():
                        E.wait_ge(dsem[k], 16 * n)

            block.tensor(lambda E: run("pe", E))
            block.scalar(lambda E: run("act", E))
            block.vector(lambda E: run("dve", E))
            block.gpsimd(lambda E: run("pool", E))
            block.sync(lambda E: run("sp", E))


C_FFN1, C_MIX, C_FFN2, C_PG, C_PP, C_FIN = 0, 8, 16, 24, 32, 40
C_MOUT = 48
C_FOUT = 52
C_CONV = 60
C_BI = 76
C_BF = 77
C_BFF = 78
NCONST = 80


class _Stop(Exception):
    pass


def build_nc(ntiles=NT, debug=False, stop=None):
    nc = bass.Bass("TRN2", target_bir_lowering=False)
    dr = lambda name, shape, kind="ExternalInput": nc.dram_tensor(name, shape, F32, kind=kind).ap()
    x_d = dr("x", [D, S])
    p_d = dr("p", [256, S])
    w1g, w1u, w1d = dr("ffn1_w_gate", [D, DFF]), dr("ffn1_w_up", [D, DFF]), dr("ffn1_w_down", [DFF, D])
    w2g, w2u, w2d = dr("ffn2_w_gate", [D, DFF]), dr("ffn2_w_up", [D, DFF]), dr("ffn2_w_down", [DFF, D])
    win_d = dr("w_in", [D, NIN])
    wgates_d = dr("w_gates", [D, 16])
    wout_d = dr("w_out", [D, D])
    wpg_d = dr("w_ple_gate", [D, D])
    wpp_d = dr("w_ple_proj", [256, D])
    consts_d = dr("consts", [128, NCONST])
    out_d = dr("out", [D, S], kind="ExternalOutput")
    dbg_d = {}
    if debug:
        for nm, w in (("hT", NKC * T), ("uT", NKC * T), ("mqk", 4 * T), ("hm", 4 * (3 + T)), ("ycf", 8 * T), ("fq", 8 * T), ("misc", 4 * T), ("ffa", HALF_A * T)):
            dbg_d[nm] = dr("dbg_" + nm, [128, w], kind="ExternalOutput")

    marks = []

    def chk(name):
        marks.append((name, len(P.ops["pe"]), len(P.ops["act"]), len(P.ops["dve"])))
        if stop == name:
            P.halt = True

    P = Prog(nc)
    st = contextlib.ExitStack()
    with st:
        def sb(name, shape, dt=F32):
            return st.enter_context(nc.sbuf_tensor("sb_" + name, shape, dt))

        hT = sb("hT", [128, NKC, T]); b_h = [Buf("h%d" % c) for c in range(NKC)]
        uT = sb("uT", [128, NKC, T], BF16); b_u = [Buf("u%d" % c) for c in range(NKC)]
        ffa = sb("ffa", [128, HALF_A, T], BF16); b_ffa = [Buf("ffa%d" % c) for c in range(HALF_A)]
        kc = sb("kc", [128, 4, S], BF16); b_kc = [[Buf() for _ in range(NT)] for _ in range(4)]
        vc = sb("vc", [128, S // 128, 8, 65], BF16); b_vc = [Buf() for _ in range(S // 128)]
        wring = sb("wring", [128, NSLOT, SLOT_ELEMS], BF16); b_slot = [Buf("slot%d" % i) for i in range(NSLOT)]
        pT = sb("pT", [128, 2, T], BF16); b_pT = Buf("pT")
        consts = sb("consts", [128, NCONST]); b_consts = Buf("consts")
        cneg = sb("cneg", [128, 4]); b_cneg = Buf("cneg")
        ghalf = sb("ghalf", [128, 12]); b_ghalf = Buf("ghalf")
        wgates = sb("wgates", [128, NKC, 16], BF16); b_wgates = Buf("wgates")
        ident = sb("ident", [128, 128]); b_ident = Buf("ident")
        identb = sb("identb", [128, 128], BF16)
        onesb = sb("onesb", [128, 128], BF16)
        ones8 = sb("ones8", [8, 128])
        mask01 = sb("mask01", [128, 128], BF16)
        maskneg = sb("maskneg", [128, 128], BF16)
        selrows = sb("selrows", [4, 4, 128])
        selpair = sb("selpair", [4, 2, 128])
        epsc = sb("epsc", [128, 1])
        zerosf = sb("zerosf", [128, 128]); onesf = sb("onesf", [128, 128])
        sel64 = sb("sel64", [65, 64])
        b_k = Buf("konst")
        sqr = sb("sqr", [128, 2, T], BF16); b_sqr = [Buf() for _ in range(2)]
        f32r = sb("f32r", [128, 3, T]); b_f32r = [Buf() for _ in range(3)]
        rsr = sb("rsr", [128, 1, T]); b_rsr = [Buf() for _ in range(1)]
        ptr = sb("ptr", [128, 3, T], BF16); b_ptr = [Buf() for _ in range(3)]
        mraw = sb("mraw", [128, 4, 3 + T]); b_mraw = [Buf() for _ in range(4)]
        mhist = sb("mhist", [128, 4, 3]); b_mhist = Buf("mhist")
        mqk = sb("mqk", [128, 4, T], BF16); b_mqk = [Buf() for _ in range(4)]
        vm = sb("vm", [128, 4, 4, 129], BF16); b_vm = [Buf() for _ in range(4)]
        fq = sb("fqz", [128, 8, T], BF16); b_fq = [Buf() for _ in range(8)]
        ycm = mqk; b_ycm = b_mqk
        ycf = sb("ycf", [64, 8, T], BF16); b_ycf = [Buf() for _ in range(8)]
        hm = mraw; b_hm = b_mraw
        g_l1 = sb("g_l1", [4, T]); g_cb = sb("g_cb", [4, 1 + T]); g_rho = sb("g_rho", [4, T])
        g_gx = sb("g_gx", [4, 1 + T]); g_ngx = sb("g_ngx", [4, 8])
        g_aw = sb("g_aw", [4, 2, 128]); g_E = sb("g_E", [4, T]); g_dec = sb("g_dec", [4, 4])
        b_gate = Buf("gate_m")
        awT = sb("awT", [128, 4, 8]); b_awT = Buf("awT")
        decp = sb("decp", [128, 2, 4]); b_decp = Buf("decp")
        cn = sb("cn", [128, 2, 129]); b_cn = [Buf(), Buf()]
        cbf = sb("cbf", [128, 2, 128], BF16); nbb = sb("nbb", [128, 2, 128], BF16); b_cbf = [Buf(), Buf()]
        kw = sb("kw", [128, 2, 128], BF16); b_kw = [Buf(), Buf()]
        ptm4 = sb("ptm4", [128, 4, 128], BF16); b_ptm4 = [Buf() for _ in range(4)]
        esb = sb("esb", [128, 2, 128]); b_esb = [Buf(), Buf()]
        t1r = sb("t1r", [128, 2, 128]); b_t1r = [Buf(), Buf()]
        f_l1 = sb("f_l1", [8, T]); f_cb = sb("f_cb", [8, 1 + T]); f_dg = sb("f_dg", [8, 8]); b_fg = Buf("gate_f")
        cbT = sb("cbT", [128, S // 128, 8]); b_cbT = Buf("cbT")
        refb = sb("refb", [128, 8]); b_refb = Buf("refb")
        biasT = sb("biasT", [128, S // 128, 8]); b_biasT = Buf("biasT")
        sel24 = sb("sel24", [128, 8, 128], BF16)
        qall = sb("qall", [128, T], BF16); b_qall = Buf("qall")
        osb = sb("osb", [65, 1, T]); b_osb = [Buf()]

        psum = [st.enter_context(nc.psum_tensor("ps%d" % i, [128, T], F32)) for i in range(8)]
        b_ps = [Buf("ps%d" % i) for i in range(8)]
        pools = {"A": [0, 1, 2, 3], "B": [4, 5], "C": [6, 7]}
        pool_ctr = {"A": 0, "B": 0, "C": 0}

        def ps_get(pool):
            lst = pools[pool]
            i = lst[pool_ctr[pool] % len(lst)]
            pool_ctr[pool] += 1
            return psum[i], b_ps[i]

        ring_ctr = {}

        def ring(name, tensor, bufs):
            i = ring_ctr.get(name, 0)
            ring_ctr[name] = i + 1
            j = i % len(bufs)
            return tensor[:, j], bufs[j]

        def cc(col, n=1, rows=128):
            return consts[0:rows, col:col + n]

        pieces = []

        def wpiece(src, parts, a, b):
            pieces.append((src, parts, a, b))

        FFN_HALVES = ((0, HALF_A), (HALF_A, NHC - HALF_A))

        def add_ffn(wg, wu, wd):
            for (c0, nch) in FFN_HALVES:
                for hp in range(nch // 2):
                    col = (c0 + 2 * hp) * 128
                    wpiece(wg[:, col:col + 256].rearrange("(kc p) n -> p kc n", p=128), 128, NKC, 256)
                    wpiece(wu[:, col:col + 256].rearrange("(kc p) n -> p kc n", p=128), 128, NKC, 256)
                for do in range(NKC):
                    wpiece(wd[c0 * 128:(c0 + nch) * 128, do * 128:(do + 1) * 128].rearrange("(j p) n -> p j n", p=128), 128, nch, 128)

        WIN_GROUPS = [0, 256, 512, 768, 1544, 1800, 2056, 2312, 2568, 2824, 1024, 1280]
        for t in range(ntiles):
            add_ffn(w1g, w1u, w1d)
            for c0 in WIN_GROUPS:
                wpiece(win_d[:, c0:c0 + 256].rearrange("(kc p) n -> p kc n", p=128), 128, NKC, 256)
            for dp in range(4):
                wpiece(wout_d[0:512, dp * 256:(dp + 1) * 256].rearrange("(c p) n -> p c n", p=128), 128, 4, 256)
                wpiece(wout_d[512:1024, dp * 256:(dp + 1) * 256].rearrange("(h p) n -> p h n", p=64), 64, 8, 256)
            add_ffn(w2g, w2u, w2d)
            for dp in range(4):
                wpiece(wpg_d[:, dp * 256:(dp + 1) * 256].rearrange("(kc p) n -> p kc n", p=128), 128, NKC, 256)
            wpiece(wpp_d.rearrange("(k p) n -> p k n", p=128), 128, 2, 1024)
        wstate = {"issued": 0, "next": 0}

        npt = len(pieces) // ntiles
        wscr = nc.dram_tensor("wscr_bf16", [npt, 128, SLOT_ELEMS], BF16).ap()
        b_wscr = [Buf() for _ in range(npt)]

        def w_issue_upto(n):
            while wstate["issued"] < min(n, len(pieces)):
                i = wstate["issued"]
                src, parts, a, b = pieces[i]
                s = i % NSLOT
                pidx = i % npt
                flat = wring[0:parts, s, 0:a * b]
                if i < npt:
                    dst = flat.rearrange("p (a b) -> p a b", a=a)
                    P.dma("pool", dst, src, ("w", s), writes=[b_slot[s]])
                    if ntiles > 1:
                        P.dma("sp", wscr[pidx, 0:parts, 0:a * b], flat, ("ws", s), reads=[b_slot[s]], writes=[b_wscr[pidx]])
                else:
                    P.dma("sp", flat, wscr[pidx, 0:parts, 0:a * b], ("w", s), reads=[b_wscr[pidx]], writes=[b_slot[s]])
                wstate["issued"] += 1

        def w_rel(n=1):
            w_issue_upto(wstate["issued"] + n)

        def w_next():
            i = wstate["next"]
            wstate["next"] += 1
            assert i < wstate["issued"] or P.halt or i >= len(pieces), (i, wstate)
            src, parts, a, b = pieces[i]
            s = i % NSLOT
            view = wring[0:parts, s, 0:a * b].rearrange("p (a b) -> p a b", a=a)
            return view, b_slot[s]

        def mm(out, lhsT, rhs, start, stop, reads, writes):
            P.op("pe", lambda E: E.matmul(out, lhsT=lhsT, rhs=rhs, start=start, stop=stop), reads, writes)

        def act(out, in_, func, reads, writes, bias=None, scale=None):
            kw_ = {}
            if bias is not None:
                kw_["bias"] = bias
            if scale is not None:
                kw_["scale"] = scale
            P.op("act", lambda E: E.activation(out=out, in_=in_, func=func, **kw_), reads, writes)

        def stt(eng, out, in0, scalar, in1, op0, op1, reads, writes):
            P.op(eng, lambda E: E.scalar_tensor_tensor(out=out, in0=in0, scalar=scalar, in1=in1, op0=op0, op1=op1), reads, writes)

        def ts(eng, out, in0, s1, s2, op0, op1, reads, writes):
            if op1 is None:
                P.op(eng, lambda E: E.tensor_scalar(out=out, in0=in0, scalar1=s1, scalar2=None, op0=op0), reads, writes)
            else:
                P.op(eng, lambda E: E.tensor_scalar(out=out, in0=in0, scalar1=s1, scalar2=s2, op0=op0, op1=op1), reads, writes)

        def tt(eng, out, in0, in1, op, reads, writes):
            P.op(eng, lambda E: E.tensor_tensor(out=out, in0=in0, in1=in1, op=op), reads, writes)

        def cp(eng, out, in_, reads, writes):
            if eng == "act":
                P.op("act", lambda E: E.copy(out=out, in_=in_), reads, writes)
            else:
                P.op(eng, lambda E: E.tensor_copy(out=out, in_=in_), reads, writes)

        def rstd_from_psum(ps_ap, ps_buf, nparts, inv_n, eps_in_sum=False):
            tmp, tb = ring("f32r", f32r, b_f32r)
            rs, rb = ring("rsr", rsr, b_rsr)
            if eps_in_sum:
                act(tmp[0:nparts], ps_ap, AF.Ln, [ps_buf], [tb])
            else:
                act(tmp[0:nparts], ps_ap, AF.Ln, [ps_buf, b_k], [tb], bias=epsc[0:nparts, 0:1], scale=inv_n)
            act(rs[0:nparts], tmp[0:nparts], AF.Exp, [tb], [rb], scale=-0.5)
            return rs, rb

        def sumsq_bcast(chunks, nparts, lhsT_ones):
            ps, pb = ps_get("C")
            n = len(chunks)
            for i, (ap, b) in enumerate(chunks):
                sq, sqb = ring("sqr", sqr, b_sqr)
                act(sq[0:nparts], ap, AF.Square, [b], [sqb])
                mm(ps[0:lhsT_ones.shape[1], :], lhsT_ones, sq[0:nparts], i == 0, i == n - 1, [sqb, b_k], [pb])
            return ps, pb

        def norm_to_uT(gcol):
            ps, pb = sumsq_bcast([(hT[:, c, :], b_h[c]) for c in range(NKC)], 128, onesb[:, :])
            rs, rb = rstd_from_psum(ps[:, :], pb, 128, 1.0 / D)
            for c in range(NKC):
                stt("dve", uT[:, c, :], hT[:, c, :], cc(gcol + c), rs[:, :], ALU.mult, ALU.mult, [b_h[c], rb, b_consts], [b_u[c]])

        def ffn():
            for (c0, nch) in FFN_HALVES:
                for hp in range(nch // 2):
                    wg_v, wg_b = w_next()
                    wu_v, wu_b = w_next()
                    for j in range(2):
                        hc = 2 * hp + j
                        pg, pgb = ps_get("A")
                        pu, pub = ps_get("A")
                        for k in range(NKC):
                            mm(pg[:, :], wg_v[:, k, j * 128:(j + 1) * 128], uT[:, k, :], k == 0, k == NKC - 1, [wg_b, b_u[k]], [pgb])
                        for k in range(NKC):
                            mm(pu[:, :], wu_v[:, k, j * 128:(j + 1) * 128], uT[:, k, :], k == 0, k == NKC - 1, [wu_b, b_u[k]], [pub])
                        sg, sgb = ring("f32r", f32r, b_f32r)
                        act(sg, pg[:, :], AF.Silu, [pgb], [sgb])
                        tt("dve", ffa[:, hc, :], sg, pu[:, :], ALU.mult, [sgb, pub], [b_ffa[hc]])
                    w_rel(2)
                for do in range(NKC):
                    wd_v, wd_b = w_next()
                    pd, pdb = ps_get("B")
                    for j in range(nch):
                        mm(pd[:, :], wd_v[:, j, :], ffa[:, j, :], j == 0, j == nch - 1, [wd_b, b_ffa[j]], [pdb])
                    w_rel(1)
                    stt("dve", hT[:, do, :], pd[:, :], 0.5, hT[:, do, :], ALU.mult, ALU.add, [pdb, b_h[do]], [b_h[do]])

        P.dma("sp", consts[:, :], consts_d[:, :], "c0", writes=[b_consts])
        P.dma("pool", wgates[:, :, :], wgates_d.rearrange("(kc p) n -> p kc n", p=128), "c1", writes=[b_wgates])
        kops = []
        kops.append(lambda E: E.memset(zerosf[:, :], 0.0))
        kops.append(lambda E: E.memset(onesf[:, :], 1.0))
        kops.append(lambda E: E.memset(epsc[:, :], EPS))
        kops.append(lambda E: E.memset(onesb[:, :], 1.0))
        kops.append(lambda E: E.memset(ones8[:, :], 1.0))
        kops.append(lambda E: E.affine_select(out=ident[:, :], in_=onesf[:, :], pattern=[[1, 128]], compare_op=ALU.is_equal, fill=0.0, base=0, channel_multiplier=-1))
        kops.append(lambda E: E.tensor_copy(out=identb[:, :], in_=ident[:, :]))
        kops.append(lambda E: E.memset(zerosf[:, :], 0.125))
        kops.append(lambda E: E.affine_select(out=mask01[:, :], in_=zerosf[:, :], pattern=[[1, 128]], compare_op=ALU.is_ge, fill=0.0, base=0, channel_multiplier=-1))
        kops.append(lambda E: E.memset(zerosf[:, :], 0.0))
        kops.append(lambda E: E.affine_select(out=maskneg[:, :], in_=zerosf[:, :], pattern=[[1, 128]], compare_op=ALU.is_ge, fill=-30000.0, base=0, channel_multiplier=-1))
        kops.append(lambda E: E.memset(selrows[:, :, :], 1.0))
        kops.append(lambda E: E.memset(selpair[:, :, :], 1.0))
        kops.append(lambda E: E.affine_select(out=selrows[:, :, :], in_=selrows[:, :, :], pattern=[[1, 4], [0, 128]], compare_op=ALU.is_equal, fill=0.0, base=0, channel_multiplier=-1))
        kops.append(lambda E: E.affine_select(out=selpair[:, :, :].rearrange("p j (a b) -> p j a b", a=2), in_=selpair[:, :, :].rearrange("p j (a b) -> p j a b", a=2), pattern=[[2, 2], [1, 2], [0, 64]], compare_op=ALU.is_equal, fill=0.0, base=0, channel_multiplier=-1))
        kops.append(lambda E: E.memset(sel64[0:64, :], 0.0))
        kops.append(lambda E: E.memset(sel64[64:65, :], 1.0))
        kops.append(lambda E: E.memset(vc[:, :, :, 64:65], 1.0))
        kops.append(lambda E: E.memset(vm[:, :, :, 128:129], 1.0))
        kops.append(lambda E: E.memset(cn[:, :, :], 0.0))
        kops.append(lambda E: E.memset(cbf[:, :, :], 0.0))
        kops.append(lambda E: E.memset(nbb[:, :, :], 0.0))
        kops.append(lambda E: E.memset(mhist[:, :, :], 0.0))
        kops.append(lambda E: E.memset(g_cb[:, 0:1], 0.0))
        kops.append(lambda E: E.memset(g_gx[:, 0:1], 0.0))
        kops.append(lambda E: E.memset(f_cb[:, 0:1], 0.0))
        kops.append(lambda E: E.memset(sel24[:, :, :], 0.0))
        kops.append(lambda E: E.memset(qall[:, :], 0.0))
        kops.append(lambda E: E.memset(fq[:, :, :], 0.0))
        kops.append(lambda E: E.memset(sel24[0:8, :, :], 1.0))
        kops.append(lambda E: E.affine_select(out=sel24[0:8, :, :], in_=sel24[0:8, :, :], pattern=[[1, 8], [0, 128]], compare_op=ALU.is_equal, fill=0.0, base=0, channel_multiplier=-1))
        for f in kops:
            P.op("pool", f, [], [b_k])
        for i_, r0_ in enumerate((8, 16)):
            P.dma("sp", sel24[r0_:r0_ + 8, :, :], sel24[0:8, :, :], "c%d" % (2 + i_), reads=[b_k], writes=[b_k])
        for b in b_vc + b_vm + b_cn + b_cbf + b_fq + [b_mhist, b_gate, b_fg, b_qall]:
            b.lw = b_k.lw
        ts("dve", cneg[0:4, 0:1], consts[0:4, C_BF:C_BF + 1], -1.0, None, ALU.mult, None, [b_consts], [b_cneg])
        ts("dve", cneg[0:8, 1:2], consts[0:8, C_BFF:C_BFF + 1], -1.0, None, ALU.mult, None, [b_consts], [b_cneg])
        ts("dve", ghalf[:, 0:4], consts[:, C_MOUT:C_MOUT + 4], 0.5, None, ALU.mult, None, [b_consts], [b_ghalf])
        ts("dve", ghalf[:, 4:12], consts[:, C_PP:C_PP + 8], 0.5, None, ALU.mult, None, [b_consts], [b_ghalf])

        w_issue_upto(NSLOT)

        def tr(out, in_, idn, reads, writes):
            P.op("pe", lambda E: E.transpose(out=out, in_=in_, identity=idn), reads, writes)

        def proj_fm(wv, wb, jj, ps, pb):
            for k in range(NKC):
                mm(ps[:, :], wv[:, k, jj * 128:(jj + 1) * 128], uT[:, k, :], k == 0, k == NKC - 1, [wb, b_u[k]], [pb])

        for t in (range(ntiles) if True else []):
            t0 = t * T
            for c in range(NKC):
                P.dma("sp", hT[:, c, :], x_d[c * 128:(c + 1) * 128, t0:t0 + T], ("xin", c), writes=[b_h[c]])
            for k in range(2):
                P.dma("pool", pT[:, k, :], p_d[k * 128:(k + 1) * 128, t0:t0 + T], ("pin", k), writes=[b_pT])

            chk("load")
            norm_to_uT(C_FFN1)
            chk("norm1")
            ffn()
            chk("ffn1")

            norm_to_uT(C_MIX)
            pg_i, pg_ib = ps_get("C")
            for k in range(NKC):
                mm(pg_i[0:4, :], wgates[:, k, 0:4], uT[:, k, :], k == 0, k == NKC - 1, [b_wgates, b_u[k]], [pg_ib])
            act(g_rho[:, :], pg_i[0:4, :], AF.Identity, [pg_ib, b_consts], [b_gate], bias=consts[0:4, C_BI:C_BI + 1], scale=1.0)
            pg_f, pg_fb = ps_get("C")
            for k in range(NKC):
                mm(pg_f[0:4, :], wgates[:, k, 4:8], uT[:, k, :], k == 0, k == NKC - 1, [b_wgates, b_u[k]], [pg_fb])
            act(g_l1[:, :], pg_f[0:4, :], AF.Exp, [pg_fb, b_cneg], [b_gate], bias=cneg[0:4, 0:1], scale=-1.0)
            pg_ff, pg_ffb = ps_get("C")
            for k in range(NKC):
                mm(pg_ff[0:8, :], wgates[:, k, 8:16], uT[:, k, :], k == 0, k == NKC - 1, [b_wgates, b_u[k]], [pg_ffb])
            act(f_l1[:, :], pg_ff[0:8, :], AF.Exp, [pg_ffb, b_cneg], [b_fg], bias=cneg[0:8, 1:2], scale=-1.0)
            act(g_l1[:, :], g_l1[:, :], AF.Ln, [b_gate], [b_gate], bias=1.0, scale=1.0)
            act(f_l1[:, :], f_l1[:, :], AF.Ln, [b_fg], [b_fg], bias=1.0, scale=1.0)
            if t > 0:
                cp("dve", g_cb[:, 0:1], g_cb[:, T:T + 1], [b_gate], [b_gate])
                cp("dve", g_gx[:, 0:1], g_gx[:, T:T + 1], [b_gate], [b_gate])
                cp("dve", f_cb[:, 0:1], f_cb[:, T:T + 1], [b_fg], [b_fg])
            P.op("dve", lambda E: E.tensor_tensor_scan(out=g_cb[:, 1:1 + T], data0=g_l1[:, :], data1=g_l1[:, :], initial=g_cb[:, 0:1], op0=ALU.add, op1=ALU.max), [b_gate], [b_gate])
            tt("dve", g_rho[:, :], g_rho[:, :], g_cb[:, 1:1 + T], ALU.add, [b_gate], [b_gate])
            P.op("dve", lambda E: E.tensor_tensor_scan(out=g_gx[:, 1:1 + T], data0=g_rho[:, :], data1=g_rho[:, :], initial=g_gx[:, 0:1], op0=ALU.max, op1=ALU.max), [b_gate], [b_gate])
            for c5 in range(5):
                ts("dve", g_ngx[:, c5:c5 + 1], g_gx[:, c5 * 128:c5 * 128 + 1], -1.0, None, ALU.mult, None, [b_gate], [b_gate])
            P.op("dve", lambda E: E.tensor_tensor_scan(out=f_cb[:, 1:1 + T], data0=f_l1[:, :], data1=f_l1[:, :], initial=f_cb[:, 0:1], op0=ALU.add, op1=ALU.max), [b_fg], [b_fg])
            vq, vqb = ring("f32r", f32r, b_f32r)
            ts("dve", vq[0:8], f_cb[:, 1:1 + T], f_cb[:, 0:1], -8.0, ALU.subtract, ALU.mult, [b_fg], [vqb])
            for r3 in range(3):
                pc_, pcb = ring("sqr", sqr, b_sqr)
                cp("dve", pc_[0:8], vq[0:8], [vqb], [pcb])
                P.dma("sp", qall[r3 * 8:(r3 + 1) * 8, :], pc_[0:8], ("qa", r3), reads=[pcb], writes=[b_qall])
                if r3 < 2:
                    tt("dve", vq[0:8], vq[0:8], pc_[0:8], ALU.subtract, [vqb, pcb], [vqb])

            for c in range(4):
                cp("dve", mraw[:, c, 0:3], mhist[:, c, :], [b_mhist], [b_mraw[c]])
            for pi in range(2):
                wv, wb = w_next()
                for jj in range(2):
                    c = 2 * pi + jj
                    ps, pb = ps_get("A")
                    proj_fm(wv, wb, jj, ps, pb)
                    cp("act", mraw[:, c, 3:3 + T], ps[:, :], [pb], [b_mraw[c]])
                w_rel(1)
            wva, wba = w_next()
            wvb, wbb = w_next()
            for blk in range(4):
                ps, pb = ps_get("A")
                for half, (wv, wb) in enumerate(((wva, wba), (wvb, wbb))):
                    for k in range(NKC):
                        mm(ps[:, half * 256:(half + 1) * 256], uT[:, k, blk * 128:(blk + 1) * 128], wv[:, k, :], k == 0, k == NKC - 1, [wb, b_u[k]], [pb])
                cp("dve", vm[:, blk, :, 0:128], ps[:, :].rearrange("p (h d) -> p h d", h=4), [pb], [b_vm[blk]])
            w_rel(2)
            for pi in range(2):
                wv, wb = w_next()
                for jj in range(2):
                    c = 2 * pi + jj
                    ps, pb = ps_get("A")
                    proj_fm(wv, wb, jj, ps, pb)
                    cp("dve", fq[0:64, 2 * c, :], ps[0:64, :], [pb], [b_fq[2 * c]])
                    cp("dve", fq[64:128, 2 * c + 1, :], ps[64:128, :], [pb], [b_fq[2 * c + 1]])
                w_rel(1)
            for pi in range(2):
                wv, wb = w_next()
                for jj in range(2):
                    c = 2 * pi + jj
                    ps, pb = ps_get("A")
                    proj_fm(wv, wb, jj, ps, pb)
                    cp("act", kc[:, c, t0:t0 + T], ps[:, :], [pb], [b_kc[c][t]])
                w_rel(1)
            wva, wba = w_next()
            wvb, wbb = w_next()
            for blk in range(4):
                ps, pb = ps_get("A")
                for half, (wv, wb) in enumerate(((wva, wba), (wvb, wbb))):
                    for k in range(NKC):
                        mm(ps[:, half * 256:(half + 1) * 256], uT[:, k, blk * 128:(blk + 1) * 128], wv[:, k, :], k == 0, k == NKC - 1, [wb, b_u[k]], [pb])
                cp("dve", vc[:, t * 4 + blk, :, 0:64], ps[:, :].rearrange("p (h d) -> p h d", h=8), [pb], [b_vc[t * 4 + blk]])
            w_rel(2)

            for c in range(4):
                acc, accb = ring("f32r", f32r, b_f32r)
                ts("dve", acc, mraw[:, c, 0:T], cc(C_CONV + c * 4 + 0), None, ALU.mult, None, [b_mraw[c], b_consts], [accb])
                for j in range(1, 4):
                    stt("dve", acc, mraw[:, c, j:j + T], cc(C_CONV + c * 4 + j), acc, ALU.mult, ALU.add, [b_mraw[c], b_consts, accb], [accb])
                act(mqk[:, c, :], acc, AF.Silu, [accb], [b_mqk[c]])
            for c in range(4):
                cp("dve", mhist[:, c, :], mraw[:, c, T:T + 3], [b_mraw[c]], [b_mhist])

            pa, pab = ps_get("C")
            for c in range(4):
                sl = slice(c * 128, (c + 1) * 128)
                act(g_aw[:, 0, :], g_rho[:, sl], AF.Exp, [b_gate], [b_gate], bias=g_ngx[:, c:c + 1], scale=1.0)
                act(g_aw[:, 1, :], g_rho[:, sl], AF.Exp, [b_gate], [b_gate], bias=g_ngx[:, c + 1:c + 2], scale=1.0)
                act(g_E[:, sl], g_cb[:, 1 + c * 128:1 + (c + 1) * 128], AF.Exp, [b_gate], [b_gate], bias=g_ngx[:, c:c + 1], scale=1.0)
                act(g_dec[:, c:c + 1], g_gx[:, c * 128:c * 128 + 1], AF.Exp, [b_gate], [b_gate], bias=g_ngx[:, c + 1:c + 2], scale=1.0)
                tr(pa[:, c * 8:c * 8 + 4], g_aw[:, 0, :], ident[0:4, 0:4], [b_gate, b_k], [pab])
                tr(pa[:, c * 8 + 4:c * 8 + 8], g_aw[:, 1, :], ident[0:4, 0:4], [b_gate, b_k], [pab])
            cp("dve", awT[:, :, :], pa[:, 0:32].rearrange("p (c e) -> p c e", c=4), [pab], [b_awT])
            pdp, pdpb = ps_get("C")
            for j in range(2):
                mm(pdp[:, j * 4:(j + 1) * 4], selpair[:, j, :], g_dec[:, :], True, True, [b_k, b_gate], [pdpb])
            cp("dve", decp[:, :, :], pdp[:, 0:8].rearrange("p (j c) -> p j c", j=2), [pdpb], [b_decp])
            pf, pfb = ps_get("C")
            for c in range(4):
                tr(pf[:, c * 8:(c + 1) * 8], f_cb[:, 1 + c * 128:1 + (c + 1) * 128], ident[0:8, 0:8], [b_fg, b_k], [pfb])
            cp("dve", cbT[:, t * 4:(t + 1) * 4, :], pf[:, 0:32].rearrange("p (c e) -> p c e", c=4), [pfb], [b_cbT])
            ts("dve", f_dg[:, :], ident[0:8, 0:8], f_cb[:, 0:1], None, ALU.mult, None, [b_fg, b_k], [b_fg])
            pr, prb = ps_get("C")
            mm(pr[:, 0:8], ones8[:, :], f_dg[:, :], True, True, [b_k, b_fg], [prb])
            cp("dve", refb[:, :], pr[:, 0:8], [prb], [b_refb])
            nkb = 4 * t + 4
            for kb in range(nkb):
                tt("dve", biasT[:, kb, :], cbT[:, kb, :], refb[:, :], ALU.subtract, [b_cbT, b_refb], [b_biasT])
            chk("conv")

            BANK_S, BANK_N = 3, (6, 7)

            def mlstm_gen():
                for c in range(4):
                    sl = slice(c * 128, (c + 1) * 128)
                    hsl = slice(3 + c * 128, 3 + (c + 1) * 128)
                    pS, pSb = psum[BANK_S], b_ps[BANK_S]
                    hq = []
                    sloc = []
                    for h in range(4):
                        j = h // 2
                        b0 = (h % 2) * 64
                        qT = mqk[b0:b0 + 64, j, sl]
                        kT = mqk[b0:b0 + 64, 2 + j, sl]
                        hq.append((j, b0, qT))
                        if h % 2 == 0:
                            so, sob = psum[BANK_S][:, j * 128:(j + 1) * 128], b_ps[BANK_S]
                        else:
                            so, sob = psum[BANK_N[j]][:, 384:512], b_ps[BANK_N[j]]
                        sloc.append((so, sob))
                        mm(so, kT, qT, True, True, [b_mqk[j], b_mqk[2 + j]], [sob])
                    yield
                    for h in range(4):
                        so, sob = sloc[h]
                        stt("dve", ptm4[:, h, :], so, awT[:, c, h:h + 1], mask01[:, :], ALU.mult, ALU.mult, [sob, b_awT, b_k], [b_ptm4[h]])
                    yield
                    for hp in range(2):
                        for h in (2 * hp, 2 * hp + 1):
                            j, b0, qT = hq[h]
                            pN, pNb = psum[BANK_N[h % 2]], b_ps[BANK_N[h % 2]]
                            pt = ptm4[:, h, :]
                            mm(pN[:, 0:128], cbf[b0:b0 + 64, j, :], qT, True, False, [b_cbf[j], b_mqk[j]], [pNb])
                            mm(pN[:, 0:128], vm[:, c, h, 0:128], pt, False, True, [b_vm[c], b_ptm4[h]], [pNb])
                            mm(pN[:, 128:256], nbb[b0:b0 + 64, j, :], qT, True, False, [b_cbf[j], b_mqk[j]], [pNb])
                            mm(pN[:, 128:256], onesb[:, :], pt, False, True, [b_k, b_ptm4[h]], [pNb])
                            mm(pN[:, 256:384], selrows[:, h, :], g_E[:, sl], True, True, [b_k, b_gate], [pNb])
                        yield
                        for h in (2 * hp, 2 * hp + 1):
                            pN, pNb = psum[BANK_N[h % 2]], b_ps[BANK_N[h % 2]]
                            es, esbb = ring("esb", esb, b_esb)
                            cp("act", es, pN[:, 256:384], [pNb], [esbb])
                            t1, t1b = ring("t1r", t1r, b_t1r)
                            act(t1, pN[:, 128:256], AF.Abs, [pNb], [t1b])
                            tt("dve", t1, t1, es, ALU.max, [t1b, esbb], [t1b])
                            P.op("dve", (lambda o: lambda E: E.reciprocal(out=o, in_=o))(t1), [t1b], [t1b])
                            tt("dve", hm[:, h, hsl], pN[:, 0:128], t1, ALU.mult, [pNb, t1b], [b_hm[h]])
                        yield
                    for j in range(2):
                        mm(pS[:, j * 128:(j + 1) * 128], mqk[:, 2 + j, sl], identb[:, :], True, True, [b_mqk[2 + j], b_k], [pSb])
                    yield
                    for j in range(2):
                        for hh in range(2):
                            ts("dve", kw[:, j, hh * 64:(hh + 1) * 64], pS[:, j * 128 + hh * 64:j * 128 + (hh + 1) * 64], awT[:, c, 4 + 2 * j + hh:5 + 2 * j + hh], None, ALU.mult, None, [pSb, b_awT], [b_kw[j]])
                    yield
                    for j in range(2):
                        pC, pCb = psum[BANK_N[j]], b_ps[BANK_N[j]]
                        mm(pC[:, 0:129], kw[:, j, :], vm[:, c, 2 * j, :], True, True, [b_kw[j], b_vm[c]], [pCb])
                        mm(pC[:, 256:385], kw[:, j, :], vm[:, c, 2 * j + 1, :], True, True, [b_kw[j], b_vm[c]], [pCb])
                    yield
                    for j in range(2):
                        pC, pCb = psum[BANK_N[j]], b_ps[BANK_N[j]]
                        stt("dve", cn[0:64, j, :], cn[0:64, j, :], decp[0:64, j, c:c + 1], pC[0:64, 0:129], ALU.mult, ALU.add, [b_cn[j], b_decp, pCb], [b_cn[j]])
                        stt("dve", cn[64:128, j, :], cn[64:128, j, :], decp[64:128, j, c:c + 1], pC[64:128, 256:385], ALU.mult, ALU.add, [b_cn[j], b_decp, pCb], [b_cn[j]])
                        P.op("act", (lambda o, i: lambda E: E.activation(out=o, in_=i, func=AF.Copy, scale=0.125))(cbf[:, j, :], cn[:, j, 0:128]), [b_cn[j]], [b_cbf[j]])
                        ts("dve", nbb[:, j, :], onesb[:, :], cn[:, j, 128:129], 0.125, ALU.mult, ALU.mult, [b_cn[j], b_k], [b_cbf[j]])
                    yield

            mg = mlstm_gen()
            import os as _os
            if _os.environ.get('MG_FIRST'):
                for _i in range(int(_os.environ.get('MG_STEPS', '1000'))):
                    if next(mg, 'end') == 'end':
                        break
                if 'MG_STEPS' in _os.environ:
                    mg = iter(())
            NFH = int(_os.environ.get('NFH', '8'))

            pools["FS"] = [0, 1, 2]
            pools["FO"] = [4, 5]
            pool_ctr.setdefault("FS", 0)
            pool_ctr.setdefault("FO", 0)
            for h in range(NFH):
                j = h // 2
                b0 = (h % 2) * 64
                pO, pOb = ps_get("FO")
                blocks = []
                for kb in range(nkb):
                    d = kb - 4 * t
                    q0 = 0 if d < 0 else d * 128
                    blocks.append((kb, q0, d >= 0))

                def issue_S(kb, q0, diag):
                    pS, pSb = ps_get("FS")
                    kt = kb // 4
                    mm(pS[:, q0:T], kc[:, j, kb * 128:(kb + 1) * 128], fq[:, h, q0:T], True, False, [b_kc[j][kt], b_fq[h]], [pSb])
                    mm(pS[:, q0:T], sel24[:, h, :], qall[:, q0:T], False, not diag, [b_k, b_qall], [pSb])
                    if diag:
                        mm(pS[:, q0:q0 + 128], identb[:, :], maskneg[:, :], False, True, [b_k], [pSb])
                    return pS, pSb

                pend = []
                LOOK = 2
                for i in range(min(LOOK, len(blocks))):
                    pend.append(issue_S(*blocks[i]))
                for i, (kb, q0, diag) in enumerate(blocks):
                    pS, pSb = pend.pop(0)
                    pt, ptb = ring("ptr", ptr, b_ptr)
                    act(pt[:, q0:T], pS[:, q0:T], AF.Exp, [pSb, b_biasT], [ptb], bias=biasT[:, kb, h:h + 1], scale=0.125)
                    if i + LOOK < len(blocks):
                        pend.append(issue_S(*blocks[i + LOOK]))
                    next(mg, None)
                    mm(pO[0:65, q0:T], vc[:, kb, h, :], pt[:, q0:T], i == 0, i == len(blocks) - 1, [b_vc[kb], ptb], [pOb])
                o_, ob = ring("osb", osb, b_osb)
                cp("act", o_, pO[0:65, :], [pOb], [ob])
                pl, plb = psum[BANK_S], b_ps[BANK_S]
                mm(pl[0:64, :], sel64[:, :], o_[0:65], True, True, [b_k, ob], [plb])
                tl, tlb = ring("f32r", f32r, b_f32r)
                act(tl[0:64], pl[0:64, :], AF.Ln, [plb], [tlb])
                act(tl[0:64], tl[0:64], AF.Exp, [tlb], [tlb], scale=-1.0)
                y0, y0b = ring("f32r", f32r, b_f32r)
                tt("dve", y0[0:64], o_[0:64], tl[0:64], ALU.mult, [ob, tlb], [y0b])
                sq, sqb = ring("sqr", sqr, b_sqr)
                act(sq[0:64], y0[0:64], AF.Square, [y0b], [sqb])
                pn, pnb = psum[BANK_S], b_ps[BANK_S]
                mm(pn[0:64, :], onesb[0:64, 0:64], sq[0:64], True, True, [b_k, sqb], [pnb])
                rs, rb = rstd_from_psum(pn[0:64, :], pnb, 64, 1.0 / 64)
                stt("dve", ycf[:, h, :], y0[0:64], consts[0:64, C_FOUT + h:C_FOUT + h + 1], rs[0:64], ALU.mult, ALU.mult, [y0b, b_consts, rb], [b_ycf[h]])
            for _ in mg:
                pass
            chk("mlstm")

            chk("fox")
            wmo = [w_next(), w_next()]
            for h in range(4):
                ps, pb = sumsq_bcast([(hm[:, h, 3:3 + T], b_hm[h])], 128, onesb[:, :])
                rs, rb = rstd_from_psum(ps[:, :], pb, 128, 1.0 / 128)
                y1, y1b = ring("f32r", f32r, b_f32r)
                stt("dve", y1, hm[:, h, 3:3 + T], ghalf[:, h:h + 1], rs[:, :], ALU.mult, ALU.mult, [b_hm[h], b_ghalf, rb], [y1b])
                wv, wb = wmo[h // 2]
                pso, psob = ps_get("A")
                proj_fm(wv, wb, h % 2, pso, psob)
                th, thb = ring("f32r", f32r, b_f32r)
                act(th, pso[:, :], AF.Tanh, [psob], [thb], scale=0.5)
                stt("dve", ycm[:, h, :], th, 1.0, y1, ALU.add, ALU.mult, [thb, y1b], [b_ycm[h]])
            w_rel(2)

            for dp in range(4):
                wm, wmb = w_next()
                wf, wfb = w_next()
                for jj in range(2):
                    do = 2 * dp + jj
                    ps, pb = ps_get("A")
                    for c in range(4):
                        mm(ps[:, :], wm[:, c, jj * 128:(jj + 1) * 128], ycm[:, c, :], c == 0, False, [wmb, b_ycm[c]], [pb])
                    for h in range(8):
                        mm(ps[:, :], wf[0:64, h, jj * 128:(jj + 1) * 128], ycf[:, h, :], False, h == 7, [wfb, b_ycf[h]], [pb])
                    tt("dve", hT[:, do, :], hT[:, do, :], ps[:, :], ALU.add, [b_h[do], pb], [b_h[do]])
                w_rel(2)

            chk("wout")
            norm_to_uT(C_FFN2)
            ffn()

            chk("ffn2")
            norm_to_uT(C_PG)
            wpieces = [w_next() for _ in range(4)]
            wpp_v, wpp_b = w_next()
            pss, pssb = ps_get("C")
            for do in range(NKC):
                ps, pb = ps_get("A")
                for k in range(2):
                    mm(ps[:, :], wpp_v[:, k, do * 128:(do + 1) * 128], pT[:, k, :], k == 0, k == 1, [wpp_b, b_pT], [pb])
                sq, sqb = ring("sqr", sqr, b_sqr)
                act(sq, ps[:, :], AF.Square, [pb], [sqb])
                mm(pss[:, :], onesb[:, :], sq, do == 0, do == NKC - 1, [sqb, b_k], [pssb])
            rsp, rspb = rstd_from_psum(pss[:, :], pssb, 128, 1.0 / D)
            for do in range(NKC):
                wv, wb = wpieces[do // 2]
                ps, pb = ps_get("A")
                proj_fm(wv, wb, do % 2, ps, pb)
                th, thb = ring("f32r", f32r, b_f32r)
                act(th, ps[:, :], AF.Tanh, [pb], [thb], scale=0.5)
                ps2, pb2 = ps_get("A")
                for k in range(2):
                    mm(ps2[:, :], wpp_v[:, k, do * 128:(do + 1) * 128], pT[:, k, :], k == 0, k == 1, [wpp_b, b_pT], [pb2])
                a1, a1b = ring("f32r", f32r, b_f32r)
                stt("dve", a1, ps2[:, :], ghalf[:, 4 + do:5 + do], rsp[:, :], ALU.mult, ALU.mult, [pb2, b_ghalf, rspb], [a1b])
                stt("dve", a1, th, 1.0, a1, ALU.add, ALU.mult, [thb, a1b], [a1b])
                tt("dve", hT[:, do, :], hT[:, do, :], a1, ALU.add, [b_h[do], a1b], [b_h[do]])
            w_rel(5)

            chk("ple")
            ps, pb = sumsq_bcast([(hT[:, c, :], b_h[c]) for c in range(NKC)], 128, onesb[:, :])
            rs, rb = rstd_from_psum(ps[:, :], pb, 128, 1.0 / D)
            for c in range(NKC):
                o1, o1b = ring("f32r", f32r, b_f32r)
                stt("dve", o1, hT[:, c, :], cc(C_FIN + c), rs[:, :], ALU.mult, ALU.mult, [b_h[c], rb, b_consts], [o1b])
                P.dma("sp", out_d[c * 128:(c + 1) * 128, t0:t0 + T], o1, ("xout", (ring_ctr["f32r"] - 1) % len(b_f32r)), reads=[o1b], is_out=True)

        if debug:
            P.halt = False
            allb = b_h + b_u + b_mqk + b_mraw + b_ycf + b_fq + b_f32r + b_rsr
            P.dma("sp", dbg_d["hT"], hT[:, :, :].rearrange("p a b -> p (a b)"), "dbg0", reads=allb, is_out=True)
            P.dma("pool", dbg_d["uT"], uT[:, :, :].rearrange("p a b -> p (a b)"), "dbg1", reads=allb, is_out=True)
            P.dma("pool", dbg_d["mqk"], mqk[:, :, :].rearrange("p a b -> p (a b)"), "dbg2", reads=allb, is_out=True)
            P.dma("sp", dbg_d["hm"], mraw[:, :, :].rearrange("p a b -> p (a b)"), "dbg3", reads=allb, is_out=True)
            P.dma("pool", dbg_d["ycf"][0:64, :], ycf[:, :, :].rearrange("p a b -> p (a b)"), "dbg4", reads=allb, is_out=True)
            P.dma("pool", dbg_d["fq"], fq[:, :, :].rearrange("p a b -> p (a b)"), "dbg5", reads=allb, is_out=True)
            P.dma("sp", dbg_d["misc"][:, 0:3 * T], f32r[:, :, :].rearrange("p a b -> p (a b)"), "dbg6", reads=allb, is_out=True)
            P.dma("sp", dbg_d["misc"][:, 3 * T:4 * T], rsr[:, 0, :], "dbg7", reads=allb, is_out=True)
            P.dma("pool", dbg_d["ffa"], ffa[:, :, :].rearrange("p a b -> p (a b)"), "dbg8", reads=allb + b_ffa, is_out=True)
        P.emit()
    nc._marks = marks
    nc._nops = {e: len(P.ops[e]) for e in P.ENGS}
    return nc


def _prep_inputs(inputs):
    f = lambda a: np.ascontiguousarray(np.asarray(a, dtype=np.float32))
    consts = np.zeros((128, NCONST), np.float32)

    def put_cols(col, vec, chunk):
        v = f(vec).reshape(-1, chunk)
        consts[:chunk, col:col + v.shape[0]] = v.T

    put_cols(C_FFN1, inputs["ffn1_norm"][0], 128)
    put_cols(C_MIX, inputs["mix_norm"][0], 128)
    put_cols(C_FFN2, inputs["ffn2_norm"][0], 128)
    put_cols(C_PG, inputs["ple_gate_norm"][0], 128)
    put_cols(C_PP, inputs["ple_proj_norm"][0], 128)
    put_cols(C_FIN, inputs["final_norm"], 128)
    put_cols(C_MOUT, inputs["mlstm_out_norm"][0], 128)
    put_cols(C_FOUT, inputs["fox_out_norm"][0], 64)
    cw = f(inputs["conv_qk"][0])
    consts[:, C_CONV:C_CONV + 16] = cw.reshape(4, 4, 128).transpose(2, 1, 0).reshape(128, 16)
    bg = f(inputs["b_mlstm_gates"][0])
    consts[0:4, C_BI] = bg[0:4]
    consts[0:4, C_BF] = bg[4:8]
    consts[0:8, C_BFF] = f(inputs["b_fox_f"][0])
    w_in = f(inputs["w_in"][0])
    w_gates = np.ascontiguousarray(np.concatenate([w_in[:, 1536:1544], w_in[:, 3080:3088]], axis=1))
    shared = {
        "ffn1_w_gate": f(inputs["ffn1_w_gate"][0]), "ffn1_w_up": f(inputs["ffn1_w_up"][0]), "ffn1_w_down": f(inputs["ffn1_w_down"][0]),
        "ffn2_w_gate": f(inputs["ffn2_w_gate"][0]), "ffn2_w_up": f(inputs["ffn2_w_up"][0]), "ffn2_w_down": f(inputs["ffn2_w_down"][0]),
        "w_in": w_in, "w_gates": w_gates, "w_out": f(inputs["w_out"][0]),
        "w_ple_gate": f(inputs["w_ple_gate"][0]), "w_ple_proj": f(inputs["w_ple_proj"][0]), "consts": consts,
    }
    x = np.asarray(inputs["x"], dtype=np.float32)
    p = np.asarray(inputs["p"], dtype=np.float32)[0]
    in_maps = []
    for b in range(8):
        m = dict(shared)
        m["x"] = np.ascontiguousarray(x[b].T)
        m["p"] = np.ascontiguousarray(p[b].T)
        in_maps.append(m)
    return in_maps


def kernel(**inputs):
    nc = build_nc()
    in_maps = _prep_inputs(inputs)
    res = run_bass_kernel_spmd(nc, in_maps, core_ids=list(range(8)))
    return np.stack([np.ascontiguousarray(np.asarray(r["out"], dtype=np.float32).T) for r in res.results], axis=0)
```

```python
import contextlib
import math
import numpy as np
import concourse.bass as bass
import concourse.mybir as mybir
from concourse.bass_utils import run_bass_kernel_spmd

F32 = mybir.dt.float32
BF16 = mybir.dt.bfloat16
AF = mybir.ActivationFunctionType
ALU = mybir.AluOpType

D = 1024
S = 4096
DFF = 2816
NIN = 3088
T = 512
NT = S // T
NKC = D // 128
NHC = DFF // 128
EPS = 1e-6
NSLOT = 5
SLOT_ELEMS = 2048
HALF_A = 12


class Buf:
    __slots__ = ("name", "lw", "rd")

    def __init__(self, name=""):
        self.name = name
        self.lw = None
        self.rd = []


class Prog:
    ENGS = ("pe", "act", "dve", "pool", "sp")

    def __init__(self, nc):
        self.nc = nc
        self.ops = {e: [] for e in self.ENGS}
        self.dma_keys = {}
        self.out_dmas = []
        self.halt = False

    def _deps(self, me, reads, writes):
        deps = set()
        for b in reads:
            if b.lw is not None:
                deps.add(b.lw)
        for b in writes:
            if b.lw is not None:
                deps.add(b.lw)
            deps.update(b.rd)
        deps.discard(me)
        return deps

    def _commit(self, me, reads, writes):
        for b in reads:
            b.rd.append(me)
        for b in writes:
            b.lw = me
            b.rd = []

    def op(self, eng, fn, reads=(), writes=()):
        if self.halt:
            return None
        idx = len(self.ops[eng])
        me = (eng, idx)
        deps = self._deps(me, reads, writes)
        self.ops[eng].append({"fn": fn, "deps": deps, "sig": False, "dma": None})
        self._commit(me, reads, writes)
        return me

    def dma(self, eng, out, in_, key, reads=(), writes=(), is_out=False):
        if self.halt:
            return None
        n = self.dma_keys.get(key, 0) + 1
        self.dma_keys[key] = n
        me = ("dma", key, n)
        deps = self._deps(me, reads, writes)
        self.ops[eng].append({"fn": None, "deps": deps, "sig": False, "dma": (out, in_, key, n)})
        self._commit(me, reads, writes)
        if is_out:
            self.out_dmas.append(me)
        return me

    def emit(self):
        nc = self.nc
        for e in self.ENGS:
            for o in self.ops[e]:
                for d in o["deps"]:
                    if d[0] != "dma":
                        self.ops[d[0]][d[1]]["sig"] = True
        signo = {}
        for e in self.ENGS:
            c = 0
            for i, o in enumerate(self.ops[e]):
                if o["sig"]:
                    c += 1
                    signo[(e, i)] = c
        with contextlib.ExitStack() as st:
            esem = {e: st.enter_context(nc.semaphore("s_" + e)) for e in self.ENGS}
            dsem = {}
            for k in self.dma_keys:
                dsem[k] = st.enter_context(nc.semaphore("d_" + str(k).replace(" ", "").replace("'", "").replace("(", "").replace(")", "").replace(",", "_")))
            block = st.enter_context(nc.Block())

            def run(ename, E):
                known = {}
                for i, o in enumerate(self.ops[ename]):
                    need = {}
                    for d in o["deps"]:
                        if d[0] == "dma":
                            k = ("dma", d[1])
                            v = 16 * d[2]
                        else:
                            if d[0] == ename and ename == "pe":
                                continue
                            k = d[0]
                            v = signo[d]
                        if v > need.get(k, 0):
                            need[k] = v
                    for k, v in need.items():
                        if known.get(k, 0) >= v:
                            continue
                        known[k] = v
                        sem = dsem[k[1]] if isinstance(k, tuple) else esem[k]
                        E.wait_ge(sem, v)
                    if o["dma"] is not None:
                        out, in_, key, n = o["dma"]
                        E.dma_start(out=out, in_=in_).then_inc(dsem[key], 16)
                    else:
                        ins = o["fn"](E)
                        if o["sig"]:
                            ins.then_inc(esem[ename], 1)
                if ename == "sp":
                    last = {}
                    for d in self.out_dmas:
                        last[d[1]] = max(last.get(d[1], 0), d[2])
                    for k, n in last.items():
                        E.wait_ge(dsem[k], 16 * n)

            block.tensor(lambda E: run("pe", E))
            block.scalar(lambda E: run("act", E))
            block.vector(lambda E: run("dve", E))
            block.gpsimd(lambda E: run("pool", E))
            block.sync(lambda E: run("sp", E))


C_FFN1, C_MIX, C_FFN2, C_PG, C_PP, C_FIN = 0, 8, 16, 24, 32, 40
C_MOUT = 48
C_FOUT = 52
C_CONV = 60
C_BI = 76
C_BF = 77
C_BFF = 78
NCONST = 80


class _Stop(Exception):
    pass


def build_nc(ntiles=NT, debug=False, stop=None):
    nc = bass.Bass("TRN2", target_bir_lowering=False)
    dr = lambda name, shape, kind="ExternalInput": nc.dram_tensor(name, shape, F32, kind=kind).ap()
    x_d = dr("x", [D, S])
    p_d = dr("p", [256, S])
    w1g, w1u, w1d = dr("ffn1_w_gate", [D, DFF]), dr("ffn1_w_up", [D, DFF]), dr("ffn1_w_down", [DFF, D])
    w2g, w2u, w2d = dr("ffn2_w_gate", [D, DFF]), dr("ffn2_w_up", [D, DFF]), dr("ffn2_w_down", [DFF, D])
    win_d = dr("w_in", [D, NIN])
    wgates_d = dr("w_gates", [D, 16])
    wout_d = dr("w_out", [D, D])
    wpg_d = dr("w_ple_gate", [D, D])
    wpp_d = dr("w_ple_proj", [256, D])
    consts_d = dr("consts", [128, NCONST])
    out_d = dr("out", [D, S], kind="ExternalOutput")
    dbg_d = {}
    if debug:
        for nm, w in (("hT", NKC * T), ("uT", NKC * T), ("mqk", 4 * T), ("hm", 4 * (3 + T)), ("ycf", 8 * T), ("fq", 8 * T), ("misc", 4 * T), ("ffa", HALF_A * T)):
            dbg_d[nm] = dr("dbg_" + nm, [128, w], kind="ExternalOutput")

    marks = []

    def chk(name):
        marks.append((name, len(P.ops["pe"]), len(P.ops["act"]), len(P.ops["dve"])))
        if stop == name:
            P.halt = True

    P = Prog(nc)
    st = contextlib.ExitStack()
    with st:
        def sb(name, shape, dt=F32):
            return st.enter_context(nc.sbuf_tensor("sb_" + name, shape, dt))

        hT = sb("hT", [128, NKC, T]); b_h = [Buf("h%d" % c) for c in range(NKC)]
        uT = sb("uT", [128, NKC, T], BF16); b_u = [Buf("u%d" % c) for c in range(NKC)]
        ffa = sb("ffa", [128, HALF_A, T], BF16); b_ffa = [Buf("ffa%d" % c) for c in range(HALF_A)]
        kc = sb("kc", [128, 4, S], BF16); b_kc = [[Buf() for _ in range(NT)] for _ in range(4)]
        vc = sb("vc", [128, S // 128, 8, 65], BF16); b_vc = [Buf() for _ in range(S // 128)]
        wring = sb("wring", [128, NSLOT, SLOT_ELEMS], BF16); b_slot = [Buf("slot%d" % i) for i in range(NSLOT)]
        pT = sb("pT", [128, 2, T], BF16); b_pT = Buf("pT")
        consts = sb("consts", [128, NCONST]); b_consts = Buf("consts")
        cneg = sb("cneg", [128, 4]); b_cneg = Buf("cneg")
        ghalf = sb("ghalf", [128, 12]); b_ghalf = Buf("ghalf")
        wgates = sb("wgates", [128, NKC, 16], BF16); b_wgates = Buf("wgates")
        ident = sb("ident", [128, 128]); b_ident = Buf("ident")
        identb = sb("identb", [128, 128], BF16)
        onesb = sb("onesb", [128, 128], BF16)
        ones8 = sb("ones8", [8, 128])
        mask01 = sb("mask01", [128, 128], BF16)
        maskneg = sb("maskneg", [128, 128], BF16)
        selrows = sb("selrows", [4, 4, 128])
        selpair = sb("selpair", [4, 2, 128])
        epsc = sb("epsc", [128, 1])
        zerosf = sb("zerosf", [128, 128]); onesf = sb("onesf", [128, 128])
        sel64 = sb("sel64", [65, 64])
        b_k = Buf("konst")
        sqr = sb("sqr", [128, 2, T], BF16); b_sqr = [Buf() for _ in range(2)]
        f32r = sb("f32r", [128, 3, T]); b_f32r = [Buf() for _ in range(3)]
        rsr = sb("rsr", [128, 1, T]); b_rsr = [Buf() for _ in range(1)]
        ptr = sb("ptr", [128, 4, T], BF16); b_ptr = [Buf() for _ in range(4)]
        mraw = sb("mraw", [128, 4, 3 + T]); b_mraw = [Buf() for _ in range(4)]
        mhist = sb("mhist", [128, 4, 3]); b_mhist = Buf("mhist")
        mqk = sb("mqk", [128, 4, T], BF16); b_mqk = [Buf() for _ in range(4)]
        vm = sb("vm", [128, 4, 4, 129], BF16); b_vm = [Buf() for _ in range(4)]
        fq = sb("fqz", [128, 8, T], BF16); b_fq = [Buf() for _ in range(8)]
        ycm = mqk; b_ycm = b_mqk
        ycf = sb("ycf", [64, 8, T], BF16); b_ycf = [Buf() for _ in range(8)]
        hm = mraw; b_hm = b_mraw
        g_l1 = sb("g_l1", [4, T]); g_cb = sb("g_cb", [4, 1 + T]); g_rho = sb("g_rho", [4, T])
        g_gx = sb("g_gx", [4, 1 + T]); g_ngx = sb("g_ngx", [4, 8])
        g_aw = sb("g_aw", [4, 2, 128]); g_E = sb("g_E", [4, T]); g_dec = sb("g_dec", [4, 4])
        b_gate = Buf("gate_m")
        awT = sb("awT", [128, 4, 8]); b_awT = Buf("awT")
        decp = sb("decp", [128, 2, 4]); b_decp = Buf("decp")
        cn = sb("cn", [128, 2, 129]); b_cn = [Buf(), Buf()]
        cbf = sb("cbf", [128, 2, 128], BF16); nbb = sb("nbb", [128, 2, 128], BF16); b_cbf = [Buf(), Buf()]
        kw = sb("kw", [128, 2, 128], BF16); b_kw = [Buf(), Buf()]
        ptm4 = sb("ptm4", [128, 4, 128], BF16); b_ptm4 = [Buf() for _ in range(4)]
        esb = sb("esb", [128, 2, 128]); b_esb = [Buf(), Buf()]
        t1r = sb("t1r", [128, 2, 128]); b_t1r = [Buf(), Buf()]
        f_l1 = sb("f_l1", [8, T]); f_cb = sb("f_cb", [8, 1 + T]); f_dg = sb("f_dg", [8, 8]); b_fg = Buf("gate_f")
        cbT = sb("cbT", [128, S // 128, 8]); b_cbT = Buf("cbT")
        refb = sb("refb", [128, 8]); b_refb = Buf("refb")
        biasT = sb("biasT", [128, S // 128, 8]); b_biasT = Buf("biasT")
        sel24 = sb("sel24", [128, 8, 128], BF16)
        qall = sb("qall", [128, T], BF16); b_qall = Buf("qall")
        osb = sb("osb", [65, 1, T]); b_osb = [Buf()]
        rspb_t = sb("rsp", [128, T]); b_rsp = Buf("rsp")

        psum = [st.enter_context(nc.psum_tensor("ps%d" % i, [128, T], F32)) for i in range(8)]
        b_ps = [Buf("ps%d" % i) for i in range(8)]
        pools = {"A": [0, 1, 2, 3], "B": [4, 5], "C": [6, 7]}
        pool_ctr = {"A": 0, "B": 0, "C": 0}

        def ps_get(pool):
            lst = pools[pool]
            i = lst[pool_ctr[pool] % len(lst)]
            pool_ctr[pool] += 1
            return psum[i], b_ps[i]

        ring_ctr = {}

        def ring(name, tensor, bufs):
            i = ring_ctr.get(name, 0)
            ring_ctr[name] = i + 1
            j = i % len(bufs)
            return tensor[:, j], bufs[j]

        def cc(col, n=1, rows=128):
            return consts[0:rows, col:col + n]

        pieces = []

        def wpiece(src, parts, a, b):
            pieces.append((src, parts, a, b))

        FFN_HALVES = ((0, HALF_A), (HALF_A, NHC - HALF_A))

        def add_ffn(wg, wu, wd):
            for (c0, nch) in FFN_HALVES:
                for hp in range(nch // 2):
                    col = (c0 + 2 * hp) * 128
                    wpiece(wg[:, col:col + 256].rearrange("(kc p) n -> p kc n", p=128), 128, NKC, 256)
                    wpiece(wu[:, col:col + 256].rearrange("(kc p) n -> p kc n", p=128), 128, NKC, 256)
                for do in range(NKC):
                    wpiece(wd[c0 * 128:(c0 + nch) * 128, do * 128:(do + 1) * 128].rearrange("(j p) n -> p j n", p=128), 128, nch, 128)

        WIN_GROUPS = [0, 256, 512, 768, 1544, 1800, 2056, 2312, 2568, 2824, 1024, 1280]
        for t in range(ntiles):
            add_ffn(w1g, w1u, w1d)
            for c0 in WIN_GROUPS:
                wpiece(win_d[:, c0:c0 + 256].rearrange("(kc p) n -> p kc n", p=128), 128, NKC, 256)
            for dp in range(4):
                wpiece(wout_d[0:512, dp * 256:(dp + 1) * 256].rearrange("(c p) n -> p c n", p=128), 128, 4, 256)
                wpiece(wout_d[512:1024, dp * 256:(dp + 1) * 256].rearrange("(h p) n -> p h n", p=64), 64, 8, 256)
            add_ffn(w2g, w2u, w2d)
            for dp in range(4):
                wpiece(wpg_d[:, dp * 256:(dp + 1) * 256].rearrange("(kc p) n -> p kc n", p=128), 128, NKC, 256)
            wpiece(wpp_d.rearrange("(k p) n -> p k n", p=128), 128, 2, 1024)
        wstate = {"issued": 0, "next": 0}

        npt = len(pieces) // ntiles
        wscr = nc.dram_tensor("wscr_bf16", [npt, 128, SLOT_ELEMS], BF16).ap()
        b_wscr = [Buf() for _ in range(npt)]

        def w_issue_upto(n):
            while wstate["issued"] < min(n, len(pieces)):
                i = wstate["issued"]
                src, parts, a, b = pieces[i]
                s = i % NSLOT
                pidx = i % npt
                flat = wring[0:parts, s, 0:a * b]
                if i < npt:
                    dst = flat.rearrange("p (a b) -> p a b", a=a)
                    P.dma("pool", dst, src, ("w", s), writes=[b_slot[s]])
                    if ntiles > 1:
                        P.dma("sp", wscr[pidx, 0:parts, 0:a * b], flat, ("ws", s), reads=[b_slot[s]], writes=[b_wscr[pidx]])
                else:
                    P.dma("sp", flat, wscr[pidx, 0:parts, 0:a * b], ("w", s), reads=[b_wscr[pidx]], writes=[b_slot[s]])
                wstate["issued"] += 1

        def w_rel(n=1):
            w_issue_upto(wstate["issued"] + n)

        def w_next():
            i = wstate["next"]
            wstate["next"] += 1
            assert i < wstate["issued"] or P.halt or i >= len(pieces), (i, wstate)
            src, parts, a, b = pieces[i]
            s = i % NSLOT
            view = wring[0:parts, s, 0:a * b].rearrange("p (a b) -> p a b", a=a)
            return view, b_slot[s]

        def mm(out, lhsT, rhs, start, stop, reads, writes):
            P.op("pe", lambda E: E.matmul(out, lhsT=lhsT, rhs=rhs, start=start, stop=stop), reads, writes)

        def act(out, in_, func, reads, writes, bias=None, scale=None):
            kw_ = {}
            if bias is not None:
                kw_["bias"] = bias
            if scale is not None:
                kw_["scale"] = scale
            P.op("act", lambda E: E.activation(out=out, in_=in_, func=func, **kw_), reads, writes)

        def stt(eng, out, in0, scalar, in1, op0, op1, reads, writes):
            P.op(eng, lambda E: E.scalar_tensor_tensor(out=out, in0=in0, scalar=scalar, in1=in1, op0=op0, op1=op1), reads, writes)

        def ts(eng, out, in0, s1, s2, op0, op1, reads, writes):
            if op1 is None:
                P.op(eng, lambda E: E.tensor_scalar(out=out, in0=in0, scalar1=s1, scalar2=None, op0=op0), reads, writes)
            else:
                P.op(eng, lambda E: E.tensor_scalar(out=out, in0=in0, scalar1=s1, scalar2=s2, op0=op0, op1=op1), reads, writes)

        def tt(eng, out, in0, in1, op, reads, writes):
            P.op(eng, lambda E: E.tensor_tensor(out=out, in0=in0, in1=in1, op=op), reads, writes)

        def cp(eng, out, in_, reads, writes):
            if eng == "act":
                P.op("act", lambda E: E.copy(out=out, in_=in_), reads, writes)
            else:
                P.op(eng, lambda E: E.tensor_copy(out=out, in_=in_), reads, writes)

        def rstd_from_psum(ps_ap, ps_buf, nparts, inv_n, eps_in_sum=False):
            tmp, tb = ring("f32r", f32r, b_f32r)
            rs, rb = ring("rsr", rsr, b_rsr)
            if eps_in_sum:
                act(tmp[0:nparts], ps_ap, AF.Ln, [ps_buf], [tb])
            else:
                act(tmp[0:nparts], ps_ap, AF.Ln, [ps_buf, b_k], [tb], bias=epsc[0:nparts, 0:1], scale=inv_n)
            act(rs[0:nparts], tmp[0:nparts], AF.Exp, [tb], [rb], scale=-0.5)
            return rs, rb

        def sumsq_bcast(chunks, nparts, lhsT_ones):
            ps, pb = ps_get("C")
            n = len(chunks)
            for i, (ap, b) in enumerate(chunks):
                sq, sqb = ring("sqr", sqr, b_sqr)
                act(sq[0:nparts], ap, AF.Square, [b], [sqb])
                mm(ps[0:lhsT_ones.shape[1], :], lhsT_ones, sq[0:nparts], i == 0, i == n - 1, [sqb, b_k], [pb])
            return ps, pb

        def norm_to_uT(gcol):
            ps, pb = sumsq_bcast([(hT[:, c, :], b_h[c]) for c in range(NKC)], 128, onesb[:, :])
            rs, rb = rstd_from_psum(ps[:, :], pb, 128, 1.0 / D)
            for c in range(NKC):
                stt("dve", uT[:, c, :], hT[:, c, :], cc(gcol + c), rs[:, :], ALU.mult, ALU.mult, [b_h[c], rb, b_consts], [b_u[c]])

        def ffn():
            for (c0, nch) in FFN_HALVES:
                for hp in range(nch // 2):
                    wg_v, wg_b = w_next()
                    wu_v, wu_b = w_next()
                    for j in range(2):
                        hc = 2 * hp + j
                        pg, pgb = ps_get("A")
                        pu, pub = ps_get("A")
                        for k in range(NKC):
                            mm(pg[:, :], wg_v[:, k, j * 128:(j + 1) * 128], uT[:, k, :], k == 0, k == NKC - 1, [wg_b, b_u[k]], [pgb])
                        for k in range(NKC):
                            mm(pu[:, :], wu_v[:, k, j * 128:(j + 1) * 128], uT[:, k, :], k == 0, k == NKC - 1, [wu_b, b_u[k]], [pub])
                        sg, sgb = ring("f32r", f32r, b_f32r)
                        act(sg, pg[:, :], AF.Silu, [pgb], [sgb])
                        tt("dve", ffa[:, hc, :], sg, pu[:, :], ALU.mult, [sgb, pub], [b_ffa[hc]])
                    w_rel(2)
                for do in range(NKC):
                    wd_v, wd_b = w_next()
                    pd, pdb = ps_get("B")
                    for j in range(nch):
                        mm(pd[:, :], wd_v[:, j, :], ffa[:, j, :], j == 0, j == nch - 1, [wd_b, b_ffa[j]], [pdb])
                    w_rel(1)
                    stt("dve", hT[:, do, :], pd[:, :], 0.5, hT[:, do, :], ALU.mult, ALU.add, [pdb, b_h[do]], [b_h[do]])

        P.dma("sp", consts[:, :], consts_d[:, :], "c0", writes=[b_consts])
        P.dma("pool", wgates[:, :, :], wgates_d.rearrange("(kc p) n -> p kc n", p=128), "c1", writes=[b_wgates])
        kops = []
        kops.append(lambda E: E.memset(zerosf[:, :], 0.0))
        kops.append(lambda E: E.memset(onesf[:, :], 1.0))
        kops.append(lambda E: E.memset(epsc[:, :], EPS))
        kops.append(lambda E: E.memset(onesb[:, :], 1.0))
        kops.append(lambda E: E.memset(ones8[:, :], 1.0))
        kops.append(lambda E: E.affine_select(out=ident[:, :], in_=onesf[:, :], pattern=[[1, 128]], compare_op=ALU.is_equal, fill=0.0, base=0, channel_multiplier=-1))
        kops.append(lambda E: E.tensor_copy(out=identb[:, :], in_=ident[:, :]))
        kops.append(lambda E: E.memset(zerosf[:, :], 0.125))
        kops.append(lambda E: E.affine_select(out=mask01[:, :], in_=zerosf[:, :], pattern=[[1, 128]], compare_op=ALU.is_ge, fill=0.0, base=0, channel_multiplier=-1))
        kops.append(lambda E: E.memset(zerosf[:, :], 0.0))
        kops.append(lambda E: E.affine_select(out=maskneg[:, :], in_=zerosf[:, :], pattern=[[1, 128]], compare_op=ALU.is_ge, fill=-30000.0, base=0, channel_multiplier=-1))
        kops.append(lambda E: E.memset(selrows[:, :, :], 1.0))
        kops.append(lambda E: E.memset(selpair[:, :, :], 1.0))
        kops.append(lambda E: E.affine_select(out=selrows[:, :, :], in_=selrows[:, :, :], pattern=[[1, 4], [0, 128]], compare_op=ALU.is_equal, fill=0.0, base=0, channel_multiplier=-1))
        kops.append(lambda E: E.affine_select(out=selpair[:, :, :].rearrange("p j (a b) -> p j a b", a=2), in_=selpair[:, :, :].rearrange("p j (a b) -> p j a b", a=2), pattern=[[2, 2], [1, 2], [0, 64]], compare_op=ALU.is_equal, fill=0.0, base=0, channel_multiplier=-1))
        kops.append(lambda E: E.memset(sel64[0:64, :], 0.0))
        kops.append(lambda E: E.memset(sel64[64:65, :], 1.0))
        kops.append(lambda E: E.memset(vc[:, :, :, 64:65], 1.0))
        kops.append(lambda E: E.memset(vm[:, :, :, 128:129], 1.0))
        kops.append(lambda E: E.memset(cn[:, :, :], 0.0))
        kops.append(lambda E: E.memset(cbf[:, :, :], 0.0))
        kops.append(lambda E: E.memset(nbb[:, :, :], 0.0))
        kops.append(lambda E: E.memset(mhist[:, :, :], 0.0))
        kops.append(lambda E: E.memset(g_cb[:, 0:1], 0.0))
        kops.append(lambda E: E.memset(g_gx[:, 0:1], 0.0))
        kops.append(lambda E: E.memset(f_cb[:, 0:1], 0.0))
        kops.append(lambda E: E.memset(sel24[:, :, :], 0.0))
        kops.append(lambda E: E.memset(qall[:, :], 0.0))
        kops.append(lambda E: E.memset(fq[:, :, :], 0.0))
        kops.append(lambda E: E.memset(sel24[0:8, :, :], 1.0))
        kops.append(lambda E: E.affine_select(out=sel24[0:8, :, :], in_=sel24[0:8, :, :], pattern=[[1, 8], [0, 128]], compare_op=ALU.is_equal, fill=0.0, base=0, channel_multiplier=-1))
        for f in kops:
            P.op("pool", f, [], [b_k])
        for i_, r0_ in enumerate((8, 16)):
            P.dma("sp", sel24[r0_:r0_ + 8, :, :], sel24[0:8, :, :], "c%d" % (2 + i_), reads=[b_k], writes=[b_k])
        for b in b_vc + b_vm + b_cn + b_cbf + b_fq + [b_mhist, b_gate, b_fg, b_qall]:
            b.lw = b_k.lw
        ts("dve", cneg[0:4, 0:1], consts[0:4, C_BF:C_BF + 1], -1.0, None, ALU.mult, None, [b_consts], [b_cneg])
        ts("dve", cneg[0:8, 1:2], consts[0:8, C_BFF:C_BFF + 1], -1.0, None, ALU.mult, None, [b_consts], [b_cneg])
        ts("dve", ghalf[:, 0:4], consts[:, C_MOUT:C_MOUT + 4], 0.5, None, ALU.mult, None, [b_consts], [b_ghalf])
        ts("dve", ghalf[:, 4:12], consts[:, C_PP:C_PP + 8], 0.5, None, ALU.mult, None, [b_consts], [b_ghalf])

        w_issue_upto(NSLOT)

        def tr(out, in_, idn, reads, writes):
            P.op("pe", lambda E: E.transpose(out=out, in_=in_, identity=idn), reads, writes)

        def proj_fm(wv, wb, jj, ps, pb):
            for k in range(NKC):
                mm(ps[:, :], wv[:, k, jj * 128:(jj + 1) * 128], uT[:, k, :], k == 0, k == NKC - 1, [wb, b_u[k]], [pb])

        for t in (range(ntiles) if True else []):
            t0 = t * T
            for c in range(NKC):
                P.dma("sp", hT[:, c, :], x_d[c * 128:(c + 1) * 128, t0:t0 + T], ("xin", c), writes=[b_h[c]])
            for k in range(2):
                P.dma("pool", pT[:, k, :], p_d[k * 128:(k + 1) * 128, t0:t0 + T], ("pin", k), writes=[b_pT])

            chk("load")
            norm_to_uT(C_FFN1)
            chk("norm1")
            ffn()
            chk("ffn1")

            norm_to_uT(C_MIX)
            pg_i, pg_ib = ps_get("C")
            for k in range(NKC):
                mm(pg_i[0:4, :], wgates[:, k, 0:4], uT[:, k, :], k == 0, k == NKC - 1, [b_wgates, b_u[k]], [pg_ib])
            act(g_rho[:, :], pg_i[0:4, :], AF.Identity, [pg_ib, b_consts], [b_gate], bias=consts[0:4, C_BI:C_BI + 1], scale=1.0)
            pg_f, pg_fb = ps_get("C")
            for k in range(NKC):
                mm(pg_f[0:4, :], wgates[:, k, 4:8], uT[:, k, :], k == 0, k == NKC - 1, [b_wgates, b_u[k]], [pg_fb])
            act(g_l1[:, :], pg_f[0:4, :], AF.Exp, [pg_fb, b_cneg], [b_gate], bias=cneg[0:4, 0:1], scale=-1.0)
            pg_ff, pg_ffb = ps_get("C")
            for k in range(NKC):
                mm(pg_ff[0:8, :], wgates[:, k, 8:16], uT[:, k, :], k == 0, k == NKC - 1, [b_wgates, b_u[k]], [pg_ffb])
            act(f_l1[:, :], pg_ff[0:8, :], AF.Exp, [pg_ffb, b_cneg], [b_fg], bias=cneg[0:8, 1:2], scale=-1.0)
            act(g_l1[:, :], g_l1[:, :], AF.Ln, [b_gate], [b_gate], bias=1.0, scale=1.0)
            act(f_l1[:, :], f_l1[:, :], AF.Ln, [b_fg], [b_fg], bias=1.0, scale=1.0)
            if t > 0:
                cp("dve", g_cb[:, 0:1], g_cb[:, T:T + 1], [b_gate], [b_gate])
                cp("dve", g_gx[:, 0:1], g_gx[:, T:T + 1], [b_gate], [b_gate])
                cp("dve", f_cb[:, 0:1], f_cb[:, T:T + 1], [b_fg], [b_fg])
            P.op("dve", lambda E: E.tensor_tensor_scan(out=g_cb[:, 1:1 + T], data0=g_l1[:, :], data1=g_l1[:, :], initial=g_cb[:, 0:1], op0=ALU.add, op1=ALU.max), [b_gate], [b_gate])
            tt("dve", g_rho[:, :], g_rho[:, :], g_cb[:, 1:1 + T], ALU.add, [b_gate], [b_gate])
            P.op("dve", lambda E: E.tensor_tensor_scan(out=g_gx[:, 1:1 + T], data0=g_rho[:, :], data1=g_rho[:, :], initial=g_gx[:, 0:1], op0=ALU.max, op1=ALU.max), [b_gate], [b_gate])
            for c5 in range(5):
                ts("dve", g_ngx[:, c5:c5 + 1], g_gx[:, c5 * 128:c5 * 128 + 1], -1.0, None, ALU.mult, None, [b_gate], [b_gate])
            P.op("dve", lambda E: E.tensor_tensor_scan(out=f_cb[:, 1:1 + T], data0=f_l1[:, :], data1=f_l1[:, :], initial=f_cb[:, 0:1], op0=ALU.add, op1=ALU.max), [b_fg], [b_fg])
            vq, vqb = ring("f32r", f32r, b_f32r)
            ts("dve", vq[0:8], f_cb[:, 1:1 + T], f_cb[:, 0:1], -8.0, ALU.subtract, ALU.mult, [b_fg], [vqb])
            for r3 in range(3):
                pc_, pcb = ring("sqr", sqr, b_sqr)
                cp("dve", pc_[0:8], vq[0:8], [vqb], [pcb])
                P.dma("sp", qall[r3 * 8:(r3 + 1) * 8, :], pc_[0:8], ("qa", r3), reads=[pcb], writes=[b_qall])
                if r3 < 2:
                    tt("dve", vq[0:8], vq[0:8], pc_[0:8], ALU.subtract, [vqb, pcb], [vqb])

            for c in range(4):
                cp("dve", mraw[:, c, 0:3], mhist[:, c, :], [b_mhist], [b_mraw[c]])
            for pi in range(2):
                wv, wb = w_next()
                for jj in range(2):
                    c = 2 * pi + jj
                    ps, pb = ps_get("A")
                    proj_fm(wv, wb, jj, ps, pb)
                    cp("act", mraw[:, c, 3:3 + T], ps[:, :], [pb], [b_mraw[c]])
                w_rel(1)
            wva, wba = w_next()
            wvb, wbb = w_next()
            for blk in range(4):
                ps, pb = ps_get("A")
                for half, (wv, wb) in enumerate(((wva, wba), (wvb, wbb))):
                    for k in range(NKC):
                        mm(ps[:, half * 256:(half + 1) * 256], uT[:, k, blk * 128:(blk + 1) * 128], wv[:, k, :], k == 0, k == NKC - 1, [wb, b_u[k]], [pb])
                cp("dve", vm[:, blk, :, 0:128], ps[:, :].rearrange("p (h d) -> p h d", h=4), [pb], [b_vm[blk]])
            w_rel(2)
            for pi in range(2):
                wv, wb = w_next()
                for jj in range(2):
                    c = 2 * pi + jj
                    ps, pb = ps_get("A")
                    proj_fm(wv, wb, jj, ps, pb)
                    cp("dve", fq[0:64, 2 * c, :], ps[0:64, :], [pb], [b_fq[2 * c]])
                    cp("dve", fq[64:128, 2 * c + 1, :], ps[64:128, :], [pb], [b_fq[2 * c + 1]])
                w_rel(1)
            for pi in range(2):
                wv, wb = w_next()
                for jj in range(2):
                    c = 2 * pi + jj
                    ps, pb = ps_get("A")
                    proj_fm(wv, wb, jj, ps, pb)
                    cp("act", kc[:, c, t0:t0 + T], ps[:, :], [pb], [b_kc[c][t]])
                w_rel(1)
            wva, wba = w_next()
            wvb, wbb = w_next()
            for blk in range(4):
                ps, pb = ps_get("A")
                for half, (wv, wb) in enumerate(((wva, wba), (wvb, wbb))):
                    for k in range(NKC):
                        mm(ps[:, half * 256:(half + 1) * 256], uT[:, k, blk * 128:(blk + 1) * 128], wv[:, k, :], k == 0, k == NKC - 1, [wb, b_u[k]], [pb])
                cp("dve", vc[:, t * 4 + blk, :, 0:64], ps[:, :].rearrange("p (h d) -> p h d", h=8), [pb], [b_vc[t * 4 + blk]])
            w_rel(2)

            pa, pab = ps_get("C")
            for c in range(4):
                sl = slice(c * 128, (c + 1) * 128)
                act(g_aw[:, 0, :], g_rho[:, sl], AF.Exp, [b_gate], [b_gate], bias=g_ngx[:, c:c + 1], scale=1.0)
                act(g_aw[:, 1, :], g_rho[:, sl], AF.Exp, [b_gate], [b_gate], bias=g_ngx[:, c + 1:c + 2], scale=1.0)
                act(g_E[:, sl], g_cb[:, 1 + c * 128:1 + (c + 1) * 128], AF.Exp, [b_gate], [b_gate], bias=g_ngx[:, c:c + 1], scale=1.0)
                act(g_dec[:, c:c + 1], g_gx[:, c * 128:c * 128 + 1], AF.Exp, [b_gate], [b_gate], bias=g_ngx[:, c + 1:c + 2], scale=1.0)
                tr(pa[:, c * 8:c * 8 + 4], g_aw[:, 0, :], ident[0:4, 0:4], [b_gate, b_k], [pab])
                tr(pa[:, c * 8 + 4:c * 8 + 8], g_aw[:, 1, :], ident[0:4, 0:4], [b_gate, b_k], [pab])
            cp("dve", awT[:, :, :], pa[:, 0:32].rearrange("p (c e) -> p c e", c=4), [pab], [b_awT])
            pdp, pdpb = ps_get("C")
            for j in range(2):
                mm(pdp[:, j * 4:(j + 1) * 4], selpair[:, j, :], g_dec[:, :], True, True, [b_k, b_gate], [pdpb])
            cp("dve", decp[:, :, :], pdp[:, 0:8].rearrange("p (j c) -> p j c", j=2), [pdpb], [b_decp])
            pf, pfb = ps_get("C")
            for c in range(4):
                tr(pf[:, c * 8:(c + 1) * 8], f_cb[:, 1 + c * 128:1 + (c + 1) * 128], ident[0:8, 0:8], [b_fg, b_k], [pfb])
            cp("dve", cbT[:, t * 4:(t + 1) * 4, :], pf[:, 0:32].rearrange("p (c e) -> p c e", c=4), [pfb], [b_cbT])
            ts("dve", f_dg[:, :], ident[0:8, 0:8], f_cb[:, 0:1], None, ALU.mult, None, [b_fg, b_k], [b_fg])
            pr, prb = ps_get("C")
            mm(pr[:, 0:8], ones8[:, :], f_dg[:, :], True, True, [b_k, b_fg], [prb])
            cp("dve", refb[:, :], pr[:, 0:8], [prb], [b_refb])
            nkb = 4 * t + 4
            for kb in range(nkb):
                tt("dve", biasT[:, kb, :], cbT[:, kb, :], refb[:, :], ALU.subtract, [b_cbT, b_refb], [b_biasT])
            for c in range(4):
                acc, accb = ring("f32r", f32r, b_f32r)
                ts("dve", acc, mraw[:, c, 0:T], cc(C_CONV + c * 4 + 0), None, ALU.mult, None, [b_mraw[c], b_consts], [accb])
                for j in range(1, 4):
                    stt("dve", acc, mraw[:, c, j:j + T], cc(C_CONV + c * 4 + j), acc, ALU.mult, ALU.add, [b_mraw[c], b_consts, accb], [accb])
                act(mqk[:, c, :], acc, AF.Silu, [accb], [b_mqk[c]])
            for c in range(4):
                cp("dve", mhist[:, c, :], mraw[:, c, T:T + 3], [b_mraw[c]], [b_mhist])

            chk("conv")

            BANK_S, BANK_N = 3, (6, 7)

            def mlstm_gen():
                for c in range(4):
                    sl = slice(c * 128, (c + 1) * 128)
                    hsl = slice(3 + c * 128, 3 + (c + 1) * 128)
                    pS, pSb = psum[BANK_S], b_ps[BANK_S]
                    hq = []
                    sloc = []
                    for h in range(4):
                        j = h // 2
                        b0 = (h % 2) * 64
                        qT = mqk[b0:b0 + 64, j, sl]
                        kT = mqk[b0:b0 + 64, 2 + j, sl]
                        hq.append((j, b0, qT))
                        if h % 2 == 0:
                            so, sob = psum[BANK_S][:, j * 128:(j + 1) * 128], b_ps[BANK_S]
                        else:
                            so, sob = psum[BANK_N[j]][:, 384:512], b_ps[BANK_N[j]]
                        sloc.append((so, sob))
                        mm(so, kT, qT, True, True, [b_mqk[j], b_mqk[2 + j]], [sob])
                    yield
                    for h in range(4):
                        so, sob = sloc[h]
                        stt("dve", ptm4[:, h, :], so, awT[:, c, h:h + 1], mask01[:, :], ALU.mult, ALU.mult, [sob, b_awT, b_k], [b_ptm4[h]])
                    yield
                    for hp in range(2):
                        for h in (2 * hp, 2 * hp + 1):
                            j, b0, qT = hq[h]
                            pN, pNb = psum[BANK_N[h % 2]], b_ps[BANK_N[h % 2]]
                            pt = ptm4[:, h, :]
                            mm(pN[:, 0:128], cbf[b0:b0 + 64, j, :], qT, True, False, [b_cbf[j], b_mqk[j]], [pNb])
                            mm(pN[:, 0:128], vm[:, c, h, 0:128], pt, False, True, [b_vm[c], b_ptm4[h]], [pNb])
                            mm(pN[:, 128:256], nbb[b0:b0 + 64, j, :], qT, True, False, [b_cbf[j], b_mqk[j]], [pNb])
                            mm(pN[:, 128:256], onesb[:, :], pt, False, True, [b_k, b_ptm4[h]], [pNb])
                            mm(pN[:, 256:384], selrows[:, h, :], g_E[:, sl], True, True, [b_k, b_gate], [pNb])
                        yield
                        for h in (2 * hp, 2 * hp + 1):
                            pN, pNb = psum[BANK_N[h % 2]], b_ps[BANK_N[h % 2]]
                            es, esbb = ring("esb", esb, b_esb)
                            cp("act", es, pN[:, 256:384], [pNb], [esbb])
                            t1, t1b = ring("t1r", t1r, b_t1r)
                            act(t1, pN[:, 128:256], AF.Abs, [pNb], [t1b])
                            tt("dve", t1, t1, es, ALU.max, [t1b, esbb], [t1b])
                            P.op("dve", (lambda o: lambda E: E.reciprocal(out=o, in_=o))(t1), [t1b], [t1b])
                            tt("dve", hm[:, h, hsl], pN[:, 0:128], t1, ALU.mult, [pNb, t1b], [b_hm[h]])
                        yield
                    for j in range(2):
                        mm(pS[:, j * 128:(j + 1) * 128], mqk[:, 2 + j, sl], identb[:, :], True, True, [b_mqk[2 + j], b_k], [pSb])
                    yield
                    for j in range(2):
                        for hh in range(2):
                            ts("dve", kw[:, j, hh * 64:(hh + 1) * 64], pS[:, j * 128 + hh * 64:j * 128 + (hh + 1) * 64], awT[:, c, 4 + 2 * j + hh:5 + 2 * j + hh], None, ALU.mult, None, [pSb, b_awT], [b_kw[j]])
                    yield
                    for j in range(2):
                        pC, pCb = psum[BANK_N[j]], b_ps[BANK_N[j]]
                        mm(pC[:, 0:129], kw[:, j, :], vm[:, c, 2 * j, :], True, True, [b_kw[j], b_vm[c]], [pCb])
                        mm(pC[:, 256:385], kw[:, j, :], vm[:, c, 2 * j + 1, :], True, True, [b_kw[j], b_vm[c]], [pCb])
                    yield
                    for j in range(2):
                        pC, pCb = psum[BANK_N[j]], b_ps[BANK_N[j]]
                        stt("dve", cn[0:64, j, :], cn[0:64, j, :], decp[0:64, j, c:c + 1], pC[0:64, 0:129], ALU.mult, ALU.add, [b_cn[j], b_decp, pCb], [b_cn[j]])
                        stt("dve", cn[64:128, j, :], cn[64:128, j, :], decp[64:128, j, c:c + 1], pC[64:128, 256:385], ALU.mult, ALU.add, [b_cn[j], b_decp, pCb], [b_cn[j]])
                        P.op("act", (lambda o, i: lambda E: E.activation(out=o, in_=i, func=AF.Copy, scale=0.125))(cbf[:, j, :], cn[:, j, 0:128]), [b_cn[j]], [b_cbf[j]])
                        ts("dve", nbb[:, j, :], onesb[:, :], cn[:, j, 128:129], 0.125, ALU.mult, ALU.mult, [b_cn[j], b_k], [b_cbf[j]])
                    yield

            mg = mlstm_gen()
            import os as _os
            if _os.environ.get('MG_FIRST'):
                for _i in range(int(_os.environ.get('MG_STEPS', '1000'))):
                    if next(mg, 'end') == 'end':
                        break
                if 'MG_STEPS' in _os.environ:
                    mg = iter(())
            NFH = int(_os.environ.get('NFH', '8'))

            pools["FS"] = [0, 1, 2]
            pools["FO"] = [4]
            pool_ctr.setdefault("FS", 0)
            pool_ctr.setdefault("FO", 0)
            def fox_fin_gen(h, pO, pOb):
                o_, ob = ring("osb", osb, b_osb)
                cp("act", o_, pO[0:65, :], [pOb], [ob])
                yield
                pl, plb = psum[5], b_ps[5]
                mm(pl[0:64, :], sel64[:, :], o_[0:65], True, True, [b_k, ob], [plb])
                yield
                tl, tlb = ring("f32r", f32r, b_f32r)
                act(tl[0:64], pl[0:64, :], AF.Ln, [plb], [tlb])
                act(tl[0:64], tl[0:64], AF.Exp, [tlb], [tlb], scale=-1.0)
                yield
                y0, y0b = ring("f32r", f32r, b_f32r)
                tt("dve", y0[0:64], o_[0:64], tl[0:64], ALU.mult, [ob, tlb], [y0b])
                sq, sqb = ring("sqr", sqr, b_sqr)
                tt("dve", sq[0:64], y0[0:64], y0[0:64], ALU.mult, [y0b], [sqb])
                yield
                pn, pnb = psum[5], b_ps[5]
                mm(pn[0:64, :], onesb[0:64, 0:64], sq[0:64], True, True, [b_k, sqb], [pnb])
                yield
                rs, rb = rstd_from_psum(pn[0:64, :], pnb, 64, 1.0 / 64)
                yield
                stt("dve", ycf[:, h, :], y0[0:64], consts[0:64, C_FOUT + h:C_FOUT + h + 1], rs[0:64], ALU.mult, ALU.mult, [y0b, b_consts, rb], [b_ycf[h]])

            fin = iter(())
            for h in range(NFH):
                j = h // 2
                b0 = (h % 2) * 64
                pO, pOb = ps_get("FO")
                blocks = []
                for kb in range(nkb):
                    d = kb - 4 * t
                    q0 = 0 if d < 0 else d * 128
                    blocks.append((kb, q0, d >= 0))

                def issue_S(kb, q0, diag):
                    pS, pSb = ps_get("FS")
                    kt = kb // 4
                    mm(pS[:, q0:T], kc[:, j, kb * 128:(kb + 1) * 128], fq[:, h, q0:T], True, False, [b_kc[j][kt], b_fq[h]], [pSb])
                    mm(pS[:, q0:T], sel24[:, h, :], qall[:, q0:T], False, not diag, [b_k, b_qall], [pSb])
                    if diag:
                        mm(pS[:, q0:q0 + 128], identb[:, :], maskneg[:, :], False, True, [b_k], [pSb])
                    return pS, pSb

                pend = []
                LOOK = 2
                for i in range(min(LOOK, len(blocks))):
                    pend.append(issue_S(*blocks[i]))
                for i, (kb, q0, diag) in enumerate(blocks):
                    pS, pSb = pend.pop(0)
                    pt, ptb = ring("ptr", ptr, b_ptr)
                    act(pt[:, q0:T], pS[:, q0:T], AF.Exp, [pSb, b_biasT], [ptb], bias=biasT[:, kb, h:h + 1], scale=0.125)
                    if i + LOOK < len(blocks):
                        pend.append(issue_S(*blocks[i + LOOK]))
                    next(mg, None)
                    next(fin, None)
                    mm(pO[0:65, q0:T], vc[:, kb, h, :], pt[:, q0:T], i == 0, i == len(blocks) - 1, [b_vc[kb], ptb], [pOb])
                for _ in fin:
                    pass
                fin = fox_fin_gen(h, pO, pOb)
                next(fin, None)
            for _ in fin:
                pass
            for _ in mg:
                pass
            chk("mlstm")

            chk("fox")
            wmo = [w_next(), w_next()]
            for h in range(4):
                ps, pb = sumsq_bcast([(hm[:, h, 3:3 + T], b_hm[h])], 128, onesb[:, :])
                rs, rb = rstd_from_psum(ps[:, :], pb, 128, 1.0 / 128)
                y1, y1b = ring("f32r", f32r, b_f32r)
                stt("dve", y1, hm[:, h, 3:3 + T], ghalf[:, h:h + 1], rs[:, :], ALU.mult, ALU.mult, [b_hm[h], b_ghalf, rb], [y1b])
                wv, wb = wmo[h // 2]
                pso, psob = ps_get("A")
                proj_fm(wv, wb, h % 2, pso, psob)
                th, thb = ring("f32r", f32r, b_f32r)
                act(th, pso[:, :], AF.Tanh, [psob], [thb], scale=0.5)
                stt("dve", ycm[:, h, :], th, 1.0, y1, ALU.add, ALU.mult, [thb, y1b], [b_ycm[h]])
            w_rel(2)

            for dp in range(4):
                wm, wmb = w_next()
                wf, wfb = w_next()
                for jj in range(2):
                    do = 2 * dp + jj
                    ps, pb = ps_get("A")
                    for c in range(4):
                        mm(ps[:, :], wm[:, c, jj * 128:(jj + 1) * 128], ycm[:, c, :], c == 0, False, [wmb, b_ycm[c]], [pb])
                    for h in range(8):
                        mm(ps[:, :], wf[0:64, h, jj * 128:(jj + 1) * 128], ycf[:, h, :], False, h == 7, [wfb, b_ycf[h]], [pb])
                    tt("dve", hT[:, do, :], hT[:, do, :], ps[:, :], ALU.add, [b_h[do], pb], [b_h[do]])
                w_rel(2)

            chk("wout")
            norm_to_uT(C_FFN2)
            ffn()

            chk("ffn2")
            wpieces = [w_next() for _ in range(4)]
            wpp_v, wpp_b = w_next()
            pss, pssb = ps_get("C")
            for do in range(NKC):
                ps, pb = ps_get("A")
                for k in range(2):
                    mm(ps[:, :], wpp_v[:, k, do * 128:(do + 1) * 128], pT[:, k, :], k == 0, k == 1, [wpp_b, b_pT], [pb])
                sq, sqb = ring("sqr", sqr, b_sqr)
                act(sq, ps[:, :], AF.Square, [pb], [sqb])
                mm(pss[:, :], onesb[:, :], sq, do == 0, do == NKC - 1, [sqb, b_k], [pssb])
            tl_, tlb_ = ring("f32r", f32r, b_f32r)
            act(tl_, pss[:, :], AF.Ln, [pssb, b_k], [tlb_], bias=epsc[:, 0:1], scale=1.0 / D)
            act(rspb_t[:, :], tl_, AF.Exp, [tlb_], [b_rsp], scale=-0.5)
            rsp, rspb = rspb_t, b_rsp
            norm_to_uT(C_PG)
            for do in range(NKC):
                wv, wb = wpieces[do // 2]
                ps, pb = ps_get("A")
                proj_fm(wv, wb, do % 2, ps, pb)
                th, thb = ring("f32r", f32r, b_f32r)
                act(th, ps[:, :], AF.Tanh, [pb], [thb], scale=0.5)
                ps2, pb2 = ps_get("A")
                for k in range(2):
                    mm(ps2[:, :], wpp_v[:, k, do * 128:(do + 1) * 128], pT[:, k, :], k == 0, k == 1, [wpp_b, b_pT], [pb2])
                a1, a1b = ring("f32r", f32r, b_f32r)
                stt("dve", a1, ps2[:, :], ghalf[:, 4 + do:5 + do], rsp[:, :], ALU.mult, ALU.mult, [pb2, b_ghalf, rspb], [a1b])
                stt("dve", a1, th, 1.0, a1, ALU.add, ALU.mult, [thb, a1b], [a1b])
                tt("dve", hT[:, do, :], hT[:, do, :], a1, ALU.add, [b_h[do], a1b], [b_h[do]])
                if do % 2 == 1:
                    w_rel(1)
            w_rel(1)

            chk("ple")
            ps, pb = sumsq_bcast([(hT[:, c, :], b_h[c]) for c in range(NKC)], 128, onesb[:, :])
            rs, rb = rstd_from_psum(ps[:, :], pb, 128, 1.0 / D)
            for c in range(NKC):
                o1, o1b = ring("f32r", f32r, b_f32r)
                stt("dve", o1, hT[:, c, :], cc(C_FIN + c), rs[:, :], ALU.mult, ALU.mult, [b_h[c], rb, b_consts], [o1b])
                P.dma("sp", out_d[c * 128:(c + 1) * 128, t0:t0 + T], o1, ("xout", (ring_ctr["f32r"] - 1) % len(b_f32r)), reads=[o1b], is_out=True)

        if debug:
            P.halt = False
            allb = b_h + b_u + b_mqk + b_mraw + b_ycf + b_fq + b_f32r + b_rsr
            P.dma("sp", dbg_d["hT"], hT[:, :, :].rearrange("p a b -> p (a b)"), "dbg0", reads=allb, is_out=True)
            P.dma("pool", dbg_d["uT"], uT[:, :, :].rearrange("p a b -> p (a b)"), "dbg1", reads=allb, is_out=True)
            P.dma("pool", dbg_d["mqk"], mqk[:, :, :].rearrange("p a b -> p (a b)"), "dbg2", reads=allb, is_out=True)
            P.dma("sp", dbg_d["hm"], mraw[:, :, :].rearrange("p a b -> p (a b)"), "dbg3", reads=allb, is_out=True)
            P.dma("pool", dbg_d["ycf"][0:64, :], ycf[:, :, :].rearrange("p a b -> p (a b)"), "dbg4", reads=allb, is_out=True)
            P.dma("pool", dbg_d["fq"], fq[:, :, :].rearrange("p a b -> p (a b)"), "dbg5", reads=allb, is_out=True)
            P.dma("sp", dbg_d["misc"][:, 0:3 * T], f32r[:, :, :].rearrange("p a b -> p (a b)"), "dbg6", reads=allb, is_out=True)
            P.dma("sp", dbg_d["misc"][:, 3 * T:4 * T], rsr[:, 0, :], "dbg7", reads=allb, is_out=True)
            P.dma("pool", dbg_d["ffa"], ffa[:, :, :].rearrange("p a b -> p (a b)"), "dbg8", reads=allb + b_ffa, is_out=True)
        P.emit()
    nc._marks = marks
    nc._nops = {e: len(P.ops[e]) for e in P.ENGS}
    return nc


def _prep_inputs(inputs):
    f = lambda a: np.ascontiguousarray(np.asarray(a, dtype=np.float32))
    consts = np.zeros((128, NCONST), np.float32)

    def put_cols(col, vec, chunk):
        v = f(vec).reshape(-1, chunk)
        consts[:chunk, col:col + v.shape[0]] = v.T

    put_cols(C_FFN1, inputs["ffn1_norm"][0], 128)
    put_cols(C_MIX, inputs["mix_norm"][0], 128)
    put_cols(C_FFN2, inputs["ffn2_norm"][0], 128)
    put_cols(C_PG, inputs["ple_gate_norm"][0], 128)
    put_cols(C_PP, inputs["ple_proj_norm"][0], 128)
    put_cols(C_FIN, inputs["final_norm"], 128)
    put_cols(C_MOUT, inputs["mlstm_out_norm"][0], 128)
    put_cols(C_FOUT, inputs["fox_out_norm"][0], 64)
    cw = f(inputs["conv_qk"][0])
    consts[:, C_CONV:C_CONV + 16] = cw.reshape(4, 4, 128).transpose(2, 1, 0).reshape(128, 16)
    bg = f(inputs["b_mlstm_gates"][0])
    consts[0:4, C_BI] = bg[0:4]
    consts[0:4, C_BF] = bg[4:8]
    consts[0:8, C_BFF] = f(inputs["b_fox_f"][0])
    w_in = f(inputs["w_in"][0])
    w_gates = np.ascontiguousarray(np.concatenate([w_in[:, 1536:1544], w_in[:, 3080:3088]], axis=1))
    shared = {
        "ffn1_w_gate": f(inputs["ffn1_w_gate"][0]), "ffn1_w_up": f(inputs["ffn1_w_up"][0]), "ffn1_w_down": f(inputs["ffn1_w_down"][0]),
        "ffn2_w_gate": f(inputs["ffn2_w_gate"][0]), "ffn2_w_up": f(inputs["ffn2_w_up"][0]), "ffn2_w_down": f(inputs["ffn2_w_down"][0]),
        "w_in": w_in, "w_gates": w_gates, "w_out": f(inputs["w_out"][0]),
        "w_ple_gate": f(inputs["w_ple_gate"][0]), "w_ple_proj": f(inputs["w_ple_proj"][0]), "consts": consts,
    }
    x = np.asarray(inputs["x"], dtype=np.float32)
    p = np.asarray(inputs["p"], dtype=np.float32)[0]
    in_maps = []
    for b in range(8):
        m = dict(shared)
        m["x"] = np.ascontiguousarray(x[b].T)
        m["p"] = np.ascontiguousarray(p[b].T)
        in_maps.append(m)
    return in_maps


def kernel(**inputs):
    nc = build_nc()
    in_maps = _prep_inputs(inputs)
    res = run_bass_kernel_spmd(nc, in_maps, core_ids=list(range(8)))
    return np.stack([np.ascontiguousarray(np.asarray(r["out"], dtype=np.float32).T) for r in res.results], axis=0)
```

```python
import contextlib
import math
import numpy as np
import concourse.bass as bass
import concourse.mybir as mybir
from concourse.bass_utils import run_bass_kernel_spmd

F32 = mybir.dt.float32
BF16 = mybir.dt.bfloat16
AF = mybir.ActivationFunctionType
ALU = mybir.AluOpType

D = 1024
S = 4096
DFF = 2816
NIN = 3088
T = 512
NT = S // T
NKC = D // 128
NHC = DFF // 128
EPS = 1e-6
NSLOT = 6
SLOT_ELEMS = 2048
HALF_A = 12


class Buf:
    __slots__ = ("name", "lw", "rd")

    def __init__(self, name=""):
        self.name = name
        self.lw = None
        self.rd = []


class Prog:
    ENGS = ("pe", "act", "dve", "pool", "sp")

    def __init__(self, nc):
        self.nc = nc
        self.ops = {e: [] for e in self.ENGS}
        self.dma_keys = {}
        self.out_dmas = []
        self.halt = False

    def _deps(self, me, reads, writes):
        deps = set()
        for b in reads:
            if b.lw is not None:
                deps.add(b.lw)
        for b in writes:
            if b.lw is not None:
                deps.add(b.lw)
            deps.update(b.rd)
        deps.discard(me)
        return deps

    def _commit(self, me, reads, writes):
        for b in reads:
            b.rd.append(me)
        for b in writes:
            b.lw = me
            b.rd = []

    def op(self, eng, fn, reads=(), writes=()):
        if self.halt:
            return None
        idx = len(self.ops[eng])
        me = (eng, idx)
        deps = self._deps(me, reads, writes)
        self.ops[eng].append({"fn": fn, "deps": deps, "sig": False, "dma": None})
        self._commit(me, reads, writes)
        return me

    def dma(self, eng, out, in_, key, reads=(), writes=(), is_out=False):
        if self.halt:
            return None
        n = self.dma_keys.get(key, 0) + 1
        self.dma_keys[key] = n
        me = ("dma", key, n)
        deps = self._deps(me, reads, writes)
        self.ops[eng].append({"fn": None, "deps": deps, "sig": False, "dma": (out, in_, key, n)})
        self._commit(me, reads, writes)
        if is_out:
            self.out_dmas.append(me)
        return me

    def emit(self):
        nc = self.nc
        for e in self.ENGS:
            for o in self.ops[e]:
                for d in o["deps"]:
                    if d[0] != "dma":
                        self.ops[d[0]][d[1]]["sig"] = True
        signo = {}
        for e in self.ENGS:
            c = 0
            for i, o in enumerate(self.ops[e]):
                if o["sig"]:
                    c += 1
                    signo[(e, i)] = c
        with contextlib.ExitStack() as st:
            esem = {e: st.enter_context(nc.semaphore("s_" + e)) for e in self.ENGS}
            dsem = {}
            for k in self.dma_keys:
                dsem[k] = st.enter_context(nc.semaphore("d_" + str(k).replace(" ", "").replace("'", "").replace("(", "").replace(")", "").replace(",", "_")))
            block = st.enter_context(nc.Block())

            def run(ename, E):
                known = {}
                for i, o in enumerate(self.ops[ename]):
                    need = {}
                    for d in o["deps"]:
                        if d[0] == "dma":
                            k = ("dma", d[1])
                            v = 16 * d[2]
                        else:
                            if d[0] == ename and ename == "pe":
                                continue
                            k = d[0]
                            v = signo[d]
                        if v > need.get(k, 0):
                            need[k] = v
                    for k, v in need.items():
                        if known.get(k, 0) >= v:
                            continue
                        known[k] = v
                        sem = dsem[k[1]] if isinstance(k, tuple) else esem[k]
                        E.wait_ge(sem, v)
                    if o["dma"] is not None:
                        out, in_, key, n = o["dma"]
                        E.dma_start(out=out, in_=in_).then_inc(dsem[key], 16)
                    else:
                        ins = o["fn"](E)
                        if o["sig"]:
                            ins.then_inc(esem[ename], 1)
                if ename == "sp":
                    last = {}
                    for d in self.out_dmas:
                        last[d[1]] = max(last.get(d[1], 0), d[2])
                    for k, n in last.items():
                        E.wait_ge(dsem[k], 16 * n)

            block.tensor(lambda E: run("pe", E))
            block.scalar(lambda E: run("act", E))
            block.vector(lambda E: run("dve", E))
            block.gpsimd(lambda E: run("pool", E))
            block.sync(lambda E: run("sp", E))


C_FFN1, C_MIX, C_FFN2, C_PG, C_PP, C_FIN = 0, 8, 16, 24, 32, 40
C_MOUT = 48
C_FOUT = 52
C_CONV = 60
C_BI = 76
C_BF = 77
C_BFF = 78
NCONST = 80


class _Stop(Exception):
    pass


def build_nc(ntiles=NT, debug=False, stop=None):
    import os as _osm
    _os_dq = bool(_osm.environ.get('DQ_SP'))
    nc = bass.Bass("TRN2", target_bir_lowering=False)
    dr = lambda name, shape, kind="ExternalInput": nc.dram_tensor(name, shape, F32, kind=kind).ap()
    x_d = dr("x", [D, S])
    p_d = dr("p", [256, S])
    w1g, w1u, w1d = dr("ffn1_w_gate", [D, DFF]), dr("ffn1_w_up", [D, DFF]), dr("ffn1_w_down", [DFF, D])
    w2g, w2u, w2d = dr("ffn2_w_gate", [D, DFF]), dr("ffn2_w_up", [D, DFF]), dr("ffn2_w_down", [DFF, D])
    win_d = dr("w_in", [D, NIN])
    wgates_d = dr("w_gates", [D, 16])
    wout_d = dr("w_out", [D, D])
    wpg_d = dr("w_ple_gate", [D, D])
    wpp_d = dr("w_ple_proj", [256, D])
    consts_d = dr("consts", [128, NCONST])
    out_d = dr("out", [D, S], kind="ExternalOutput")
    dbg_d = {}
    if debug:
        for nm, w in (("hT", NKC * T), ("uT", NKC * T), ("mqk", 4 * T), ("hm", 4 * (3 + T)), ("ycf", 8 * T), ("fq", 8 * T), ("misc", 4 * T), ("ffa", HALF_A * T)):
            dbg_d[nm] = dr("dbg_" + nm, [128, w], kind="ExternalOutput")

    marks = []

    def chk(name):
        marks.append((name, len(P.ops["pe"]), len(P.ops["act"]), len(P.ops["dve"])))
        if stop == name:
            P.halt = True

    P = Prog(nc)
    st = contextlib.ExitStack()
    with st:
        def sb(name, shape, dt=F32):
            return st.enter_context(nc.sbuf_tensor("sb_" + name, shape, dt))

        hT = sb("hT", [128, NKC, T]); b_h = [Buf("h%d" % c) for c in range(NKC)]
        uT = sb("uT", [128, NKC, T], BF16); b_u = [Buf("u%d" % c) for c in range(NKC)]
        ffa = sb("ffa", [128, HALF_A, T], BF16); b_ffa = [Buf("ffa%d" % c) for c in range(HALF_A)]
        kc = sb("kc", [128, 4, S], BF16); b_kc = [[Buf() for _ in range(NT)] for _ in range(4)]
        vc = sb("vc", [128, S // 128, 8, 65], BF16); b_vc = [Buf() for _ in range(S // 128)]
        wring = sb("wring", [128, NSLOT, SLOT_ELEMS], BF16); b_slot = [Buf("slot%d" % i) for i in range(NSLOT)]
        pT = sb("pT", [128, 2, T], BF16); b_pT = Buf("pT")
        consts = sb("consts", [128, NCONST]); b_consts = Buf("consts")
        cneg = sb("cneg", [128, 4]); b_cneg = Buf("cneg")
        ghalf = sb("ghalf", [128, 12]); b_ghalf = Buf("ghalf")
        wgates = sb("wgates", [128, NKC, 16], BF16); b_wgates = Buf("wgates")
        ident = sb("ident", [128, 128]); b_ident = Buf("ident")
        identb = sb("identb", [128, 128], BF16)
        onesb = sb("onesb", [128, 128], BF16)
        mask01 = sb("mask01", [128, 128], BF16)
        maskneg = sb("maskneg", [128, 128], BF16)
        selrows = sb("selrows", [4, 4, 128])
        selpair = sb("selpair", [4, 2, 128])
        epsc = sb("epsc", [128, 1])
        zerosf = None; onesf = None
        sel64 = sb("sel64", [65, 64])
        b_k = Buf("konst")
        sqr = sb("sqr", [128, 2, T], BF16); b_sqr = [Buf() for _ in range(2)]
        f32r = sb("f32r", [128, 3, T]); b_f32r = [Buf() for _ in range(3)]
        zerosf = f32r[:, 0, 0:128]; onesf = f32r[:, 1, 0:128]
        rsr = sb("rsr", [128, 1, T]); b_rsr = [Buf() for _ in range(1)]
        ptr = sb("ptr", [128, 3, T], BF16); b_ptr = [Buf() for _ in range(3)]
        mraw = sb("mraw", [128, 4, 3 + T]); b_mraw = [Buf() for _ in range(4)]
        mhist = sb("mhist", [128, 4, 3]); b_mhist = Buf("mhist")
        mqk = sb("mqk", [128, 4, T], BF16); b_mqk = [Buf() for _ in range(4)]
        vm = sb("vm", [128, 4, 4, 129], BF16); b_vm = [Buf() for _ in range(4)]
        fq = sb("fqz", [128, 8, T], BF16); b_fq = [Buf() for _ in range(8)]
        ycm = mqk; b_ycm = b_mqk
        ycf = sb("ycf", [64, 8, T], BF16); b_ycf = [Buf() for _ in range(8)]
        hm = mraw; b_hm = b_mraw
        g_l1 = sb("g_l1", [4, T]); g_cb = sb("g_cb", [4, 1 + T]); g_rho = sb("g_rho", [4, T])
        g_gx = sb("g_gx", [4, 1 + T]); g_ngx = sb("g_ngx", [4, 8])
        g_aw = sb("g_aw", [4, 2, 128]); g_E = sb("g_E", [4, T]); g_dec = sb("g_dec", [4, 4])
        b_gate = Buf("gate_m")
        awT = sb("awT", [128, 4, 8]); b_awT = Buf("awT")
        decp = sb("decp", [128, 2, 4]); b_decp = Buf("decp")
        cn = sb("cn", [128, 2, 129]); b_cn = [Buf(), Buf()]
        cbf = sb("cbf", [128, 2, 128], BF16); nbb = sb("nbb", [128, 2, 128], BF16); b_cbf = [Buf(), Buf()]
        kw = sb("kw", [128, 2, 128], BF16); b_kw = [Buf(), Buf()]
        ptm4 = sb("ptm4", [128, 4, 128], BF16); b_ptm4 = [Buf() for _ in range(4)]
        esb = sb("esb", [128, 2, 128]); b_esb = [Buf(), Buf()]
        t1r = sb("t1r", [128, 2, 128]); b_t1r = [Buf(), Buf()]
        f_l1 = sb("f_l1", [8, T]); f_cb = sb("f_cb", [8, 1 + T]); b_fg = Buf("gate_f")
        cbT = sb("cbT", [128, S // 128, 8]); b_cbT = Buf("cbT")
        sel24 = sb("sel24", [128, 8, 128], BF16)
        qall = sb("qall", [128, T], BF16); b_qall = Buf("qall")
        osb = sb("osb", [65, 1, T]); b_osb = [Buf()]
        rspb_t = sb("rsp", [128, T]); b_rsp = Buf("rsp")

        psum = [st.enter_context(nc.psum_tensor("ps%d" % i, [128, T], F32)) for i in range(8)]
        b_ps = [Buf("ps%d" % i) for i in range(8)]
        pools = {"A": [0, 1, 2, 3], "B": [4, 5], "C": [6, 7]}
        pool_ctr = {"A": 0, "B": 0, "C": 0}

        def ps_get(pool):
            lst = pools[pool]
            i = lst[pool_ctr[pool] % len(lst)]
            pool_ctr[pool] += 1
            return psum[i], b_ps[i]

        ring_ctr = {}

        def ring(name, tensor, bufs):
            i = ring_ctr.get(name, 0)
            ring_ctr[name] = i + 1
            j = i % len(bufs)
            return tensor[:, j], bufs[j]

        def cc(col, n=1, rows=128):
            return consts[0:rows, col:col + n]

        pieces = []

        def wpiece(src, parts, a, b):
            pieces.append((src, parts, a, b))

        FFN_HALVES = ((0, HALF_A), (HALF_A, NHC - HALF_A))

        def add_ffn(wg, wu, wd):
            for (c0, nch) in FFN_HALVES:
                for hp in range(nch // 2):
                    col = (c0 + 2 * hp) * 128
                    wpiece(wg[:, col:col + 256].rearrange("(kc p) n -> p kc n", p=128), 128, NKC, 256)
                    wpiece(wu[:, col:col + 256].rearrange("(kc p) n -> p kc n", p=128), 128, NKC, 256)
                for do in range(NKC):
                    wpiece(wd[c0 * 128:(c0 + nch) * 128, do * 128:(do + 1) * 128].rearrange("(j p) n -> p j n", p=128), 128, nch, 128)

        WIN_GROUPS = [0, 256, 512, 768, 1544, 1800, 2056, 2312, 2568, 2824, 1024, 1280]
        for t in range(ntiles):
            add_ffn(w1g, w1u, w1d)
            for c0 in WIN_GROUPS:
                wpiece(win_d[:, c0:c0 + 256].rearrange("(kc p) n -> p kc n", p=128), 128, NKC, 256)
            for dp in range(4):
                wpiece(wout_d[0:512, dp * 256:(dp + 1) * 256].rearrange("(c p) n -> p c n", p=128), 128, 4, 256)
                wpiece(wout_d[512:1024, dp * 256:(dp + 1) * 256].rearrange("(h p) n -> p h n", p=64), 64, 8, 256)
            add_ffn(w2g, w2u, w2d)
            for dp in range(4):
                wpiece(wpg_d[:, dp * 256:(dp + 1) * 256].rearrange("(kc p) n -> p kc n", p=128), 128, NKC, 256)
            wpiece(wpp_d.rearrange("(k p) n -> p k n", p=128), 128, 2, 1024)
        wstate = {"issued": 0, "next": 0}

        npt = len(pieces) // ntiles
        wscr = nc.dram_tensor("wscr_bf16", [npt, 128, SLOT_ELEMS], BF16).ap()
        b_wscr = [Buf() for _ in range(npt)]

        def w_issue_upto(n):
            while wstate["issued"] < min(n, len(pieces)):
                i = wstate["issued"]
                src, parts, a, b = pieces[i]
                s = i % NSLOT
                pidx = i % npt
                flat = wring[0:parts, s, 0:a * b]
                if i < npt:
                    dst = flat.rearrange("p (a b) -> p a b", a=a)
                    P.dma("pool", dst, src, ("w", s), writes=[b_slot[s]])
                    if ntiles > 1:
                        P.dma("sp", wscr[pidx, 0:parts, 0:a * b], flat, ("ws", s), reads=[b_slot[s]], writes=[b_wscr[pidx]])
                else:
                    P.dma("sp", flat, wscr[pidx, 0:parts, 0:a * b], ("w", s), reads=[b_wscr[pidx]], writes=[b_slot[s]])
                wstate["issued"] += 1

        def w_rel(n=1):
            w_issue_upto(wstate["issued"] + n)

        def w_next():
            i = wstate["next"]
            wstate["next"] += 1
            assert i < wstate["issued"] or P.halt or i >= len(pieces), (i, wstate)
            src, parts, a, b = pieces[i]
            s = i % NSLOT
            view = wring[0:parts, s, 0:a * b].rearrange("p (a b) -> p a b", a=a)
            return view, b_slot[s]

        def mm(out, lhsT, rhs, start, stop, reads, writes):
            P.op("pe", lambda E: E.matmul(out, lhsT=lhsT, rhs=rhs, start=start, stop=stop), reads, writes)

        def act(out, in_, func, reads, writes, bias=None, scale=None):
            kw_ = {}
            if bias is not None:
                kw_["bias"] = bias
            if scale is not None:
                kw_["scale"] = scale
            P.op("act", lambda E: E.activation(out=out, in_=in_, func=func, **kw_), reads, writes)

        def stt(eng, out, in0, scalar, in1, op0, op1, reads, writes):
            P.op(eng, lambda E: E.scalar_tensor_tensor(out=out, in0=in0, scalar=scalar, in1=in1, op0=op0, op1=op1), reads, writes)

        def ts(eng, out, in0, s1, s2, op0, op1, reads, writes):
            if op1 is None:
                P.op(eng, lambda E: E.tensor_scalar(out=out, in0=in0, scalar1=s1, scalar2=None, op0=op0), reads, writes)
            else:
                P.op(eng, lambda E: E.tensor_scalar(out=out, in0=in0, scalar1=s1, scalar2=s2, op0=op0, op1=op1), reads, writes)

        def tt(eng, out, in0, in1, op, reads, writes):
            P.op(eng, lambda E: E.tensor_tensor(out=out, in0=in0, in1=in1, op=op), reads, writes)

        def cp(eng, out, in_, reads, writes):
            if eng == "act":
                P.op("act", lambda E: E.copy(out=out, in_=in_), reads, writes)
            else:
                P.op(eng, lambda E: E.tensor_copy(out=out, in_=in_), reads, writes)

        def rstd_from_psum(ps_ap, ps_buf, nparts, inv_n, eps_in_sum=False):
            tmp, tb = ring("f32r", f32r, b_f32r)
            rs, rb = ring("rsr", rsr, b_rsr)
            if eps_in_sum:
                act(tmp[0:nparts], ps_ap, AF.Ln, [ps_buf], [tb])
            else:
                act(tmp[0:nparts], ps_ap, AF.Ln, [ps_buf, b_k], [tb], bias=epsc[0:nparts, 0:1], scale=inv_n)
            act(rs[0:nparts], tmp[0:nparts], AF.Exp, [tb], [rb], scale=-0.5)
            return rs, rb

        def sumsq_bcast(chunks, nparts, lhsT_ones):
            ps, pb = ps_get("C")
            n = len(chunks)
            for i, (ap, b) in enumerate(chunks):
                sq, sqb = ring("sqr", sqr, b_sqr)
                act(sq[0:nparts], ap, AF.Square, [b], [sqb])
                mm(ps[0:lhsT_ones.shape[1], :], lhsT_ones, sq[0:nparts], i == 0, i == n - 1, [sqb, b_k], [pb])
            return ps, pb

        def norm_to_uT(gcol):
            ps, pb = sumsq_bcast([(hT[:, c, :], b_h[c]) for c in range(NKC)], 128, onesb[:, :])
            rs, rb = rstd_from_psum(ps[:, :], pb, 128, 1.0 / D)
            for c in range(NKC):
                stt("dve", uT[:, c, :], hT[:, c, :], cc(gcol + c), rs[:, :], ALU.mult, ALU.mult, [b_h[c], rb, b_consts], [b_u[c]])

        def ffn():
            for (c0, nch) in FFN_HALVES:
                for hp in range(nch // 2):
                    wg_v, wg_b = w_next()
                    wu_v, wu_b = w_next()
                    for j in range(2):
                        hc = 2 * hp + j
                        pg, pgb = ps_get("A")
                        pu, pub = ps_get("A")
                        for k in range(NKC):
                            mm(pg[:, :], wg_v[:, k, j * 128:(j + 1) * 128], uT[:, k, :], k == 0, k == NKC - 1, [wg_b, b_u[k]], [pgb])
                        for k in range(NKC):
                            mm(pu[:, :], wu_v[:, k, j * 128:(j + 1) * 128], uT[:, k, :], k == 0, k == NKC - 1, [wu_b, b_u[k]], [pub])
                        sg, sgb = ring("f32r", f32r, b_f32r)
                        act(sg, pg[:, :], AF.Silu, [pgb], [sgb])
                        tt("dve", ffa[:, hc, :], sg, pu[:, :], ALU.mult, [sgb, pub], [b_ffa[hc]])
                    w_rel(2)
                for do in range(NKC):
                    wd_v, wd_b = w_next()
                    pd, pdb = ps_get("B")
                    for j in range(nch):
                        mm(pd[:, :], wd_v[:, j, :], ffa[:, j, :], j == 0, j == nch - 1, [wd_b, b_ffa[j]], [pdb])
                    w_rel(1)
                    stt("dve", hT[:, do, :], pd[:, :], 0.5, hT[:, do, :], ALU.mult, ALU.add, [pdb, b_h[do]], [b_h[do]])

        P.dma("sp", consts[:, :], consts_d[:, :], "c0", writes=[b_consts])
        P.dma("pool", wgates[:, :, :], wgates_d.rearrange("(kc p) n -> p kc n", p=128), "c1", writes=[b_wgates])
        kops = []
        kops.append(lambda E: E.memset(zerosf, 0.0))
        kops.append(lambda E: E.memset(onesf, 1.0))
        kops.append(lambda E: E.memset(epsc[:, :], EPS))
        kops.append(lambda E: E.memset(onesb[:, :], 1.0))
        kops.append(lambda E: E.affine_select(out=ident[:, :], in_=onesf, pattern=[[1, 128]], compare_op=ALU.is_equal, fill=0.0, base=0, channel_multiplier=-1))
        kops.append(lambda E: E.tensor_copy(out=identb[:, :], in_=ident[:, :]))
        kops.append(lambda E: E.memset(zerosf, 0.125))
        kops.append(lambda E: E.affine_select(out=mask01[:, :], in_=zerosf, pattern=[[1, 128]], compare_op=ALU.is_ge, fill=0.0, base=0, channel_multiplier=-1))
        kops.append(lambda E: E.memset(zerosf, 0.0))
        kops.append(lambda E: E.affine_select(out=maskneg[:, :], in_=zerosf, pattern=[[1, 128]], compare_op=ALU.is_ge, fill=-30000.0, base=0, channel_multiplier=-1))
        kops.append(lambda E: E.memset(selrows[:, :, :], 1.0))
        kops.append(lambda E: E.memset(selpair[:, :, :], 1.0))
        kops.append(lambda E: E.affine_select(out=selrows[:, :, :], in_=selrows[:, :, :], pattern=[[1, 4], [0, 128]], compare_op=ALU.is_equal, fill=0.0, base=0, channel_multiplier=-1))
        kops.append(lambda E: E.affine_select(out=selpair[:, :, :].rearrange("p j (a b) -> p j a b", a=2), in_=selpair[:, :, :].rearrange("p j (a b) -> p j a b", a=2), pattern=[[2, 2], [1, 2], [0, 64]], compare_op=ALU.is_equal, fill=0.0, base=0, channel_multiplier=-1))
        kops.append(lambda E: E.memset(sel64[0:64, :], 0.0))
        kops.append(lambda E: E.memset(sel64[64:65, :], 1.0))
        kops.append(lambda E: E.memset(vc[:, :, :, 64:65], 1.0))
        kops.append(lambda E: E.memset(vm[:, :, :, 128:129], 1.0))
        kops.append(lambda E: E.memset(cn[:, :, :], 0.0))
        kops.append(lambda E: E.memset(cbf[:, :, :], 0.0))
        kops.append(lambda E: E.memset(nbb[:, :, :], 0.0))
        kops.append(lambda E: E.memset(mhist[:, :, :], 0.0))
        kops.append(lambda E: E.memset(g_cb[:, 0:1], 0.0))
        kops.append(lambda E: E.memset(g_gx[:, 0:1], 0.0))
        kops.append(lambda E: E.memset(f_cb[:, 0:1], 0.0))
        kops.append(lambda E: E.memset(sel24[:, :, :], 0.0))
        kops.append(lambda E: E.memset(qall[:, :], 0.0))
        kops.append(lambda E: E.memset(fq[:, :, :], 0.0))
        kops.append(lambda E: E.memset(sel24[0:8, :, :], 1.0))
        kops.append(lambda E: E.affine_select(out=sel24[0:8, :, :], in_=sel24[0:8, :, :], pattern=[[1, 8], [0, 128]], compare_op=ALU.is_equal, fill=0.0, base=0, channel_multiplier=-1))
        for f in kops:
            P.op("pool", f, [], [b_k])
        for i_, r0_ in enumerate((8, 16)):
            P.dma("sp", sel24[r0_:r0_ + 8, :, :], sel24[0:8, :, :], "c%d" % (2 + i_), reads=[b_k], writes=[b_k])
        for b in b_vc + b_vm + b_cn + b_cbf + b_fq + b_f32r + [b_mhist, b_gate, b_fg, b_qall]:
            b.lw = b_k.lw
        ts("dve", cneg[0:4, 0:1], consts[0:4, C_BF:C_BF + 1], -1.0, None, ALU.mult, None, [b_consts], [b_cneg])
        ts("dve", cneg[0:8, 1:2], consts[0:8, C_BFF:C_BFF + 1], -1.0, None, ALU.mult, None, [b_consts], [b_cneg])
        ts("dve", ghalf[:, 0:4], consts[:, C_MOUT:C_MOUT + 4], 0.5, None, ALU.mult, None, [b_consts], [b_ghalf])
        ts("dve", ghalf[:, 4:12], consts[:, C_PP:C_PP + 8], 0.5, None, ALU.mult, None, [b_consts], [b_ghalf])

        w_issue_upto(NSLOT)

        def tr(out, in_, idn, reads, writes):
            P.op("pe", lambda E: E.transpose(out=out, in_=in_, identity=idn), reads, writes)

        def proj_fm(wv, wb, jj, ps, pb):
            for k in range(NKC):
                mm(ps[:, :], wv[:, k, jj * 128:(jj + 1) * 128], uT[:, k, :], k == 0, k == NKC - 1, [wb, b_u[k]], [pb])

        for t in (range(ntiles) if True else []):
            t0 = t * T
            for c in range(NKC):
                P.dma("sp" if (t == 0 or _os_dq) else "pool", hT[:, c, :], x_d[c * 128:(c + 1) * 128, t0:t0 + T], ("xin", c), writes=[b_h[c]])
            for k in range(2):
                P.dma("pool", pT[:, k, :], p_d[k * 128:(k + 1) * 128, t0:t0 + T], ("pin", k), writes=[b_pT])

            chk("load")
            norm_to_uT(C_FFN1)
            chk("norm1")
            ffn()
            chk("ffn1")

            norm_to_uT(C_MIX)
            pg_i, pg_ib = ps_get("C")
            for k in range(NKC):
                mm(pg_i[0:4, :], wgates[:, k, 0:4], uT[:, k, :], k == 0, k == NKC - 1, [b_wgates, b_u[k]], [pg_ib])
            act(g_rho[:, :], pg_i[0:4, :], AF.Identity, [pg_ib, b_consts], [b_gate], bias=consts[0:4, C_BI:C_BI + 1], scale=1.0)
            pg_f, pg_fb = ps_get("C")
            for k in range(NKC):
                mm(pg_f[0:4, :], wgates[:, k, 4:8], uT[:, k, :], k == 0, k == NKC - 1, [b_wgates, b_u[k]], [pg_fb])
            act(g_l1[:, :], pg_f[0:4, :], AF.Exp, [pg_fb, b_cneg], [b_gate], bias=cneg[0:4, 0:1], scale=-1.0)
            pg_ff, pg_ffb = ps_get("C")
            for k in range(NKC):
                mm(pg_ff[0:8, :], wgates[:, k, 8:16], uT[:, k, :], k == 0, k == NKC - 1, [b_wgates, b_u[k]], [pg_ffb])
            act(f_l1[:, :], pg_ff[0:8, :], AF.Exp, [pg_ffb, b_cneg], [b_fg], bias=cneg[0:8, 1:2], scale=-1.0)
            act(g_l1[:, :], g_l1[:, :], AF.Ln, [b_gate], [b_gate], bias=1.0, scale=1.0)
            act(f_l1[:, :], f_l1[:, :], AF.Ln, [b_fg], [b_fg], bias=1.0, scale=1.0)
            if t > 0:
                cp("dve", g_cb[:, 0:1], g_cb[:, T:T + 1], [b_gate], [b_gate])
                cp("dve", g_gx[:, 0:1], g_gx[:, T:T + 1], [b_gate], [b_gate])
                cp("dve", f_cb[:, 0:1], f_cb[:, T:T + 1], [b_fg], [b_fg])
            P.op("dve", lambda E: E.tensor_tensor_scan(out=g_cb[:, 1:1 + T], data0=g_l1[:, :], data1=g_l1[:, :], initial=g_cb[:, 0:1], op0=ALU.add, op1=ALU.max), [b_gate], [b_gate])
            tt("dve", g_rho[:, :], g_rho[:, :], g_cb[:, 1:1 + T], ALU.add, [b_gate], [b_gate])
            P.op("dve", lambda E: E.tensor_tensor_scan(out=g_gx[:, 1:1 + T], data0=g_rho[:, :], data1=g_rho[:, :], initial=g_gx[:, 0:1], op0=ALU.max, op1=ALU.max), [b_gate], [b_gate])
            for c5 in range(5):
                ts("dve", g_ngx[:, c5:c5 + 1], g_gx[:, c5 * 128:c5 * 128 + 1], -1.0, None, ALU.mult, None, [b_gate], [b_gate])
            P.op("dve", lambda E: E.tensor_tensor_scan(out=f_cb[:, 1:1 + T], data0=f_l1[:, :], data1=f_l1[:, :], initial=f_cb[:, 0:1], op0=ALU.add, op1=ALU.max), [b_fg], [b_fg])
            vq, vqb = ring("f32r", f32r, b_f32r)
            ts("dve", vq[0:8], f_cb[:, 1:1 + T], -8.0, None, ALU.mult, None, [b_fg], [vqb])
            for r3 in range(3):
                pc_, pcb = ring("sqr", sqr, b_sqr)
                cp("dve", pc_[0:8], vq[0:8], [vqb], [pcb])
                P.dma("sp" if (t == 0 or _os_dq) else "pool", qall[r3 * 8:(r3 + 1) * 8, :], pc_[0:8], ("qa", r3), reads=[pcb], writes=[b_qall])
                if r3 < 2:
                    tt("dve", vq[0:8], vq[0:8], pc_[0:8], ALU.subtract, [vqb, pcb], [vqb])

            for c in range(4):
                cp("dve", mraw[:, c, 0:3], mhist[:, c, :], [b_mhist], [b_mraw[c]])
            for pi in range(2):
                wv, wb = w_next()
                for jj in range(2):
                    c = 2 * pi + jj
                    ps, pb = ps_get("A")
                    proj_fm(wv, wb, jj, ps, pb)
                    cp("act", mraw[:, c, 3:3 + T], ps[:, :], [pb], [b_mraw[c]])
                w_rel(1)
            wva, wba = w_next()
            wvb, wbb = w_next()
            for blk in range(4):
                ps, pb = ps_get("A")
                for half, (wv, wb) in enumerate(((wva, wba), (wvb, wbb))):
                    for k in range(NKC):
                        mm(ps[:, half * 256:(half + 1) * 256], uT[:, k, blk * 128:(blk + 1) * 128], wv[:, k, :], k == 0, k == NKC - 1, [wb, b_u[k]], [pb])
                cp("dve", vm[:, blk, :, 0:128], ps[:, :].rearrange("p (h d) -> p h d", h=4), [pb], [b_vm[blk]])
            w_rel(2)
            for pi in range(2):
                wv, wb = w_next()
                for jj in range(2):
                    c = 2 * pi + jj
                    ps, pb = ps_get("A")
                    proj_fm(wv, wb, jj, ps, pb)
                    cp("dve", fq[0:64, 2 * c, :], ps[0:64, :], [pb], [b_fq[2 * c]])
                    cp("dve", fq[64:128, 2 * c + 1, :], ps[64:128, :], [pb], [b_fq[2 * c + 1]])
                w_rel(1)
            for pi in range(2):
                wv, wb = w_next()
                for jj in range(2):
                    c = 2 * pi + jj
                    ps, pb = ps_get("A")
                    proj_fm(wv, wb, jj, ps, pb)
                    cp("act", kc[:, c, t0:t0 + T], ps[:, :], [pb], [b_kc[c][t]])
                w_rel(1)
            wva, wba = w_next()
            wvb, wbb = w_next()
            for blk in range(4):
                ps, pb = ps_get("A")
                for half, (wv, wb) in enumerate(((wva, wba), (wvb, wbb))):
                    for k in range(NKC):
                        mm(ps[:, half * 256:(half + 1) * 256], uT[:, k, blk * 128:(blk + 1) * 128], wv[:, k, :], k == 0, k == NKC - 1, [wb, b_u[k]], [pb])
                cp("dve", vc[:, t * 4 + blk, :, 0:64], ps[:, :].rearrange("p (h d) -> p h d", h=8), [pb], [b_vc[t * 4 + blk]])
            w_rel(2)

            pa, pab = ps_get("C")
            for c in range(4):
                sl = slice(c * 128, (c + 1) * 128)
                act(g_aw[:, 0, :], g_rho[:, sl], AF.Exp, [b_gate], [b_gate], bias=g_ngx[:, c:c + 1], scale=1.0)
                act(g_aw[:, 1, :], g_rho[:, sl], AF.Exp, [b_gate], [b_gate], bias=g_ngx[:, c + 1:c + 2], scale=1.0)
                act(g_E[:, sl], g_cb[:, 1 + c * 128:1 + (c + 1) * 128], AF.Exp, [b_gate], [b_gate], bias=g_ngx[:, c:c + 1], scale=1.0)
                act(g_dec[:, c:c + 1], g_gx[:, c * 128:c * 128 + 1], AF.Exp, [b_gate], [b_gate], bias=g_ngx[:, c + 1:c + 2], scale=1.0)
                tr(pa[:, c * 8:c * 8 + 4], g_aw[:, 0, :], ident[0:4, 0:4], [b_gate, b_k], [pab])
                tr(pa[:, c * 8 + 4:c * 8 + 8], g_aw[:, 1, :], ident[0:4, 0:4], [b_gate, b_k], [pab])
            cp("dve", awT[:, :, :], pa[:, 0:32].rearrange("p (c e) -> p c e", c=4), [pab], [b_awT])
            pdp, pdpb = ps_get("C")
            for j in range(2):
                mm(pdp[:, j * 4:(j + 1) * 4], selpair[:, j, :], g_dec[:, :], True, True, [b_k, b_gate], [pdpb])
            cp("dve", decp[:, :, :], pdp[:, 0:8].rearrange("p (j c) -> p j c", j=2), [pdpb], [b_decp])
            pf, pfb = ps_get("C")
            for c in range(4):
                tr(pf[:, c * 8:(c + 1) * 8], f_cb[:, 1 + c * 128:1 + (c + 1) * 128], ident[0:8, 0:8], [b_fg, b_k], [pfb])
            cp("dve", cbT[:, t * 4:(t + 1) * 4, :], pf[:, 0:32].rearrange("p (c e) -> p c e", c=4), [pfb], [b_cbT])
            nkb = 4 * t + 4
            for c in range(4):
                acc, accb = ring("f32r", f32r, b_f32r)
                ts("dve", acc, mraw[:, c, 0:T], cc(C_CONV + c * 4 + 0), None, ALU.mult, None, [b_mraw[c], b_consts], [accb])
                for j in range(1, 4):
                    stt("dve", acc, mraw[:, c, j:j + T], cc(C_CONV + c * 4 + j), acc, ALU.mult, ALU.add, [b_mraw[c], b_consts, accb], [accb])
                act(mqk[:, c, :], acc, AF.Silu, [accb], [b_mqk[c]])
            for c in range(4):
                cp("dve", mhist[:, c, :], mraw[:, c, T:T + 3], [b_mraw[c]], [b_mhist])

            chk("conv")

            BANK_S, BANK_N = 3, (6, 7)

            def mlstm_gen():
                for c in range(4):
                    sl = slice(c * 128, (c + 1) * 128)
                    hsl = slice(3 + c * 128, 3 + (c + 1) * 128)
                    pS, pSb = psum[BANK_S], b_ps[BANK_S]
                    hq = []
                    sloc = []
                    for h in range(4):
                        j = h // 2
                        b0 = (h % 2) * 64
                        qT = mqk[b0:b0 + 64, j, sl]
                        kT = mqk[b0:b0 + 64, 2 + j, sl]
                        hq.append((j, b0, qT))
                        if h % 2 == 0:
                            so, sob = psum[BANK_S][:, j * 128:(j + 1) * 128], b_ps[BANK_S]
                        else:
                            so, sob = psum[BANK_N[j]][:, 384:512], b_ps[BANK_N[j]]
                        sloc.append((so, sob))
                        mm(so, kT, qT, True, True, [b_mqk[j], b_mqk[2 + j]], [sob])
                    yield
                    for h in range(4):
                        so, sob = sloc[h]
                        stt("dve", ptm4[:, h, :], so, awT[:, c, h:h + 1], mask01[:, :], ALU.mult, ALU.mult, [sob, b_awT, b_k], [b_ptm4[h]])
                    yield
                    for hp in range(2):
                        for h in (2 * hp, 2 * hp + 1):
                            j, b0, qT = hq[h]
                            pN, pNb = psum[BANK_N[h % 2]], b_ps[BANK_N[h % 2]]
                            pt = ptm4[:, h, :]
                            mm(pN[:, 0:128], cbf[b0:b0 + 64, j, :], qT, True, False, [b_cbf[j], b_mqk[j]], [pNb])
                            mm(pN[:, 0:128], vm[:, c, h, 0:128], pt, False, True, [b_vm[c], b_ptm4[h]], [pNb])
                            mm(pN[:, 128:256], nbb[b0:b0 + 64, j, :], qT, True, False, [b_cbf[j], b_mqk[j]], [pNb])
                            mm(pN[:, 128:256], onesb[:, :], pt, False, True, [b_k, b_ptm4[h]], [pNb])
                            mm(pN[:, 256:384], selrows[:, h, :], g_E[:, sl], True, True, [b_k, b_gate], [pNb])
                        yield
                        for h in (2 * hp, 2 * hp + 1):
                            pN, pNb = psum[BANK_N[h % 2]], b_ps[BANK_N[h % 2]]
                            es, esbb = ring("esb", esb, b_esb)
                            cp("act", es, pN[:, 256:384], [pNb], [esbb])
                            t1, t1b = ring("t1r", t1r, b_t1r)
                            act(t1, pN[:, 128:256], AF.Abs, [pNb], [t1b])
                            tt("dve", t1, t1, es, ALU.max, [t1b, esbb], [t1b])
                            P.op("dve", (lambda o: lambda E: E.reciprocal(out=o, in_=o))(t1), [t1b], [t1b])
                            tt("dve", hm[:, h, hsl], pN[:, 0:128], t1, ALU.mult, [pNb, t1b], [b_hm[h]])
                        yield
                    for j in range(2):
                        mm(pS[:, j * 128:(j + 1) * 128], mqk[:, 2 + j, sl], identb[:, :], True, True, [b_mqk[2 + j], b_k], [pSb])
                    yield
                    for j in range(2):
                        for hh in range(2):
                            ts("dve", kw[:, j, hh * 64:(hh + 1) * 64], pS[:, j * 128 + hh * 64:j * 128 + (hh + 1) * 64], awT[:, c, 4 + 2 * j + hh:5 + 2 * j + hh], None, ALU.mult, None, [pSb, b_awT], [b_kw[j]])
                    yield
                    for j in range(2):
                        pC, pCb = psum[BANK_N[j]], b_ps[BANK_N[j]]
                        mm(pC[:, 0:129], kw[:, j, :], vm[:, c, 2 * j, :], True, True, [b_kw[j], b_vm[c]], [pCb])
                        mm(pC[:, 256:385], kw[:, j, :], vm[:, c, 2 * j + 1, :], True, True, [b_kw[j], b_vm[c]], [pCb])
                    yield
                    for j in range(2):
                        pC, pCb = psum[BANK_N[j]], b_ps[BANK_N[j]]
                        stt("dve", cn[0:64, j, :], cn[0:64, j, :], decp[0:64, j, c:c + 1], pC[0:64, 0:129], ALU.mult, ALU.add, [b_cn[j], b_decp, pCb], [b_cn[j]])
                        stt("dve", cn[64:128, j, :], cn[64:128, j, :], decp[64:128, j, c:c + 1], pC[64:128, 256:385], ALU.mult, ALU.add, [b_cn[j], b_decp, pCb], [b_cn[j]])
                        P.op("act", (lambda o, i: lambda E: E.activation(out=o, in_=i, func=AF.Copy, scale=0.125))(cbf[:, j, :], cn[:, j, 0:128]), [b_cn[j]], [b_cbf[j]])
                        ts("dve", nbb[:, j, :], onesb[:, :], cn[:, j, 128:129], 0.125, ALU.mult, ALU.mult, [b_cn[j], b_k], [b_cbf[j]])
                    yield

            mg = mlstm_gen()
            import os as _os
            if _os.environ.get('MG_FIRST'):
                for _i in range(int(_os.environ.get('MG_STEPS', '1000'))):
                    if next(mg, 'end') == 'end':
                        break
                if 'MG_STEPS' in _os.environ:
                    mg = iter(())
            NFH = int(_os.environ.get('NFH', '8'))

            pools["FS"] = [0, 1, 2]
            pools["FO"] = [4]
            pool_ctr.setdefault("FS", 0)
            pool_ctr.setdefault("FO", 0)
            def fox_fin_gen(h, pO, pOb):
                o_, ob = ring("osb", osb, b_osb)
                cp("dve", o_, pO[0:65, :], [pOb], [ob])
                yield
                pl, plb = psum[5], b_ps[5]
                mm(pl[0:64, :], sel64[:, :], o_[0:65], True, True, [b_k, ob], [plb])
                yield
                tl, tlb = ring("f32r", f32r, b_f32r)
                act(tl[0:64], pl[0:64, :], AF.Ln, [plb], [tlb])
                act(tl[0:64], tl[0:64], AF.Exp, [tlb], [tlb], scale=-1.0)
                yield
                y0, y0b = ring("f32r", f32r, b_f32r)
                tt("dve", y0[0:64], o_[0:64], tl[0:64], ALU.mult, [ob, tlb], [y0b])
                sq, sqb = ring("sqr", sqr, b_sqr)
                tt("dve", sq[0:64], y0[0:64], y0[0:64], ALU.mult, [y0b], [sqb])
                yield
                pn, pnb = psum[5], b_ps[5]
                mm(pn[0:64, :], onesb[0:64, 0:64], sq[0:64], True, True, [b_k, sqb], [pnb])
                yield
                rs, rb = rstd_from_psum(pn[0:64, :], pnb, 64, 1.0 / 64)
                yield
                stt("dve", ycf[:, h, :], y0[0:64], consts[0:64, C_FOUT + h:C_FOUT + h + 1], rs[0:64], ALU.mult, ALU.mult, [y0b, b_consts, rb], [b_ycf[h]])

            fin = iter(())
            for h in range(NFH):
                j = h // 2
                b0 = (h % 2) * 64
                pO, pOb = ps_get("FO")
                blocks = []
                for kb in range(nkb):
                    d = kb - 4 * t
                    q0 = 0 if d < 0 else d * 128
                    blocks.append((kb, q0, d >= 0))

                def issue_S(kb, q0, diag):
                    pS, pSb = ps_get("FS")
                    kt = kb // 4
                    mm(pS[:, q0:T], kc[:, j, kb * 128:(kb + 1) * 128], fq[:, h, q0:T], True, False, [b_kc[j][kt], b_fq[h]], [pSb])
                    mm(pS[:, q0:T], sel24[:, h, :], qall[:, q0:T], False, not diag, [b_k, b_qall], [pSb])
                    if diag:
                        mm(pS[:, q0:q0 + 128], identb[:, :], maskneg[:, :], False, True, [b_k], [pSb])
                    return pS, pSb

                pend = []
                LOOK = 2
                for i in range(min(LOOK, len(blocks))):
                    pend.append(issue_S(*blocks[i]))
                for i, (kb, q0, diag) in enumerate(blocks):
                    pS, pSb = pend.pop(0)
                    pt, ptb = ring("ptr", ptr, b_ptr)
                    act(pt[:, q0:T], pS[:, q0:T], AF.Exp, [pSb, b_cbT], [ptb], bias=cbT[:, kb, h:h + 1], scale=0.125)
                    if i + LOOK < len(blocks):
                        pend.append(issue_S(*blocks[i + LOOK]))
                    next(mg, None)
                    next(fin, None)
                    mm(pO[0:65, q0:T], vc[:, kb, h, :], pt[:, q0:T], i == 0, i == len(blocks) - 1, [b_vc[kb], ptb], [pOb])
                for _ in fin:
                    pass
                fin = fox_fin_gen(h, pO, pOb)
                next(fin, None)
            for _ in fin:
                pass
            for _ in mg:
                pass
            chk("mlstm")

            chk("fox")
            wmo = [w_next(), w_next()]
            for h in range(4):
                ps, pb = sumsq_bcast([(hm[:, h, 3:3 + T], b_hm[h])], 128, onesb[:, :])
                rs, rb = rstd_from_psum(ps[:, :], pb, 128, 1.0 / 128)
                y1, y1b = ring("f32r", f32r, b_f32r)
                stt("dve", y1, hm[:, h, 3:3 + T], ghalf[:, h:h + 1], rs[:, :], ALU.mult, ALU.mult, [b_hm[h], b_ghalf, rb], [y1b])
                wv, wb = wmo[h // 2]
                pso, psob = ps_get("A")
                proj_fm(wv, wb, h % 2, pso, psob)
                th, thb = ring("f32r", f32r, b_f32r)
                act(th, pso[:, :], AF.Tanh, [psob], [thb], scale=0.5)
                stt("dve", ycm[:, h, :], th, 1.0, y1, ALU.add, ALU.mult, [thb, y1b], [b_ycm[h]])
            w_rel(2)

            for dp in range(4):
                wm, wmb = w_next()
                wf, wfb = w_next()
                for jj in range(2):
                    do = 2 * dp + jj
                    ps, pb = ps_get("A")
                    for c in range(4):
                        mm(ps[:, :], wm[:, c, jj * 128:(jj + 1) * 128], ycm[:, c, :], c == 0, False, [wmb, b_ycm[c]], [pb])
                    for h in range(8):
                        mm(ps[:, :], wf[0:64, h, jj * 128:(jj + 1) * 128], ycf[:, h, :], False, h == 7, [wfb, b_ycf[h]], [pb])
                    tt("dve", hT[:, do, :], hT[:, do, :], ps[:, :], ALU.add, [b_h[do], pb], [b_h[do]])
                w_rel(2)

            chk("wout")
            norm_to_uT(C_FFN2)
            ffn()

            chk("ffn2")
            wpieces = [w_next() for _ in range(4)]
            wpp_v, wpp_b = w_next()
            pss, pssb = ps_get("C")
            for do in range(NKC):
                ps, pb = ps_get("A")
                for k in range(2):
                    mm(ps[:, :], wpp_v[:, k, do * 128:(do + 1) * 128], pT[:, k, :], k == 0, k == 1, [wpp_b, b_pT], [pb])
                sq, sqb = ring("sqr", sqr, b_sqr)
                act(sq, ps[:, :], AF.Square, [pb], [sqb])
                mm(pss[:, :], onesb[:, :], sq, do == 0, do == NKC - 1, [sqb, b_k], [pssb])
            tl_, tlb_ = ring("f32r", f32r, b_f32r)
            act(tl_, pss[:, :], AF.Ln, [pssb, b_k], [tlb_], bias=epsc[:, 0:1], scale=1.0 / D)
            act(rspb_t[:, :], tl_, AF.Exp, [tlb_], [b_rsp], scale=-0.5)
            rsp, rspb = rspb_t, b_rsp
            norm_to_uT(C_PG)
            for do in range(NKC):
                wv, wb = wpieces[do // 2]
                ps, pb = ps_get("A")
                proj_fm(wv, wb, do % 2, ps, pb)
                th, thb = ring("f32r", f32r, b_f32r)
                act(th, ps[:, :], AF.Tanh, [pb], [thb], scale=0.5)
                ps2, pb2 = ps_get("A")
                for k in range(2):
                    mm(ps2[:, :], wpp_v[:, k, do * 128:(do + 1) * 128], pT[:, k, :], k == 0, k == 1, [wpp_b, b_pT], [pb2])
                a1, a1b = ring("f32r", f32r, b_f32r)
                stt("dve", a1, ps2[:, :], ghalf[:, 4 + do:5 + do], rsp[:, :], ALU.mult, ALU.mult, [pb2, b_ghalf, rspb], [a1b])
                stt("dve", a1, th, 1.0, a1, ALU.add, ALU.mult, [thb, a1b], [a1b])
                tt("dve", hT[:, do, :], hT[:, do, :], a1, ALU.add, [b_h[do], a1b], [b_h[do]])
                if do % 2 == 1:
                    w_rel(1)
            w_rel(1)

            chk("ple")
            ps, pb = sumsq_bcast([(hT[:, c, :], b_h[c]) for c in range(NKC)], 128, onesb[:, :])
            rs, rb = rstd_from_psum(ps[:, :], pb, 128, 1.0 / D)
            for c in range(NKC):
                o1, o1b = ring("f32r", f32r, b_f32r)
                stt("dve", o1, hT[:, c, :], cc(C_FIN + c), rs[:, :], ALU.mult, ALU.mult, [b_h[c], rb, b_consts], [o1b])
                P.dma("sp" if (t == 0 or _os_dq) else "pool", out_d[c * 128:(c + 1) * 128, t0:t0 + T], o1, ("xout", (ring_ctr["f32r"] - 1) % len(b_f32r)), reads=[o1b], is_out=True)

        if debug:
            P.halt = False
            allb = b_h + b_u + b_mqk + b_mraw + b_ycf + b_fq + b_f32r + b_rsr
            P.dma("sp", dbg_d["hT"], hT[:, :, :].rearrange("p a b -> p (a b)"), "dbg0", reads=allb, is_out=True)
            P.dma("pool", dbg_d["uT"], uT[:, :, :].rearrange("p a b -> p (a b)"), "dbg1", reads=allb, is_out=True)
            P.dma("pool", dbg_d["mqk"], mqk[:, :, :].rearrange("p a b -> p (a b)"), "dbg2", reads=allb, is_out=True)
            P.dma("sp", dbg_d["hm"], mraw[:, :, :].rearrange("p a b -> p (a b)"), "dbg3", reads=allb, is_out=True)
            P.dma("pool", dbg_d["ycf"][0:64, :], ycf[:, :, :].rearrange("p a b -> p (a b)"), "dbg4", reads=allb, is_out=True)
            P.dma("pool", dbg_d["fq"], fq[:, :, :].rearrange("p a b -> p (a b)"), "dbg5", reads=allb, is_out=True)
            P.dma("sp", dbg_d["misc"][:, 0:3 * T], f32r[:, :, :].rearrange("p a b -> p (a b)"), "dbg6", reads=allb, is_out=True)
            P.dma("sp", dbg_d["misc"][:, 3 * T:4 * T], rsr[:, 0, :], "dbg7", reads=allb, is_out=True)
            P.dma("pool", dbg_d["ffa"], ffa[:, :, :].rearrange("p a b -> p (a b)"), "dbg8", reads=allb + b_ffa, is_out=True)
        P.emit()
    nc._marks = marks
    nc._nops = {e: len(P.ops[e]) for e in P.ENGS}
    return nc


def _prep_inputs(inputs):
    f = lambda a: np.ascontiguousarray(np.asarray(a, dtype=np.float32))
    consts = np.zeros((128, NCONST), np.float32)

    def put_cols(col, vec, chunk):
        v = f(vec).reshape(-1, chunk)
        consts[:chunk, col:col + v.shape[0]] = v.T

    put_cols(C_FFN1, inputs["ffn1_norm"][0], 128)
    put_cols(C_MIX, inputs["mix_norm"][0], 128)
    put_cols(C_FFN2, inputs["ffn2_norm"][0], 128)
    put_cols(C_PG, inputs["ple_gate_norm"][0], 128)
    put_cols(C_PP, inputs["ple_proj_norm"][0], 128)
    put_cols(C_FIN, inputs["final_norm"], 128)
    put_cols(C_MOUT, inputs["mlstm_out_norm"][0], 128)
    put_cols(C_FOUT, inputs["fox_out_norm"][0], 64)
    cw = f(inputs["conv_qk"][0])
    consts[:, C_CONV:C_CONV + 16] = cw.reshape(4, 4, 128).transpose(2, 1, 0).reshape(128, 16)
    bg = f(inputs["b_mlstm_gates"][0])
    consts[0:4, C_BI] = bg[0:4]
    consts[0:4, C_BF] = bg[4:8]
    consts[0:8, C_BFF] = f(inputs["b_fox_f"][0])
    w_in = f(inputs["w_in"][0])
    w_gates = np.ascontiguousarray(np.concatenate([w_in[:, 1536:1544], w_in[:, 3080:3088]], axis=1))
    shared = {
        "ffn1_w_gate": f(inputs["ffn1_w_gate"][0]), "ffn1_w_up": f(inputs["ffn1_w_up"][0]), "ffn1_w_down": f(inputs["ffn1_w_down"][0]),
        "ffn2_w_gate": f(inputs["ffn2_w_gate"][0]), "ffn2_w_up": f(inputs["ffn2_w_up"][0]), "ffn2_w_down": f(inputs["ffn2_w_down"][0]),
        "w_in": w_in, "w_gates": w_gates, "w_out": f(inputs["w_out"][0]),
        "w_ple_gate": f(inputs["w_ple_gate"][0]), "w_ple_proj": f(inputs["w_ple_proj"][0]), "consts": consts,
    }
    x = np.asarray(inputs["x"], dtype=np.float32)
    p = np.asarray(inputs["p"], dtype=np.float32)[0]
    in_maps = []
    for b in range(8):
        m = dict(shared)
        m["x"] = np.ascontiguousarray(x[b].T)
        m["p"] = np.ascontiguousarray(p[b].T)
        in_maps.append(m)
    return in_maps


def kernel(**inputs):
    nc = build_nc()
    in_maps = _prep_inputs(inputs)
    res = run_bass_kernel_spmd(nc, in_maps, core_ids=list(range(8)))
    return np.stack([np.ascontiguousarray(np.asarray(r["out"], dtype=np.float32).T) for r in res.results], axis=0)
```

```python
import contextlib
import math
import numpy as np
import concourse.bass as bass
import concourse.mybir as mybir
from concourse.bass_utils import run_bass_kernel_spmd

F32 = mybir.dt.float32
BF16 = mybir.dt.bfloat16
AF = mybir.ActivationFunctionType
ALU = mybir.AluOpType

D = 1024
S = 4096
DFF = 2816
NIN = 3088
T = 512
NT = S // T
NKC = D // 128
NHC = DFF // 128
EPS = 1e-6
NSLOT = 7
SLOT_ELEMS = 2048
HALF_A = 12


class Buf:
    __slots__ = ("name", "lw", "rd")

    def __init__(self, name=""):
        self.name = name
        self.lw = None
        self.rd = []


class Prog:
    ENGS = ("pe", "act", "dve", "pool", "sp")

    def __init__(self, nc):
        self.nc = nc
        self.ops = {e: [] for e in self.ENGS}
        self.dma_keys = {}
        self.out_dmas = []
        self.halt = False

    def _deps(self, me, reads, writes):
        deps = set()
        for b in reads:
            if b.lw is not None:
                deps.add(b.lw)
        for b in writes:
            if b.lw is not None:
                deps.add(b.lw)
            deps.update(b.rd)
        deps.discard(me)
        return deps

    def _commit(self, me, reads, writes):
        for b in reads:
            b.rd.append(me)
        for b in writes:
            b.lw = me
            b.rd = []

    def op(self, eng, fn, reads=(), writes=()):
        if self.halt:
            return None
        idx = len(self.ops[eng])
        me = (eng, idx)
        deps = self._deps(me, reads, writes)
        self.ops[eng].append({"fn": fn, "deps": deps, "sig": False, "dma": None})
        self._commit(me, reads, writes)
        return me

    def dma(self, eng, out, in_, key, reads=(), writes=(), is_out=False):
        if self.halt:
            return None
        n = self.dma_keys.get(key, 0) + 1
        self.dma_keys[key] = n
        me = ("dma", key, n)
        deps = self._deps(me, reads, writes)
        self.ops[eng].append({"fn": None, "deps": deps, "sig": False, "dma": (out, in_, key, n)})
        self._commit(me, reads, writes)
        if is_out:
            self.out_dmas.append(me)
        return me

    def emit(self):
        nc = self.nc
        for e in self.ENGS:
            for o in self.ops[e]:
                for d in o["deps"]:
                    if d[0] != "dma":
                        self.ops[d[0]][d[1]]["sig"] = True
        signo = {}
        for e in self.ENGS:
            c = 0
            for i, o in enumerate(self.ops[e]):
                if o["sig"]:
                    c += 1
                    signo[(e, i)] = c
        with contextlib.ExitStack() as st:
            esem = {e: st.enter_context(nc.semaphore("s_" + e)) for e in self.ENGS}
            dsem = {}
            for k in self.dma_keys:
                dsem[k] = st.enter_context(nc.semaphore("d_" + str(k).replace(" ", "").replace("'", "").replace("(", "").replace(")", "").replace(",", "_")))
            block = st.enter_context(nc.Block())

            def run(ename, E):
                known = {}
                for i, o in enumerate(self.ops[ename]):
                    need = {}
                    for d in o["deps"]:
                        if d[0] == "dma":
                            k = ("dma", d[1])
                            v = 16 * d[2]
                        else:
                            if d[0] == ename and ename == "pe":
                                continue
                            k = d[0]
                            v = signo[d]
                        if v > need.get(k, 0):
                            need[k] = v
                    for k, v in need.items():
                        if known.get(k, 0) >= v:
                            continue
                        known[k] = v
                        sem = dsem[k[1]] if isinstance(k, tuple) else esem[k]
                        E.wait_ge(sem, v)
                    if o["dma"] is not None:
                        out, in_, key, n = o["dma"]
                        E.dma_start(out=out, in_=in_).then_inc(dsem[key], 16)
                    else:
                        ins = o["fn"](E)
                        if o["sig"]:
                            ins.then_inc(esem[ename], 1)
                if ename == "sp":
                    last = {}
                    for d in self.out_dmas:
                        last[d[1]] = max(last.get(d[1], 0), d[2])
                    for k, n in last.items():
                        E.wait_ge(dsem[k], 16 * n)

            block.tensor(lambda E: run("pe", E))
            block.scalar(lambda E: run("act", E))
            block.vector(lambda E: run("dve", E))
            block.gpsimd(lambda E: run("pool", E))
            block.sync(lambda E: run("sp", E))


C_FFN1, C_MIX, C_FFN2, C_PG, C_PP, C_FIN = 0, 8, 16, 24, 32, 40
C_MOUT = 48
C_FOUT = 52
C_CONV = 60
C_BI = 76
C_BF = 77
C_BFF = 78
NCONST = 80


class _Stop(Exception):
    pass


def build_nc(ntiles=NT, debug=False, stop=None):
    import os as _osm
    _os_dq = bool(_osm.environ.get('DQ_SP'))
    nc = bass.Bass("TRN2", target_bir_lowering=False)
    dr = lambda name, shape, kind="ExternalInput": nc.dram_tensor(name, shape, F32, kind=kind).ap()
    x_d = dr("x", [D, S])
    p_d = dr("p", [256, S])
    w1g, w1u, w1d = dr("ffn1_w_gate", [D, DFF]), dr("ffn1_w_up", [D, DFF]), dr("ffn1_w_down", [DFF, D])
    w2g, w2u, w2d = dr("ffn2_w_gate", [D, DFF]), dr("ffn2_w_up", [D, DFF]), dr("ffn2_w_down", [DFF, D])
    win_d = dr("w_in", [D, NIN])
    wgates_d = dr("w_gates", [D, 16])
    wout_d = dr("w_out", [D, D])
    wpg_d = dr("w_ple_gate", [D, D])
    wpp_d = dr("w_ple_proj", [256, D])
    consts_d = dr("consts", [128, NCONST])
    out_d = dr("out", [D, S], kind="ExternalOutput")
    dbg_d = {}
    if debug:
        for nm, w in (("hT", NKC * T), ("uT", NKC * T), ("mqk", 4 * T), ("hm", 4 * (3 + T)), ("ycf", 8 * T), ("fq", 8 * T), ("misc", 4 * T), ("ffa", HALF_A * T)):
            dbg_d[nm] = dr("dbg_" + nm, [128, w], kind="ExternalOutput")

    marks = []

    def chk(name):
        marks.append((name, len(P.ops["pe"]), len(P.ops["act"]), len(P.ops["dve"])))
        if stop == name:
            P.halt = True

    P = Prog(nc)
    st = contextlib.ExitStack()
    with st:
        def sb(name, shape, dt=F32):
            return st.enter_context(nc.sbuf_tensor("sb_" + name, shape, dt))

        hT = sb("hT", [128, NKC, T]); b_h = [Buf("h%d" % c) for c in range(NKC)]
        uT = sb("uT", [128, NKC, T], BF16); b_u = [Buf("u%d" % c) for c in range(NKC)]
        ffa = sb("ffa", [128, HALF_A, T], BF16); b_ffa = [Buf("ffa%d" % c) for c in range(HALF_A)]
        kc = sb("kc", [128, 4, S], BF16); b_kc = [[Buf() for _ in range(NT)] for _ in range(4)]
        vc = sb("vc", [128, S // 128, 8, 65], BF16); b_vc = [Buf() for _ in range(S // 128)]
        wring = sb("wring", [128, NSLOT, SLOT_ELEMS], BF16); b_slot = [Buf("slot%d" % i) for i in range(NSLOT)]
        pT = sb("pT", [128, 2, T], BF16); b_pT = Buf("pT")
        consts = sb("consts", [128, NCONST]); b_consts = Buf("consts")
        cneg = sb("cneg", [128, 4]); b_cneg = Buf("cneg")
        ghalf = sb("ghalf", [128, 12]); b_ghalf = Buf("ghalf")
        wgates = sb("wgates", [128, NKC, 16], BF16); b_wgates = Buf("wgates")
        ident = sb("ident", [128, 128]); b_ident = Buf("ident")
        identb = sb("identb", [128, 128], BF16)
        onesb = sb("onesb", [128, 128], BF16)
        mask01 = sb("mask01", [128, 128], BF16)
        maskneg = sb("maskneg", [128, 128], BF16)
        selrows = sb("selrows", [4, 4, 128])
        selpair = sb("selpair", [4, 2, 128])
        epsc = sb("epsc", [128, 1])
        zerosf = None; onesf = None
        sel64 = sb("sel64", [65, 64])
        b_k = Buf("konst")
        sqr = sb("sqr", [128, 2, T], BF16); b_sqr = [Buf() for _ in range(2)]
        f32r = sb("f32r", [128, 3, T]); b_f32r = [Buf() for _ in range(3)]
        zerosf = f32r[:, 0, 0:128]; onesf = f32r[:, 1, 0:128]
        rsr = sb("rsr", [128, 1, T]); b_rsr = [Buf() for _ in range(1)]
        ptr = sb("ptr", [128, 3, T], BF16); b_ptr = [Buf() for _ in range(3)]
        mraw = sb("mraw", [128, 4, 3 + T]); b_mraw = [Buf() for _ in range(4)]
        mhist = sb("mhist", [128, 4, 3]); b_mhist = Buf("mhist")
        mqk = sb("mqk", [128, 4, T], BF16); b_mqk = [Buf() for _ in range(4)]
        vm = sb("vm", [128, 4, 4, 129], BF16); b_vm = [Buf() for _ in range(4)]
        fq = sb("fqz", [128, 8, T], BF16); b_fq = [Buf() for _ in range(8)]
        ycm = mqk; b_ycm = b_mqk
        ycf = sb("ycf", [64, 8, T], BF16); b_ycf = [Buf() for _ in range(8)]
        hm = mraw; b_hm = b_mraw
        g_l1 = sb("g_l1", [4, T]); g_cb = sb("g_cb", [4, 1 + T]); g_rho = sb("g_rho", [4, T])
        g_gx = sb("g_gx", [4, 1 + T]); g_ngx = sb("g_ngx", [4, 8])
        g_aw = sb("g_aw", [4, 2, 128]); g_E = sb("g_E", [4, T]); g_dec = sb("g_dec", [4, 4])
        b_gate = Buf("gate_m")
        awT = sb("awT", [128, 4, 8]); b_awT = Buf("awT")
        decp = sb("decp", [128, 2, 4]); b_decp = Buf("decp")
        cn = sb("cn", [128, 2, 129]); b_cn = [Buf(), Buf()]
        cbf = sb("cbf", [128, 2, 128], BF16); nbb = sb("nbb", [128, 2, 128], BF16); b_cbf = [Buf(), Buf()]
        kw = sb("kw", [128, 2, 128], BF16); b_kw = [Buf(), Buf()]
        ptm4 = sb("ptm4", [128, 4, 128], BF16); b_ptm4 = [Buf() for _ in range(4)]
        esb = sb("esb", [128, 1, 128]); b_esb = [Buf()]
        t1r = sb("t1r", [128, 1, 128]); b_t1r = [Buf()]
        f_l1 = sb("f_l1", [8, T]); f_cb = sb("f_cb", [8, 1 + T]); b_fg = Buf("gate_f")
        cbT = sb("cbT", [128, S // 128, 8]); b_cbT = Buf("cbT")
        sel24 = sb("sel24", [128, 8, 128], BF16)
        qall = sb("qall", [128, T], BF16); b_qall = Buf("qall")
        osb = sb("osb", [65, 1, T]); b_osb = [Buf()]

        psum = [st.enter_context(nc.psum_tensor("ps%d" % i, [128, T], F32)) for i in range(8)]
        b_ps = [Buf("ps%d" % i) for i in range(8)]
        pools = {"A": [0, 1, 2, 3], "B": [4, 5], "C": [6, 7]}
        pool_ctr = {"A": 0, "B": 0, "C": 0}

        def ps_get(pool):
            lst = pools[pool]
            i = lst[pool_ctr[pool] % len(lst)]
            pool_ctr[pool] += 1
            return psum[i], b_ps[i]

        ring_ctr = {}

        def ring(name, tensor, bufs):
            i = ring_ctr.get(name, 0)
            ring_ctr[name] = i + 1
            j = i % len(bufs)
            return tensor[:, j], bufs[j]

        def cc(col, n=1, rows=128):
            return consts[0:rows, col:col + n]

        pieces = []

        def wpiece(src, parts, a, b):
            pieces.append((src, parts, a, b))

        FFN_HALVES = ((0, HALF_A), (HALF_A, NHC - HALF_A))

        def add_ffn(wg, wu, wd):
            for (c0, nch) in FFN_HALVES:
                for hp in range(nch // 2):
                    col = (c0 + 2 * hp) * 128
                    wpiece(wg[:, col:col + 256].rearrange("(kc p) n -> p kc n", p=128), 128, NKC, 256)
                    wpiece(wu[:, col:col + 256].rearrange("(kc p) n -> p kc n", p=128), 128, NKC, 256)
                for do in range(NKC):
                    wpiece(wd[c0 * 128:(c0 + nch) * 128, do * 128:(do + 1) * 128].rearrange("(j p) n -> p j n", p=128), 128, nch, 128)

        WIN_GROUPS = [0, 256, 512, 768, 1544, 1800, 2056, 2312, 2568, 2824, 1024, 1280]
        for t in range(ntiles):
            add_ffn(w1g, w1u, w1d)
            for c0 in WIN_GROUPS:
                wpiece(win_d[:, c0:c0 + 256].rearrange("(kc p) n -> p kc n", p=128), 128, NKC, 256)
            for dp in range(4):
                wpiece(wout_d[0:512, dp * 256:(dp + 1) * 256].rearrange("(c p) n -> p c n", p=128), 128, 4, 256)
                wpiece(wout_d[512:1024, dp * 256:(dp + 1) * 256].rearrange("(h p) n -> p h n", p=64), 64, 8, 256)
            add_ffn(w2g, w2u, w2d)
            for dp in range(4):
                wpiece(wpg_d[:, dp * 256:(dp + 1) * 256].rearrange("(kc p) n -> p kc n", p=128), 128, NKC, 256)
            wpiece(wpp_d.rearrange("(k p) n -> p k n", p=128), 128, 2, 1024)
        wstate = {"issued": 0, "next": 0}

        npt = len(pieces) // ntiles
        wscr = nc.dram_tensor("wscr_bf16", [npt, 128, SLOT_ELEMS], BF16).ap()
        b_wscr = [Buf() for _ in range(npt)]

        def w_issue_upto(n):
            while wstate["issued"] < min(n, len(pieces)):
                i = wstate["issued"]
                src, parts, a, b = pieces[i]
                s = i % NSLOT
                pidx = i % npt
                flat = wring[0:parts, s, 0:a * b]
                if i < npt:
                    dst = flat.rearrange("p (a b) -> p a b", a=a)
                    P.dma("pool", dst, src, ("w", s), writes=[b_slot[s]])
                    if ntiles > 1:
                        P.dma("sp", wscr[pidx, 0:parts, 0:a * b], flat, ("ws", s), reads=[b_slot[s]], writes=[b_wscr[pidx]])
                else:
                    P.dma("sp", flat, wscr[pidx, 0:parts, 0:a * b], ("w", s), reads=[b_wscr[pidx]], writes=[b_slot[s]])
                wstate["issued"] += 1

        def w_rel(n=1):
            w_issue_upto(wstate["issued"] + n)

        def w_next():
            i = wstate["next"]
            wstate["next"] += 1
            assert i < wstate["issued"] or P.halt or i >= len(pieces), (i, wstate)
            src, parts, a, b = pieces[i]
            s = i % NSLOT
            view = wring[0:parts, s, 0:a * b].rearrange("p (a b) -> p a b", a=a)
            return view, b_slot[s]

        def mm(out, lhsT, rhs, start, stop, reads, writes):
            P.op("pe", lambda E: E.matmul(out, lhsT=lhsT, rhs=rhs, start=start, stop=stop), reads, writes)

        def act(out, in_, func, reads, writes, bias=None, scale=None):
            kw_ = {}
            if bias is not None:
                kw_["bias"] = bias
            if scale is not None:
                kw_["scale"] = scale
            P.op("act", lambda E: E.activation(out=out, in_=in_, func=func, **kw_), reads, writes)

        def stt(eng, out, in0, scalar, in1, op0, op1, reads, writes):
            P.op(eng, lambda E: E.scalar_tensor_tensor(out=out, in0=in0, scalar=scalar, in1=in1, op0=op0, op1=op1), reads, writes)

        def ts(eng, out, in0, s1, s2, op0, op1, reads, writes):
            if op1 is None:
                P.op(eng, lambda E: E.tensor_scalar(out=out, in0=in0, scalar1=s1, scalar2=None, op0=op0), reads, writes)
            else:
                P.op(eng, lambda E: E.tensor_scalar(out=out, in0=in0, scalar1=s1, scalar2=s2, op0=op0, op1=op1), reads, writes)

        def tt(eng, out, in0, in1, op, reads, writes):
            P.op(eng, lambda E: E.tensor_tensor(out=out, in0=in0, in1=in1, op=op), reads, writes)

        def cp(eng, out, in_, reads, writes):
            if eng == "act":
                P.op("act", lambda E: E.copy(out=out, in_=in_), reads, writes)
            else:
                P.op(eng, lambda E: E.tensor_copy(out=out, in_=in_), reads, writes)

        def rstd_from_psum(ps_ap, ps_buf, nparts, inv_n, eps_in_sum=False):
            tmp, tb = ring("f32r", f32r, b_f32r)
            rs, rb = ring("rsr", rsr, b_rsr)
            if eps_in_sum:
                act(tmp[0:nparts], ps_ap, AF.Ln, [ps_buf], [tb])
            else:
                act(tmp[0:nparts], ps_ap, AF.Ln, [ps_buf, b_k], [tb], bias=epsc[0:nparts, 0:1], scale=inv_n)
            act(rs[0:nparts], tmp[0:nparts], AF.Exp, [tb], [rb], scale=-0.5)
            return rs, rb

        def sumsq_bcast(chunks, nparts, lhsT_ones):
            ps, pb = ps_get("C")
            n = len(chunks)
            for i, (ap, b) in enumerate(chunks):
                sq, sqb = ring("sqr", sqr, b_sqr)
                act(sq[0:nparts], ap, AF.Square, [b], [sqb])
                mm(ps[0:lhsT_ones.shape[1], :], lhsT_ones, sq[0:nparts], i == 0, i == n - 1, [sqb, b_k], [pb])
            return ps, pb

        def norm_to_uT(gcol):
            ps, pb = sumsq_bcast([(hT[:, c, :], b_h[c]) for c in range(NKC)], 128, onesb[:, :])
            rs, rb = rstd_from_psum(ps[:, :], pb, 128, 1.0 / D)
            for c in range(NKC):
                stt("dve", uT[:, c, :], hT[:, c, :], cc(gcol + c), rs[:, :], ALU.mult, ALU.mult, [b_h[c], rb, b_consts], [b_u[c]])

        def ffn():
            for (c0, nch) in FFN_HALVES:
                for hp in range(nch // 2):
                    wg_v, wg_b = w_next()
                    wu_v, wu_b = w_next()
                    for j in range(2):
                        hc = 2 * hp + j
                        pg, pgb = ps_get("A")
                        pu, pub = ps_get("A")
                        for k in range(NKC):
                            mm(pg[:, :], wg_v[:, k, j * 128:(j + 1) * 128], uT[:, k, :], k == 0, k == NKC - 1, [wg_b, b_u[k]], [pgb])
                        for k in range(NKC):
                            mm(pu[:, :], wu_v[:, k, j * 128:(j + 1) * 128], uT[:, k, :], k == 0, k == NKC - 1, [wu_b, b_u[k]], [pub])
                        sg, sgb = ring("f32r", f32r, b_f32r)
                        act(sg, pg[:, :], AF.Silu, [pgb], [sgb])
                        tt("dve", ffa[:, hc, :], sg, pu[:, :], ALU.mult, [sgb, pub], [b_ffa[hc]])
                    w_rel(2)
                for do in range(NKC):
                    wd_v, wd_b = w_next()
                    pd, pdb = ps_get("B")
                    for j in range(nch):
                        mm(pd[:, :], wd_v[:, j, :], ffa[:, j, :], j == 0, j == nch - 1, [wd_b, b_ffa[j]], [pdb])
                    w_rel(1)
                    stt("dve", hT[:, do, :], pd[:, :], 0.5, hT[:, do, :], ALU.mult, ALU.add, [pdb, b_h[do]], [b_h[do]])

        P.dma("sp", consts[:, :], consts_d[:, :], "c0", writes=[b_consts])
        P.dma("pool", wgates[:, :, :], wgates_d.rearrange("(kc p) n -> p kc n", p=128), "c1", writes=[b_wgates])
        kops = []
        kops.append(lambda E: E.memset(zerosf, 0.0))
        kops.append(lambda E: E.memset(onesf, 1.0))
        kops.append(lambda E: E.memset(epsc[:, :], EPS))
        kops.append(lambda E: E.memset(onesb[:, :], 1.0))
        kops.append(lambda E: E.affine_select(out=ident[:, :], in_=onesf, pattern=[[1, 128]], compare_op=ALU.is_equal, fill=0.0, base=0, channel_multiplier=-1))
        kops.append(lambda E: E.tensor_copy(out=identb[:, :], in_=ident[:, :]))
        kops.append(lambda E: E.memset(zerosf, 0.125))
        kops.append(lambda E: E.affine_select(out=mask01[:, :], in_=zerosf, pattern=[[1, 128]], compare_op=ALU.is_ge, fill=0.0, base=0, channel_multiplier=-1))
        kops.append(lambda E: E.memset(zerosf, 0.0))
        kops.append(lambda E: E.affine_select(out=maskneg[:, :], in_=zerosf, pattern=[[1, 128]], compare_op=ALU.is_ge, fill=-30000.0, base=0, channel_multiplier=-1))
        kops.append(lambda E: E.memset(selrows[:, :, :], 1.0))
        kops.append(lambda E: E.memset(selpair[:, :, :], 1.0))
        kops.append(lambda E: E.affine_select(out=selrows[:, :, :], in_=selrows[:, :, :], pattern=[[1, 4], [0, 128]], compare_op=ALU.is_equal, fill=0.0, base=0, channel_multiplier=-1))
        kops.append(lambda E: E.affine_select(out=selpair[:, :, :].rearrange("p j (a b) -> p j a b", a=2), in_=selpair[:, :, :].rearrange("p j (a b) -> p j a b", a=2), pattern=[[2, 2], [1, 2], [0, 64]], compare_op=ALU.is_equal, fill=0.0, base=0, channel_multiplier=-1))
        kops.append(lambda E: E.memset(sel64[0:64, :], 0.0))
        kops.append(lambda E: E.memset(sel64[64:65, :], 1.0))
        kops.append(lambda E: E.memset(vc[:, :, :, 64:65], 1.0))
        kops.append(lambda E: E.memset(vm[:, :, :, 128:129], 1.0))
        kops.append(lambda E: E.memset(cn[:, :, :], 0.0))
        kops.append(lambda E: E.memset(cbf[:, :, :], 0.0))
        kops.append(lambda E: E.memset(nbb[:, :, :], 0.0))
        kops.append(lambda E: E.memset(mhist[:, :, :], 0.0))
        kops.append(lambda E: E.memset(g_cb[:, 0:1], 0.0))
        kops.append(lambda E: E.memset(g_gx[:, 0:1], 0.0))
        kops.append(lambda E: E.memset(f_cb[:, 0:1], 0.0))
        kops.append(lambda E: E.memset(sel24[:, :, :], 0.0))
        kops.append(lambda E: E.memset(qall[:, :], 0.0))
        kops.append(lambda E: E.memset(fq[:, :, :], 0.0))
        kops.append(lambda E: E.memset(sel24[0:8, :, :], 1.0))
        kops.append(lambda E: E.affine_select(out=sel24[0:8, :, :], in_=sel24[0:8, :, :], pattern=[[1, 8], [0, 128]], compare_op=ALU.is_equal, fill=0.0, base=0, channel_multiplier=-1))
        for f in kops:
            P.op("pool", f, [], [b_k])
        for i_, r0_ in enumerate((8, 16)):
            P.dma("sp", sel24[r0_:r0_ + 8, :, :], sel24[0:8, :, :], "c%d" % (2 + i_), reads=[b_k], writes=[b_k])
        for b in b_vc + b_vm + b_cn + b_cbf + b_fq + b_f32r + [b_mhist, b_gate, b_fg, b_qall]:
            b.lw = b_k.lw
        ts("dve", cneg[0:4, 0:1], consts[0:4, C_BF:C_BF + 1], -1.0, None, ALU.mult, None, [b_consts], [b_cneg])
        ts("dve", cneg[0:8, 1:2], consts[0:8, C_BFF:C_BFF + 1], -1.0, None, ALU.mult, None, [b_consts], [b_cneg])
        ts("dve", ghalf[:, 0:4], consts[:, C_MOUT:C_MOUT + 4], 0.5, None, ALU.mult, None, [b_consts], [b_ghalf])
        ts("dve", ghalf[:, 4:12], consts[:, C_PP:C_PP + 8], 0.5, None, ALU.mult, None, [b_consts], [b_ghalf])

        w_issue_upto(NSLOT)

        def tr(out, in_, idn, reads, writes):
            P.op("pe", lambda E: E.transpose(out=out, in_=in_, identity=idn), reads, writes)

        def proj_fm(wv, wb, jj, ps, pb):
            for k in range(NKC):
                mm(ps[:, :], wv[:, k, jj * 128:(jj + 1) * 128], uT[:, k, :], k == 0, k == NKC - 1, [wb, b_u[k]], [pb])

        for t in (range(ntiles) if True else []):
            t0 = t * T
            for c in range(NKC):
                P.dma("sp" if (t == 0 or _os_dq) else "pool", hT[:, c, :], x_d[c * 128:(c + 1) * 128, t0:t0 + T], ("xin", c), writes=[b_h[c]])
            for k in range(2):
                P.dma("pool", pT[:, k, :], p_d[k * 128:(k + 1) * 128, t0:t0 + T], ("pin", k), writes=[b_pT])

            chk("load")
            norm_to_uT(C_FFN1)
            chk("norm1")
            ffn()
            chk("ffn1")

            norm_to_uT(C_MIX)
            pg_i, pg_ib = ps_get("C")
            for k in range(NKC):
                mm(pg_i[0:4, :], wgates[:, k, 0:4], uT[:, k, :], k == 0, k == NKC - 1, [b_wgates, b_u[k]], [pg_ib])
            act(g_rho[:, :], pg_i[0:4, :], AF.Identity, [pg_ib, b_consts], [b_gate], bias=consts[0:4, C_BI:C_BI + 1], scale=1.0)
            pg_f, pg_fb = ps_get("C")
            for k in range(NKC):
                mm(pg_f[0:4, :], wgates[:, k, 4:8], uT[:, k, :], k == 0, k == NKC - 1, [b_wgates, b_u[k]], [pg_fb])
            act(g_l1[:, :], pg_f[0:4, :], AF.Exp, [pg_fb, b_cneg], [b_gate], bias=cneg[0:4, 0:1], scale=-1.0)
            pg_ff, pg_ffb = ps_get("C")
            for k in range(NKC):
                mm(pg_ff[0:8, :], wgates[:, k, 8:16], uT[:, k, :], k == 0, k == NKC - 1, [b_wgates, b_u[k]], [pg_ffb])
            act(f_l1[:, :], pg_ff[0:8, :], AF.Exp, [pg_ffb, b_cneg], [b_fg], bias=cneg[0:8, 1:2], scale=-1.0)
            act(g_l1[:, :], g_l1[:, :], AF.Ln, [b_gate], [b_gate], bias=1.0, scale=1.0)
            act(f_l1[:, :], f_l1[:, :], AF.Ln, [b_fg], [b_fg], bias=1.0, scale=1.0)
            if t > 0:
                cp("dve", g_cb[:, 0:1], g_cb[:, T:T + 1], [b_gate], [b_gate])
                cp("dve", g_gx[:, 0:1], g_gx[:, T:T + 1], [b_gate], [b_gate])
                cp("dve", f_cb[:, 0:1], f_cb[:, T:T + 1], [b_fg], [b_fg])
            P.op("dve", lambda E: E.tensor_tensor_scan(out=g_cb[:, 1:1 + T], data0=g_l1[:, :], data1=g_l1[:, :], initial=g_cb[:, 0:1], op0=ALU.add, op1=ALU.max), [b_gate], [b_gate])
            tt("dve", g_rho[:, :], g_rho[:, :], g_cb[:, 1:1 + T], ALU.add, [b_gate], [b_gate])
            P.op("dve", lambda E: E.tensor_tensor_scan(out=g_gx[:, 1:1 + T], data0=g_rho[:, :], data1=g_rho[:, :], initial=g_gx[:, 0:1], op0=ALU.max, op1=ALU.max), [b_gate], [b_gate])
            for c5 in range(5):
                ts("dve", g_ngx[:, c5:c5 + 1], g_gx[:, c5 * 128:c5 * 128 + 1], -1.0, None, ALU.mult, None, [b_gate], [b_gate])
            P.op("dve", lambda E: E.tensor_tensor_scan(out=f_cb[:, 1:1 + T], data0=f_l1[:, :], data1=f_l1[:, :], initial=f_cb[:, 0:1], op0=ALU.add, op1=ALU.max), [b_fg], [b_fg])
            vq, vqb = ring("f32r", f32r, b_f32r)
            ts("dve", vq[0:8], f_cb[:, 1:1 + T], -8.0, None, ALU.mult, None, [b_fg], [vqb])
            for r3 in range(3):
                pc_, pcb = ring("sqr", sqr, b_sqr)
                cp("dve", pc_[0:8], vq[0:8], [vqb], [pcb])
                P.dma("sp" if (t == 0 or _os_dq) else "pool", qall[r3 * 8:(r3 + 1) * 8, :], pc_[0:8], ("qa", r3), reads=[pcb], writes=[b_qall])
                if r3 < 2:
                    tt("dve", vq[0:8], vq[0:8], pc_[0:8], ALU.subtract, [vqb, pcb], [vqb])

            for c in range(4):
                cp("dve", mraw[:, c, 0:3], mhist[:, c, :], [b_mhist], [b_mraw[c]])
            for pi in range(2):
                wv, wb = w_next()
                for jj in range(2):
                    c = 2 * pi + jj
                    ps, pb = ps_get("A")
                    proj_fm(wv, wb, jj, ps, pb)
                    cp("act", mraw[:, c, 3:3 + T], ps[:, :], [pb], [b_mraw[c]])
                w_rel(1)
            wva, wba = w_next()
            wvb, wbb = w_next()
            for blk in range(4):
                ps, pb = ps_get("A")
                for half, (wv, wb) in enumerate(((wva, wba), (wvb, wbb))):
                    for k in range(NKC):
                        mm(ps[:, half * 256:(half + 1) * 256], uT[:, k, blk * 128:(blk + 1) * 128], wv[:, k, :], k == 0, k == NKC - 1, [wb, b_u[k]], [pb])
                cp("dve", vm[:, blk, :, 0:128], ps[:, :].rearrange("p (h d) -> p h d", h=4), [pb], [b_vm[blk]])
            w_rel(2)
            for pi in range(2):
                wv, wb = w_next()
                for jj in range(2):
                    c = 2 * pi + jj
                    ps, pb = ps_get("A")
                    proj_fm(wv, wb, jj, ps, pb)
                    cp("dve", fq[0:64, 2 * c, :], ps[0:64, :], [pb], [b_fq[2 * c]])
                    cp("dve", fq[64:128, 2 * c + 1, :], ps[64:128, :], [pb], [b_fq[2 * c + 1]])
                w_rel(1)
            for pi in range(2):
                wv, wb = w_next()
                for jj in range(2):
                    c = 2 * pi + jj
                    ps, pb = ps_get("A")
                    proj_fm(wv, wb, jj, ps, pb)
                    cp("act", kc[:, c, t0:t0 + T], ps[:, :], [pb], [b_kc[c][t]])
                w_rel(1)
            wva, wba = w_next()
            wvb, wbb = w_next()
            for blk in range(4):
                ps, pb = ps_get("A")
                for half, (wv, wb) in enumerate(((wva, wba), (wvb, wbb))):
                    for k in range(NKC):
                        mm(ps[:, half * 256:(half + 1) * 256], uT[:, k, blk * 128:(blk + 1) * 128], wv[:, k, :], k == 0, k == NKC - 1, [wb, b_u[k]], [pb])
                cp("dve", vc[:, t * 4 + blk, :, 0:64], ps[:, :].rearrange("p (h d) -> p h d", h=8), [pb], [b_vc[t * 4 + blk]])
            w_rel(2)

            pa, pab = ps_get("C")
            for c in range(4):
                sl = slice(c * 128, (c + 1) * 128)
                act(g_aw[:, 0, :], g_rho[:, sl], AF.Exp, [b_gate], [b_gate], bias=g_ngx[:, c:c + 1], scale=1.0)
                act(g_aw[:, 1, :], g_rho[:, sl], AF.Exp, [b_gate], [b_gate], bias=g_ngx[:, c + 1:c + 2], scale=1.0)
                act(g_E[:, sl], g_cb[:, 1 + c * 128:1 + (c + 1) * 128], AF.Exp, [b_gate], [b_gate], bias=g_ngx[:, c:c + 1], scale=1.0)
                act(g_dec[:, c:c + 1], g_gx[:, c * 128:c * 128 + 1], AF.Exp, [b_gate], [b_gate], bias=g_ngx[:, c + 1:c + 2], scale=1.0)
                tr(pa[:, c * 8:c * 8 + 4], g_aw[:, 0, :], ident[0:4, 0:4], [b_gate, b_k], [pab])
                tr(pa[:, c * 8 + 4:c * 8 + 8], g_aw[:, 1, :], ident[0:4, 0:4], [b_gate, b_k], [pab])
            cp("dve", awT[:, :, :], pa[:, 0:32].rearrange("p (c e) -> p c e", c=4), [pab], [b_awT])
            pdp, pdpb = ps_get("C")
            for j in range(2):
                mm(pdp[:, j * 4:(j + 1) * 4], selpair[:, j, :], g_dec[:, :], True, True, [b_k, b_gate], [pdpb])
            cp("dve", decp[:, :, :], pdp[:, 0:8].rearrange("p (j c) -> p j c", j=2), [pdpb], [b_decp])
            pf, pfb = ps_get("C")
            for c in range(4):
                tr(pf[:, c * 8:(c + 1) * 8], f_cb[:, 1 + c * 128:1 + (c + 1) * 128], ident[0:8, 0:8], [b_fg, b_k], [pfb])
            cp("dve", cbT[:, t * 4:(t + 1) * 4, :], pf[:, 0:32].rearrange("p (c e) -> p c e", c=4), [pfb], [b_cbT])
            nkb = 4 * t + 4
            for c in range(4):
                acc, accb = ring("f32r", f32r, b_f32r)
                ts("dve", acc, mraw[:, c, 0:T], cc(C_CONV + c * 4 + 0), None, ALU.mult, None, [b_mraw[c], b_consts], [accb])
                for j in range(1, 4):
                    stt("dve", acc, mraw[:, c, j:j + T], cc(C_CONV + c * 4 + j), acc, ALU.mult, ALU.add, [b_mraw[c], b_consts, accb], [accb])
                act(mqk[:, c, :], acc, AF.Silu, [accb], [b_mqk[c]])
            for c in range(4):
                cp("dve", mhist[:, c, :], mraw[:, c, T:T + 3], [b_mraw[c]], [b_mhist])

            chk("conv")

            BANK_S, BANK_N = 3, (6, 7)

            def mlstm_gen():
                for c in range(4):
                    sl = slice(c * 128, (c + 1) * 128)
                    hsl = slice(3 + c * 128, 3 + (c + 1) * 128)
                    pS, pSb = psum[BANK_S], b_ps[BANK_S]
                    hq = []
                    sloc = []
                    for h in range(4):
                        j = h // 2
                        b0 = (h % 2) * 64
                        qT = mqk[b0:b0 + 64, j, sl]
                        kT = mqk[b0:b0 + 64, 2 + j, sl]
                        hq.append((j, b0, qT))
                        if h % 2 == 0:
                            so, sob = psum[BANK_S][:, j * 128:(j + 1) * 128], b_ps[BANK_S]
                        else:
                            so, sob = psum[BANK_N[j]][:, 384:512], b_ps[BANK_N[j]]
                        sloc.append((so, sob))
                        mm(so, kT, qT, True, True, [b_mqk[j], b_mqk[2 + j]], [sob])
                    yield
                    for h in range(4):
                        so, sob = sloc[h]
                        stt("dve", ptm4[:, h, :], so, awT[:, c, h:h + 1], mask01[:, :], ALU.mult, ALU.mult, [sob, b_awT, b_k], [b_ptm4[h]])
                    yield
                    for hp in range(2):
                        for h in (2 * hp, 2 * hp + 1):
                            j, b0, qT = hq[h]
                            pN, pNb = psum[BANK_N[h % 2]], b_ps[BANK_N[h % 2]]
                            pt = ptm4[:, h, :]
                            mm(pN[:, 0:128], cbf[b0:b0 + 64, j, :], qT, True, False, [b_cbf[j], b_mqk[j]], [pNb])
                            mm(pN[:, 0:128], vm[:, c, h, 0:128], pt, False, True, [b_vm[c], b_ptm4[h]], [pNb])
                            mm(pN[:, 128:256], nbb[b0:b0 + 64, j, :], qT, True, False, [b_cbf[j], b_mqk[j]], [pNb])
                            mm(pN[:, 128:256], onesb[:, :], pt, False, True, [b_k, b_ptm4[h]], [pNb])
                            mm(pN[:, 256:384], selrows[:, h, :], g_E[:, sl], True, True, [b_k, b_gate], [pNb])
                        yield
                        for h in (2 * hp, 2 * hp + 1):
                            pN, pNb = psum[BANK_N[h % 2]], b_ps[BANK_N[h % 2]]
                            es, esbb = ring("esb", esb, b_esb)
                            cp("act", es, pN[:, 256:384], [pNb], [esbb])
                            t1, t1b = ring("t1r", t1r, b_t1r)
                            act(t1, pN[:, 128:256], AF.Abs, [pNb], [t1b])
                            tt("dve", t1, t1, es, ALU.max, [t1b, esbb], [t1b])
                            P.op("dve", (lambda o: lambda E: E.reciprocal(out=o, in_=o))(t1), [t1b], [t1b])
                            tt("dve", hm[:, h, hsl], pN[:, 0:128], t1, ALU.mult, [pNb, t1b], [b_hm[h]])
                        yield
                    for j in range(2):
                        mm(pS[:, j * 128:(j + 1) * 128], mqk[:, 2 + j, sl], identb[:, :], True, True, [b_mqk[2 + j], b_k], [pSb])
                    yield
                    for j in range(2):
                        for hh in range(2):
                            ts("dve", kw[:, j, hh * 64:(hh + 1) * 64], pS[:, j * 128 + hh * 64:j * 128 + (hh + 1) * 64], awT[:, c, 4 + 2 * j + hh:5 + 2 * j + hh], None, ALU.mult, None, [pSb, b_awT], [b_kw[j]])
                    yield
                    for j in range(2):
                        pC, pCb = psum[BANK_N[j]], b_ps[BANK_N[j]]
                        mm(pC[:, 0:129], kw[:, j, :], vm[:, c, 2 * j, :], True, True, [b_kw[j], b_vm[c]], [pCb])
                        mm(pC[:, 256:385], kw[:, j, :], vm[:, c, 2 * j + 1, :], True, True, [b_kw[j], b_vm[c]], [pCb])
                    yield
                    for j in range(2):
                        pC, pCb = psum[BANK_N[j]], b_ps[BANK_N[j]]
                        stt("dve", cn[0:64, j, :], cn[0:64, j, :], decp[0:64, j, c:c + 1], pC[0:64, 0:129], ALU.mult, ALU.add, [b_cn[j], b_decp, pCb], [b_cn[j]])
                        stt("dve", cn[64:128, j, :], cn[64:128, j, :], decp[64:128, j, c:c + 1], pC[64:128, 256:385], ALU.mult, ALU.add, [b_cn[j], b_decp, pCb], [b_cn[j]])
                        P.op("act", (lambda o, i: lambda E: E.activation(out=o, in_=i, func=AF.Copy, scale=0.125))(cbf[:, j, :], cn[:, j, 0:128]), [b_cn[j]], [b_cbf[j]])
                        ts("dve", nbb[:, j, :], onesb[:, :], cn[:, j, 128:129], 0.125, ALU.mult, ALU.mult, [b_cn[j], b_k], [b_cbf[j]])
                    yield

            mg = mlstm_gen()
            import os as _os
            if _os.environ.get('MG_FIRST'):
                for _i in range(int(_os.environ.get('MG_STEPS', '1000'))):
                    if next(mg, 'end') == 'end':
                        break
                if 'MG_STEPS' in _os.environ:
                    mg = iter(())
            NFH = int(_os.environ.get('NFH', '8'))

            pools["FS"] = [0, 1, 2]
            pools["FO"] = [4]
            pool_ctr.setdefault("FS", 0)
            pool_ctr.setdefault("FO", 0)
            def fox_fin_gen(h, pO, pOb):
                o_, ob = ring("osb", osb, b_osb)
                cp("dve", o_, pO[0:65, :], [pOb], [ob])
                yield
                pl, plb = psum[5], b_ps[5]
                mm(pl[0:64, :], sel64[:, :], o_[0:65], True, True, [b_k, ob], [plb])
                yield
                tl, tlb = ring("f32r", f32r, b_f32r)
                act(tl[0:64], pl[0:64, :], AF.Ln, [plb], [tlb])
                act(tl[0:64], tl[0:64], AF.Exp, [tlb], [tlb], scale=-1.0)
                yield
                y0, y0b = ring("f32r", f32r, b_f32r)
                tt("dve", y0[0:64], o_[0:64], tl[0:64], ALU.mult, [ob, tlb], [y0b])
                sq, sqb = ring("sqr", sqr, b_sqr)
                tt("dve", sq[0:64], y0[0:64], y0[0:64], ALU.mult, [y0b], [sqb])
                yield
                pn, pnb = psum[5], b_ps[5]
                mm(pn[0:64, :], onesb[0:64, 0:64], sq[0:64], True, True, [b_k, sqb], [pnb])
                yield
                rs, rb = rstd_from_psum(pn[0:64, :], pnb, 64, 1.0 / 64)
                yield
                stt("dve", ycf[:, h, :], y0[0:64], consts[0:64, C_FOUT + h:C_FOUT + h + 1], rs[0:64], ALU.mult, ALU.mult, [y0b, b_consts, rb], [b_ycf[h]])

            fin = iter(())
            for h in range(NFH):
                j = h // 2
                b0 = (h % 2) * 64
                pO, pOb = ps_get("FO")
                blocks = []
                for kb in range(nkb):
                    d = kb - 4 * t
                    q0 = 0 if d < 0 else d * 128
                    blocks.append((kb, q0, d >= 0))

                def issue_S(kb, q0, diag):
                    pS, pSb = ps_get("FS")
                    kt = kb // 4
                    mm(pS[:, q0:T], kc[:, j, kb * 128:(kb + 1) * 128], fq[:, h, q0:T], True, False, [b_kc[j][kt], b_fq[h]], [pSb])
                    mm(pS[:, q0:T], sel24[:, h, :], qall[:, q0:T], False, not diag, [b_k, b_qall], [pSb])
                    if diag:
                        mm(pS[:, q0:q0 + 128], identb[:, :], maskneg[:, :], False, True, [b_k], [pSb])
                    return pS, pSb

                pend = []
                LOOK = 2
                for i in range(min(LOOK, len(blocks))):
                    pend.append(issue_S(*blocks[i]))
                for i, (kb, q0, diag) in enumerate(blocks):
                    pS, pSb = pend.pop(0)
                    pt, ptb = ring("ptr", ptr, b_ptr)
                    act(pt[:, q0:T], pS[:, q0:T], AF.Exp, [pSb, b_cbT], [ptb], bias=cbT[:, kb, h:h + 1], scale=0.125)
                    if i + LOOK < len(blocks):
                        pend.append(issue_S(*blocks[i + LOOK]))
                    next(mg, None)
                    next(fin, None)
                    mm(pO[0:65, q0:T], vc[:, kb, h, :], pt[:, q0:T], i == 0, i == len(blocks) - 1, [b_vc[kb], ptb], [pOb])
                for _ in fin:
                    pass
                fin = fox_fin_gen(h, pO, pOb)
                next(fin, None)
            for _ in fin:
                pass
            for _ in mg:
                pass
            chk("mlstm")

            chk("fox")
            wmo = [w_next(), w_next()]
            for h in range(4):
                ps, pb = sumsq_bcast([(hm[:, h, 3:3 + T], b_hm[h])], 128, onesb[:, :])
                rs, rb = rstd_from_psum(ps[:, :], pb, 128, 1.0 / 128)
                y1, y1b = ring("f32r", f32r, b_f32r)
                stt("dve", y1, hm[:, h, 3:3 + T], ghalf[:, h:h + 1], rs[:, :], ALU.mult, ALU.mult, [b_hm[h], b_ghalf, rb], [y1b])
                wv, wb = wmo[h // 2]
                pso, psob = ps_get("A")
                proj_fm(wv, wb, h % 2, pso, psob)
                th, thb = ring("f32r", f32r, b_f32r)
                act(th, pso[:, :], AF.Tanh, [psob], [thb], scale=0.5)
                stt("dve", ycm[:, h, :], th, 1.0, y1, ALU.add, ALU.mult, [thb, y1b], [b_ycm[h]])
            w_rel(2)

            for dp in range(4):
                wm, wmb = w_next()
                wf, wfb = w_next()
                for jj in range(2):
                    do = 2 * dp + jj
                    ps, pb = ps_get("A")
                    for c in range(4):
                        mm(ps[:, :], wm[:, c, jj * 128:(jj + 1) * 128], ycm[:, c, :], c == 0, False, [wmb, b_ycm[c]], [pb])
                    for h in range(8):
                        mm(ps[:, :], wf[0:64, h, jj * 128:(jj + 1) * 128], ycf[:, h, :], False, h == 7, [wfb, b_ycf[h]], [pb])
                    tt("dve", hT[:, do, :], hT[:, do, :], ps[:, :], ALU.add, [b_h[do], pb], [b_h[do]])
                w_rel(2)

            chk("wout")
            norm_to_uT(C_FFN2)
            ffn()

            chk("ffn2")
            wpieces = [w_next() for _ in range(4)]
            wpp_v, wpp_b = w_next()
            pss, pssb = ps_get("C")
            for do in range(NKC):
                ps, pb = ps_get("A")
                for k in range(2):
                    mm(ps[:, :], wpp_v[:, k, do * 128:(do + 1) * 128], pT[:, k, :], k == 0, k == 1, [wpp_b, b_pT], [pb])
                sq, sqb = ring("sqr", sqr, b_sqr)
                act(sq, ps[:, :], AF.Square, [pb], [sqb])
                mm(pss[:, :], onesb[:, :], sq, do == 0, do == NKC - 1, [sqb, b_k], [pssb])
            norm_to_uT(C_PG)
            rsp, rspb = rstd_from_psum(pss[:, :], pssb, 128, 1.0 / D)
            for do in range(NKC):
                wv, wb = wpieces[do // 2]
                ps, pb = ps_get("A")
                proj_fm(wv, wb, do % 2, ps, pb)
                th, thb = ring("f32r", f32r, b_f32r)
                act(th, ps[:, :], AF.Tanh, [pb], [thb], scale=0.5)
                ps2, pb2 = ps_get("A")
                for k in range(2):
                    mm(ps2[:, :], wpp_v[:, k, do * 128:(do + 1) * 128], pT[:, k, :], k == 0, k == 1, [wpp_b, b_pT], [pb2])
                a1, a1b = ring("f32r", f32r, b_f32r)
                stt("dve", a1, ps2[:, :], ghalf[:, 4 + do:5 + do], rsp[:, :], ALU.mult, ALU.mult, [pb2, b_ghalf, rspb], [a1b])
                stt("dve", a1, th, 1.0, a1, ALU.add, ALU.mult, [thb, a1b], [a1b])
                tt("dve", hT[:, do, :], hT[:, do, :], a1, ALU.add, [b_h[do], a1b], [b_h[do]])
                if do % 2 == 1:
                    w_rel(1)
            w_rel(1)

            chk("ple")
            ps, pb = sumsq_bcast([(hT[:, c, :], b_h[c]) for c in range(NKC)], 128, onesb[:, :])
            rs, rb = rstd_from_psum(ps[:, :], pb, 128, 1.0 / D)
            for c in range(NKC):
                o1, o1b = ring("f32r", f32r, b_f32r)
                stt("dve", o1, hT[:, c, :], cc(C_FIN + c), rs[:, :], ALU.mult, ALU.mult, [b_h[c], rb, b_consts], [o1b])
                P.dma("sp" if (t == 0 or _os_dq) else "pool", out_d[c * 128:(c + 1) * 128, t0:t0 + T], o1, ("xout", (ring_ctr["f32r"] - 1) % len(b_f32r)), reads=[o1b], is_out=True)

        if debug:
            P.halt = False
            allb = b_h + b_u + b_mqk + b_mraw + b_ycf + b_fq + b_f32r + b_rsr
            P.dma("sp", dbg_d["hT"], hT[:, :, :].rearrange("p a b -> p (a b)"), "dbg0", reads=allb, is_out=True)
            P.dma("pool", dbg_d["uT"], uT[:, :, :].rearrange("p a b -> p (a b)"), "dbg1", reads=allb, is_out=True)
            P.dma("pool", dbg_d["mqk"], mqk[:, :, :].rearrange("p a b -> p (a b)"), "dbg2", reads=allb, is_out=True)
            P.dma("sp", dbg_d["hm"], mraw[:, :, :].rearrange("p a b -> p (a b)"), "dbg3", reads=allb, is_out=True)
            P.dma("pool", dbg_d["ycf"][0:64, :], ycf[:, :, :].rearrange("p a b -> p (a b)"), "dbg4", reads=allb, is_out=True)
            P.dma("pool", dbg_d["fq"], fq[:, :, :].rearrange("p a b -> p (a b)"), "dbg5", reads=allb, is_out=True)
            P.dma("sp", dbg_d["misc"][:, 0:3 * T], f32r[:, :, :].rearrange("p a b -> p (a b)"), "dbg6", reads=allb, is_out=True)
            P.dma("sp", dbg_d["misc"][:, 3 * T:4 * T], rsr[:, 0, :], "dbg7", reads=allb, is_out=True)
            P.dma("pool", dbg_d["ffa"], ffa[:, :, :].rearrange("p a b -> p (a b)"), "dbg8", reads=allb + b_ffa, is_out=True)
        P.emit()
    nc._marks = marks
    nc._nops = {e: len(P.ops[e]) for e in P.ENGS}
    return nc


def _prep_inputs(inputs):
    f = lambda a: np.ascontiguousarray(np.asarray(a, dtype=np.float32))
    consts = np.zeros((128, NCONST), np.float32)

    def put_cols(col, vec, chunk):
        v = f(vec).reshape(-1, chunk)
        consts[:chunk, col:col + v.shape[0]] = v.T

    put_cols(C_FFN1, inputs["ffn1_norm"][0], 128)
    put_cols(C_MIX, inputs["mix_norm"][0], 128)
    put_cols(C_FFN2, inputs["ffn2_norm"][0], 128)
    put_cols(C_PG, inputs["ple_gate_norm"][0], 128)
    put_cols(C_PP, inputs["ple_proj_norm"][0], 128)
    put_cols(C_FIN, inputs["final_norm"], 128)
    put_cols(C_MOUT, inputs["mlstm_out_norm"][0], 128)
    put_cols(C_FOUT, inputs["fox_out_norm"][0], 64)
    cw = f(inputs["conv_qk"][0])
    consts[:, C_CONV:C_CONV + 16] = cw.reshape(4, 4, 128).transpose(2, 1, 0).reshape(128, 16)
    bg = f(inputs["b_mlstm_gates"][0])
    consts[0:4, C_BI] = bg[0:4]
    consts[0:4, C_BF] = bg[4:8]
    consts[0:8, C_BFF] = f(inputs["b_fox_f"][0])
    w_in = f(inputs["w_in"][0])
    w_gates = np.ascontiguousarray(np.concatenate([w_in[:, 1536:1544], w_in[:, 3080:3088]], axis=1))
    shared = {
        "ffn1_w_gate": f(inputs["ffn1_w_gate"][0]), "ffn1_w_up": f(inputs["ffn1_w_up"][0]), "ffn1_w_down": f(inputs["ffn1_w_down"][0]),
        "ffn2_w_gate": f(inputs["ffn2_w_gate"][0]), "ffn2_w_up": f(inputs["ffn2_w_up"][0]), "ffn2_w_down": f(inputs["ffn2_w_down"][0]),
        "w_in": w_in, "w_gates": w_gates, "w_out": f(inputs["w_out"][0]),
        "w_ple_gate": f(inputs["w_ple_gate"][0]), "w_ple_proj": f(inputs["w_ple_proj"][0]), "consts": consts,
    }
    x = np.asarray(inputs["x"], dtype=np.float32)
    p = np.asarray(inputs["p"], dtype=np.float32)[0]
    in_maps = []
    for b in range(8):
        m = dict(shared)
        m["x"] = np.ascontiguousarray(x[b].T)
        m["p"] = np.ascontiguousarray(p[b].T)
        in_maps.append(m)
    return in_maps


def kernel(**inputs):
    nc = build_nc()
    in_maps = _prep_inputs(inputs)
    res = run_bass_kernel_spmd(nc, in_maps, core_ids=list(range(8)))
    return np.stack([np.ascontiguousarray(np.asarray(r["out"], dtype=np.float32).T) for r in res.results], axis=0)
```
